# Optimizing a Trainium2 kernel written in Bass

```python
import math
import jax, jax.numpy as jnp
from jax import lax
import numpy as np

D_MODEL = 1024
BATCH = 8
SEQ = 4096
DEPTH = 2

HEAD_DIM = 64
BRANCH_WIDTH = D_MODEL // 2
DIFF_HEADS = BRANCH_WIDTH // (2 * HEAD_DIM)
DIFF_VDIM = 2 * HEAD_DIM
FOX_HEADS = BRANCH_WIDTH // HEAD_DIM
DIL_HEADS = BRANCH_WIDTH // HEAD_DIM
DIL_PATTERNS = ((128, 1), (512, 4), (2048, 16))
N_BRANCHES = 3
Q_BLOCK = 128
RMS_EPS = 1e-6
ALIBI_MAX_EXP = 8.0

kernel_name = "hybrid_diff_fox_dilated_gated_block"

_SPLITS = (
    ("diff_q", 2 * DIFF_HEADS * HEAD_DIM), ("diff_k", 2 * DIFF_HEADS * HEAD_DIM),
    ("diff_v", DIFF_HEADS * DIFF_VDIM), ("diff_z", BRANCH_WIDTH),
    ("fox_q", FOX_HEADS * HEAD_DIM), ("fox_k", FOX_HEADS * HEAD_DIM),
    ("fox_v", FOX_HEADS * HEAD_DIM), ("fox_f", FOX_HEADS), ("fox_z", BRANCH_WIDTH),
    ("dil_q", DIL_HEADS * HEAD_DIM), ("dil_k", DIL_HEADS * HEAD_DIM),
    ("dil_v", DIL_HEADS * HEAD_DIM), ("dil_z", BRANCH_WIDTH),
    ("merge_g", N_BRANCHES * D_MODEL),
)
_NAMES = [n for n, _ in _SPLITS]
_OFFSETS = [int(o) for o in np.cumsum([w for _, w in _SPLITS])[:-1]]
IN_WIDTH = int(sum(w for _, w in _SPLITS))


def rmsnorm(x, g):
    xf = x.astype(jnp.float32)
    y = xf * lax.rsqrt(jnp.mean(xf * xf, axis=-1, keepdims=True) + RMS_EPS)
    return (y * g.astype(jnp.float32)).astype(x.dtype)


def alibi_slopes(n_heads):
    return jnp.asarray(np.power(2.0, -ALIBI_MAX_EXP * np.arange(1, n_heads + 1) / n_heads), dtype=jnp.float32)


def causal_sweep(block_fn, seq_len):
    return jnp.concatenate([block_fn(s, s + Q_BLOCK) for s in range(0, seq_len, Q_BLOCK)], axis=1)


def diff_attention(q, k, v, lam, slopes):
    seq_len = q.shape[1]
    scale = HEAD_DIM ** -0.5

    def block(start, end):
        s = jnp.einsum('bqhcd,bkhcd->bhcqk', q[:, start:end], k[:, :end]).astype(jnp.float32) * scale
        dist = (jnp.arange(start, end)[:, None] - jnp.arange(end)[None, :]).astype(jnp.float32)
        s = s - slopes[:, None, None, None] * dist
        s = jnp.where(dist >= 0, s, -jnp.inf)
        p = jax.nn.softmax(s, axis=-1)
        pd = p[:, :, 0] - lam * p[:, :, 1]
        return jnp.einsum('bhqk,bkhe->bqhe', pd, v[:, :end].astype(jnp.float32))

    return causal_sweep(block, seq_len)


def forgetting_attention(q, k, v, cum_logf):
    seq_len = q.shape[1]
    scale = HEAD_DIM ** -0.5
    c = cum_logf.transpose(0, 2, 1)

    def block(start, end):
        s = jnp.einsum('bqhd,bkhd->bhqk', q[:, start:end], k[:, :end]).astype(jnp.float32) * scale
        s = s + c[:, :, start:end, None] - c[:, :, None, :end]
        causal = jnp.arange(start, end)[:, None] >= jnp.arange(end)[None, :]
        s = jnp.where(causal, s, -jnp.inf)
        p = jax.nn.softmax(s, axis=-1)
        return jnp.einsum('bhqk,bkhd->bqhd', p, v[:, :end].astype(jnp.float32))

    return causal_sweep(block, seq_len)


def dilated_pattern(q, k, v, window, dilation, slopes):
    B, S, H, E = q.shape
    n = S // dilation
    L = window // dilation
    n_pad = -(-n // L) * L
    nb = n_pad // L
    scale = HEAD_DIM ** -0.5

    def to_sub(a):
        a = a.reshape(B, n, dilation, H, E).transpose(0, 2, 3, 1, 4)
        a = jnp.pad(a, ((0, 0), (0, 0), (0, 0), (0, n_pad - n), (0, 0)))
        return a.reshape(B, dilation, H, nb, L, E)

    def with_prev(a):
        prev = jnp.pad(a[:, :, :, :-1], ((0, 0), (0, 0), (0, 0), (1, 0), (0, 0), (0, 0)))
        return jnp.concatenate([prev, a], axis=4)

    qs = to_sub(q)
    kk = with_prev(to_sub(k))
    vv = with_prev(to_sub(v))
    s = jnp.einsum('brhnqe,brhnke->brhnqk', qs, kk).astype(jnp.float32) * scale
    i = jnp.arange(L)[:, None]
    j = jnp.arange(2 * L)[None, :]
    delta = L + i - j
    blk = jnp.arange(nb)[:, None, None]
    valid = (delta >= 0) & (delta <= L) & (blk * L + j - L >= 0)
    s = s - slopes[:, None, None, None] * (dilation * delta).astype(jnp.float32)
    s = jnp.where(valid, s, -jnp.inf)
    m = jnp.max(s, axis=-1, keepdims=True)
    e = jnp.exp(s - m)
    den = jnp.sum(e, axis=-1, keepdims=True)
    num = jnp.einsum('brhnqk,brhnke->brhnqe', e, vv.astype(jnp.float32))

    def from_sub(a):
        F = a.shape[-1]
        a = a.reshape(B, dilation, H, n_pad, F)[:, :, :, :n]
        return a.transpose(0, 3, 1, 2, 4).reshape(B, S, H, F)

    return from_sub(num), from_sub(m)[..., 0], from_sub(den)[..., 0]


def dilated_attention(q, k, v, slopes):
    res = [dilated_pattern(q, k, v, w, d, slopes) for w, d in DIL_PATTERNS]
    mx = jnp.max(jnp.stack([r[1] for r in res], axis=0), axis=0)
    num = sum(jnp.exp(r[1] - mx)[..., None] * r[0] for r in res)
    den = sum(jnp.exp(r[1] - mx) * r[2] for r in res)
    return num / den[..., None]


def setup_inputs(seed: int = 0) -> dict:
    key = jax.random.key(seed)
    ks = jax.random.split(key, 10)
    f32 = jnp.float32
    x = jax.random.normal(ks[0], (BATCH, SEQ, D_MODEL), f32)
    norm_g = 1.0 + 0.05 * jax.random.normal(ks[1], (DEPTH, D_MODEL), f32)
    w_in = jax.random.normal(ks[2], (DEPTH, D_MODEL, IN_WIDTH), f32) * D_MODEL ** -0.5
    fox_fb = jax.random.uniform(ks[3], (DEPTH, FOX_HEADS), f32, minval=1.0, maxval=4.0)
    diff_lam = 0.1 * jax.random.normal(ks[4], (DEPTH, 4, HEAD_DIM), f32)
    diff_norm_g = 1.0 + 0.05 * jax.random.normal(ks[5], (DEPTH, DIFF_HEADS * DIFF_VDIM), f32)
    w_branch = jax.random.normal(ks[6], (DEPTH, N_BRANCHES, BRANCH_WIDTH, D_MODEL), f32) * BRANCH_WIDTH ** -0.5
    w_out = jax.random.normal(ks[7], (DEPTH, D_MODEL, D_MODEL), f32) * D_MODEL ** -0.5
    final_g = 1.0 + 0.05 * jax.random.normal(ks[8], (D_MODEL,), f32)
    return {"x": x, "norm_g": norm_g, "w_in": w_in, "fox_fb": fox_fb, "diff_lam": diff_lam,
            "diff_norm_g": diff_norm_g, "w_branch": w_branch, "w_out": w_out, "final_g": final_g}


def reference(x, norm_g, w_in, fox_fb, diff_lam, diff_norm_g, w_branch, w_out, final_g):
    B, S, _ = x.shape
    diff_slopes = alibi_slopes(DIFF_HEADS)
    dil_slopes = alibi_slopes(DIL_HEADS)
    for l in range(DEPTH):
        h = rmsnorm(x, norm_g[l])
        proj = h @ w_in[l]
        p = dict(zip(_NAMES, jnp.split(proj, _OFFSETS, axis=-1)))

        lam_init = 0.8 - 0.6 * math.exp(-0.3 * l)
        dl = diff_lam[l].astype(jnp.float32)
        lam = jnp.exp(jnp.sum(dl[0] * dl[1])) - jnp.exp(jnp.sum(dl[2] * dl[3])) + lam_init
        qa = p["diff_q"].reshape(B, S, DIFF_HEADS, 2, HEAD_DIM)
        ka = p["diff_k"].reshape(B, S, DIFF_HEADS, 2, HEAD_DIM)
        va = p["diff_v"].reshape(B, S, DIFF_HEADS, DIFF_VDIM)
        oa = diff_attention(qa, ka, va, lam, diff_slopes)
        oa = rmsnorm(oa, diff_norm_g[l].reshape(DIFF_HEADS, DIFF_VDIM)) * (1.0 - lam_init)
        ya = oa.reshape(B, S, BRANCH_WIDTH).astype(x.dtype) * jax.nn.silu(p["diff_z"])

        logf = jax.nn.log_sigmoid(p["fox_f"].astype(jnp.float32) + fox_fb[l].astype(jnp.float32))
        cum_logf = jnp.cumsum(logf, axis=1)
        qb = p["fox_q"].reshape(B, S, FOX_HEADS, HEAD_DIM)
        kb = p["fox_k"].reshape(B, S, FOX_HEADS, HEAD_DIM)
        vb = p["fox_v"].reshape(B, S, FOX_HEADS, HEAD_DIM)
        ob = forgetting_attention(qb, kb, vb, cum_logf)
        yb = ob.reshape(B, S, BRANCH_WIDTH).astype(x.dtype) * jax.nn.silu(p["fox_z"])

        qc = p["dil_q"].reshape(B, S, DIL_HEADS, HEAD_DIM)
        kc = p["dil_k"].reshape(B, S, DIL_HEADS, HEAD_DIM)
        vc = p["dil_v"].reshape(B, S, DIL_HEADS, HEAD_DIM)
        oc = dilated_attention(qc, kc, vc, dil_slopes)
        yc = oc.reshape(B, S, BRANCH_WIDTH).astype(x.dtype) * jax.nn.silu(p["dil_z"])

        y = jnp.stack([ya, yb, yc], axis=2)
        gates = jax.nn.sigmoid(p["merge_g"]).reshape(B, S, N_BRANCHES, D_MODEL)
        branch = jnp.einsum('bsne,nef->bsnf', y, w_branch[l])
        merged = jnp.sum(gates * branch, axis=2)
        x = x + merged @ w_out[l]
    return rmsnorm(x, final_g)
```

```python
import math
import numpy as np
import ml_dtypes
from contextlib import ExitStack
import concourse.bass as bass
import concourse.mybir as mybir
from concourse.bass_utils import run_bass_kernel_spmd

F32 = mybir.dt.float32
BF16 = mybir.dt.bfloat16
AF = mybir.ActivationFunctionType
ALU = mybir.AluOpType

D = 1024
INW = 9224
OFF = dict(diff_q=0, diff_k=512, diff_v=1024, diff_z=1536, fox_q=2048, fox_k=2560, fox_v=3072,
           fox_f=3584, fox_z=3592, dil_q=4104, dil_k=4616, dil_v=5128, dil_z=5640, merge_g=6152)
EPS = 1e-6
ENGS = ("pe", "act", "dve", "pool", "sp")
SWDGE_DEPTH = 3
SAME_ENG_WINDOW = 4


def I(meth, *args, **kw):
    f = lambda e: getattr(e, meth)(*args, **kw)
    f.multi = (meth in ("matmul", "transpose")) or (kw.get("accum_out") is not None)
    return f


class Res:
    __slots__ = ("name", "last_w", "readers", "dma_readers", "dsem", "dcount")

    def __init__(self, name):
        self.name = name
        self.last_w = None
        self.readers = {}
        self.dma_readers = []
        self.dsem = None
        self.dcount = 0


class Op:
    __slots__ = ("eng", "emit", "deps", "signal", "sigsem", "sigval", "is_dma", "epoch", "seq")

    def __init__(self, eng, emit, is_dma, epoch):
        self.seq = 0
        self.eng = eng
        self.emit = emit
        self.deps = []
        self.signal = False
        self.sigsem = None
        self.sigval = 0
        self.is_dma = is_dma
        self.epoch = epoch


class Prog:
    def __init__(self):
        self.ops = {e: [] for e in ENGS}
        self.epoch = 0
        self.chans = []
        self.chan_last = {}
        self.last_op = {}
        self.pending_bar = {e: [] for e in ENGS}
        self.res = {}
        self.pool_dmas = []

    def R(self, name):
        r = self.res.get(name)
        if r is None:
            r = self.res[name] = Res(name)
        return r

    def op(self, eng, emit, reads=(), writes=(), dma_chan=None):
        is_dma = dma_chan is not None
        o = Op(eng, emit, is_dma, self.epoch)
        o.seq = len(self.ops[eng])
        deps = list(self.pending_bar[eng])
        self.pending_bar[eng] = []
        for r in reads:
            if r.last_w is not None:
                deps.append(r.last_w)
        for r in writes:
            if r.last_w is not None:
                deps.append(r.last_w)
            deps.extend(r.readers.values())
            deps.extend(r.dma_readers)
        seen = set()
        for d in deps:
            if id(d) in seen:
                continue
            seen.add(id(d))
            if (not d.is_dma) and d.eng == eng and (eng == "pe" or o.seq - d.seq > SAME_ENG_WINDOW):
                continue
            o.deps.append(d)
            d.signal = True
        if is_dma and eng == "pool":
            self.pool_dmas.append(o)
            if len(self.pool_dmas) > SWDGE_DEPTH:
                d = self.pool_dmas[-1 - SWDGE_DEPTH]
                if id(d) not in seen:
                    o.deps.append(d)
        for r in reads:
            if is_dma:
                r.dma_readers.append(o)
            else:
                r.readers[eng] = o
        for r in writes:
            r.last_w = o
            r.readers = {}
            r.dma_readers = []
        if is_dma:
            if dma_chan.dsem is None:
                dma_chan.dsem = "ch%d" % len(self.chans)
                self.chans.append(dma_chan)
            dma_chan.dcount += 16
            o.sigsem = dma_chan.dsem
            o.sigval = dma_chan.dcount
            o.signal = True
            self.chan_last[dma_chan.dsem] = o
        else:
            self.last_op[eng] = o
        self.ops[eng].append(o)
        return o

    def barrier(self):
        deps = list(self.last_op.values()) + list(self.chan_last.values())
        for d in deps:
            d.signal = True
        for e in ENGS:
            self.pending_bar[e] = list(deps)

    def emit(self, nc, stack):
        semkeys = set()
        for e in ENGS:
            cnt = {}
            for o in self.ops[e]:
                if o.is_dma:
                    semkeys.add(o.sigsem)
                elif o.signal:
                    k = "%s_%d" % (e, o.epoch)
                    cnt[k] = cnt.get(k, 0) + 1
                    o.sigsem = k
                    o.sigval = cnt[k]
                    semkeys.add(k)
        sems = {k: stack.enter_context(nc.semaphore(k)) for k in sorted(semkeys)}
        block = stack.enter_context(nc.Block())
        engmap = {"pe": block.tensor, "act": block.scalar, "dve": block.vector,
                  "pool": block.gpsimd, "sp": block.sync}
        stats = {}
        chans = self.chans
        for e in ENGS:
            def body(eng, ops=self.ops[e], e=e):
                waited = {}
                nw = 0
                for o in ops:
                    need = {}
                    for d in o.deps:
                        if waited.get(d.sigsem, 0) >= d.sigval:
                            continue
                        need[d.sigsem] = max(need.get(d.sigsem, 0), d.sigval)
                    need = list(need.items())
                    attach = None
                    if need and e != "pe" and not getattr(o.emit, "multi", True):
                        attach = need.pop()
                    for k, v in need:
                        eng.wait_ge(sems[k], v)
                        waited[k] = v
                        nw += 1
                    ins = o.emit(eng)
                    if attach is not None:
                        ins._wait_ge(sems[attach[0]], attach[1])
                        waited[attach[0]] = attach[1]
                    if o.signal:
                        ins.then_inc(sems[o.sigsem], 16 if o.is_dma else 1)
                if e == "sp":
                    for r in chans:
                        eng.wait_ge(sems[r.dsem], r.dcount)
                stats[e] = (len(ops), nw)
            engmap[e](body)
        stats["nsem"] = len(sems)
        return stats


class Arena:
    def __init__(self, ap_f32):
        self.base = ap_f32
        self.cap = ap_f32.shape[1] * 4
        self.off = 0
        self.peak = 0

    def reset(self):
        self.off = 0

    def alloc(self, shape, dtype):
        esz = 2 if dtype == BF16 else 4
        n = 1
        for s in shape[1:]:
            n *= s
        nbytes = (n * esz + 31) // 32 * 32
        assert self.off + nbytes <= self.cap, "arena overflow %d + %d > %d" % (self.off, nbytes, self.cap)
        v = self.base[:, self.off // 4:(self.off + nbytes) // 4]
        if dtype == BF16:
            v = v.bitcast(BF16)
        v = v[0:shape[0], 0:n]
        if len(shape) == 3:
            v = v.rearrange("p (a b) -> p a b", a=shape[1])
        elif len(shape) == 4:
            v = v.rearrange("p (a b c) -> p a b c", a=shape[1], b=shape[2])
        self.off += nbytes
        self.peak = max(self.peak, self.off)
        return v


def build(S, DEPTH):
    NT = S // 128
    NG = S // 512
    nc = bass.Bass("TRN2", target_bir_lowering=False)
    dram = lambda name, shape, dt, kind: nc.dram_tensor(name, shape, dt, kind=kind).ap()
    x_d = dram("x", [S, D], F32, "ExternalInput")
    norm_g_d = dram("norm_g", [DEPTH, D], F32, "ExternalInput")
    w_in_d = dram("w_in", [DEPTH, D, INW], F32, "ExternalInput")
    fox_fb_d = dram("fox_fb", [DEPTH, 8], F32, "ExternalInput")
    diff_lam_d = dram("diff_lam", [DEPTH, 256], F32, "ExternalInput")
    diff_ng_d = dram("diff_norm_g", [DEPTH, 512], F32, "ExternalInput")
    w_br_d = dram("w_branch", [DEPTH, 1536, D], F32, "ExternalInput")
    w_out_d = dram("w_out", [DEPTH, D, D], F32, "ExternalInput")
    final_g_d = dram("final_g", [1, D], F32, "ExternalInput")
    identb_d = dram("ident_bf", [128, 128], BF16, "ExternalInput")
    identf_d = dram("ident_f", [128, 128], F32, "ExternalInput")
    qaug_d = dram("qaug", [4, S], BF16, "ExternalInput")
    kaug_d = dram("kaug", [8, 4, S], BF16, "ExternalInput")
    ones3_d = dram("ones3", [3, S], BF16, "ExternalInput")
    cmask_d = dram("cmask", [128, 128], BF16, "ExternalInput")
    dmask_d = dram("dmask", [128, 17 * 128], BF16, "ExternalInput")
    out_d = dram("out", [S, D], F32, "ExternalOutput")
    xs_d = dram("xs_scr", [S, D], F32, "Internal")
    yT_d = dram("yT_scr", [12, 128, S], BF16, "Internal")
    caug_d = dram("caug_scr", [8, 6, S], BF16, "Internal")

    P = Prog()
    R = P.R
    st = ExitStack()
    with st:
        hT = st.enter_context(nc.sbuf_tensor("hT", [128, 8, S], BF16))
        identb = st.enter_context(nc.sbuf_tensor("identb", [128, 128], BF16))
        identf = st.enter_context(nc.sbuf_tensor("identf", [128, 128], F32))
        small = st.enter_context(nc.sbuf_tensor("small", [128, 64], F32))
        PERS = 8 * S * 2 + 256 + 512 + 256
        arena_t = st.enter_context(nc.sbuf_tensor("arena", [128, (208000 - PERS) // 4 - 64], F32))
        A = Arena(arena_t[:, :])
        banks = [st.enter_context(nc.psum_tensor("bank%d" % i, [128, 512], F32)) for i in range(8)]
        r_bank = [R("bank%d" % i) for i in range(8)]
        r_hT = [R("hT%d" % t) for t in range(NT)]
        r_small = {}

        def sm(name, lo, hi):
            r_small[name] = R("sm_" + name)
            return small[:, lo:hi]

        ring = {}

        def nxt(name, n):
            v = ring.get(name, 0)
            ring[name] = v + 1
            return v % n

        MISC = (6, 7)

        def misc_bank():
            return MISC[nxt("misc", 2)]

        P.op("sp", I("dma_start", out=identb[:, :], in_=identb_d[:, :]), writes=[R("identb")], dma_chan=R("identb"))
        P.op("sp", I("dma_start", out=identf[:, :], in_=identf_d[:, :]), writes=[R("identf")], dma_chan=R("identf"))

        def norm_tile(src, r_src, t, gbc, r_gbc, bufs):
            hb, r_hb, junk, r_junk = bufs
            s = nxt("hb", 2)
            ss = small[:, 0 + 4 * s:4 + 4 * s]
            r_ss = R("sm_ss%d" % s)
            P.op("act", I("activation", out=junk, in_=src, func=AF.Square, accum_out=ss[:, 0:1]),
                 reads=[r_src], writes=[r_junk, r_ss])
            P.op("act", I("activation", out=ss[:, 1:2], in_=ss[:, 0:1], func=AF.Sqrt, scale=1.0 / D, bias=EPS),
                 reads=[r_ss], writes=[r_ss])
            P.op("dve", I("reciprocal", out=ss[:, 2:3], in_=ss[:, 1:2]), reads=[r_ss], writes=[r_ss])
            P.op("dve", I("scalar_tensor_tensor", out=hb[s], in0=src, scalar=ss[:, 2:3], in1=gbc,
                                                         op0=ALU.mult, op1=ALU.mult),
                 reads=[r_src, r_ss, r_gbc], writes=[r_hb[s]])
            b = misc_bank()
            tpv = banks[b][:, :].bitcast(BF16)
            for c in range(8):
                P.op("pe", I("transpose", out=tpv[:, c * 128:(c + 1) * 128], in_=hb[s][:, c * 128:(c + 1) * 128],
                                                      identity=identb[:, :]),
                     reads=[r_hb[s], R("identb")], writes=[r_bank[b]])
            P.op("dve", I("tensor_copy", out=hT[:, :, t * 128:(t + 1) * 128],
                                                in_=tpv.rearrange("p (c k) -> p c k", c=8)),
                 reads=[r_bank[b]], writes=[r_hT[t]])
            return ss, r_ss

        def load_gbc(gbc, r_gbc, src_row):
            P.op("sp", I("dma_start", out=gbc, in_=src_row.to_broadcast([128, D])), writes=[r_gbc], dma_chan=r_gbc)

        def phaseA():
            P.epoch += 1
            A.reset()
            gbc = A.alloc([128, D], F32); r_gbc = R("gbc")
            xt = [A.alloc([128, D], F32) for _ in range(2)]; r_xt = [R("xt%d" % i) for i in range(2)]
            hb = [A.alloc([128, D], BF16) for _ in range(2)]; r_hb = [R("hb%d" % i) for i in range(2)]
            junk = A.alloc([128, D], BF16); r_junk = R("junk")
            load_gbc(gbc, r_gbc, norm_g_d[0:1, :])
            for t in range(NT):
                s = t % 2
                P.op("sp", I("dma_start", out=xt[s], in_=x_d[t * 128:(t + 1) * 128, :]),
                     writes=[r_xt[s]], dma_chan=r_xt[s])
                norm_tile(xt[s], r_xt[s], t, gbc, r_gbc, (hb, r_hb, junk, r_junk))

        def phaseB(l):
            P.epoch += 1
            A.reset()
            lam_init = 0.8 - 0.6 * math.exp(-0.3 * l)
            Vaug = A.alloc([128, NT * 520], BF16); r_V = R("Vaug")
            QT = [A.alloc([128, S], BF16) for _ in range(2)]; r_QT = [R("QT%d" % i) for i in range(2)]
            KT = [A.alloc([128, S], BF16) for _ in range(2)]; r_KT = [R("KT%d" % i) for i in range(2)]
            O = A.alloc([128, NT, 128], F32); r_O = [R("O%d" % g) for g in range(NG)]
            siluT = A.alloc([128, S], BF16); r_silu = [R("silu%d" % g) for g in range(NG)]
            PT = [A.alloc([128, 512], BF16) for _ in range(4)]; r_PT = [R("PT%d" % i) for i in range(4)]
            Wv = A.alloc([128, 8, 512], BF16); r_Wv = R("Wv")
            Wz = [A.alloc([128, 8, 128], BF16) for _ in range(2)]; r_Wz = [R("Wz%d" % i) for i in range(2)]
            Wq = [A.alloc([128, 8, 64], BF16) for _ in range(2)]; r_Wq = [R("Wq%d" % i) for i in range(2)]
            Wk = [A.alloc([128, 8, 64], BF16) for _ in range(2)]; r_Wk = [R("Wk%d" % i) for i in range(2)]
            ystage = [A.alloc([128, 512], BF16) for _ in range(2)]; r_ys = [R("ys%d" % i) for i in range(2)]
            ze = [A.alloc([128, 512], F32) for _ in range(2)]; r_ze = [R("ze%d" % i) for i in range(2)]
            Obf = [A.alloc([128, 4, 128], BF16) for _ in range(2)]; r_Obf = [R("Obf%d" % i) for i in range(2)]
            cmask = A.alloc([128, 128], BF16); r_cm = R("cmask")
            dmask = A.alloc([128, 17 * 128], BF16); r_dm = R("dmask")
            junkf = A.alloc([128, 128], F32); r_junkf = R("junkf")
            dl = A.alloc([128, 256], F32); r_dl = R("dl")
            gn = A.alloc([128, 4], F32); r_gn = R("gn")
            Wf = A.alloc([128, 8, 8], BF16); r_Wf = R("Wf")
            fe = A.alloc([8, 256], F32); r_fe = R("fe")
            fsp = A.alloc([8, 256], F32); r_fsp = R("fsp")
            fC = [A.alloc([8, 256], F32) for _ in range(2)]; r_fC = [R("fC%d" % i) for i in range(2)]
            fr = A.alloc([8, 256], F32); r_fr = R("fr")
            ones8 = A.alloc([8, 256], F32); r_ones8 = R("ones8")
            aug6 = [A.alloc([8, 6, 256], BF16) for _ in range(2)]; r_aug6 = [R("aug6_%d" % i) for i in range(2)]
            fb8 = A.alloc([8, 2], F32); r_fb8 = R("fb8")
            r_caug = [R("caug%d" % i) for i in range(S // 256)]
            lsum = small[:, 16:18]; lexp = small[:, 18:20]; ltmp = small[:, 20:21]; neglam = small[:, 21:22]
            r_lam = R("sm_lam")
            gnp = small[:, 24:28]; r_gnp = R("sm_gnp")
            rden = [small[:, 32 + 4 * i:36 + 4 * i] for i in range(2)]; r_rden = [R("sm_rden%d" % i) for i in range(2)]
            ssq = small[:, 40:44]; lnv = small[:, 44:48]; rstd = small[:, 48:52]; r_rs = R("sm_rs")

            P.op("sp", I("dma_start", out=cmask, in_=cmask_d[:, :]), writes=[r_cm], dma_chan=r_cm)
            P.op("sp", I("dma_start", out=dmask, in_=dmask_d[:, :]), writes=[r_dm], dma_chan=r_dm)
            P.op("sp", I("dma_start", out=dl, in_=diff_lam_d[l:l + 1, :].to_broadcast([128, 256])), writes=[r_dl], dma_chan=r_dl)
            for a in range(4):
                P.op("sp", I("dma_start", out=gn[:, a:a + 1],
                                                      in_=diff_ng_d[l, a * 128:(a + 1) * 128].rearrange("(p o) -> p o", o=1)),
                     writes=[r_gn], dma_chan=r_gn)
            P.op("sp", I("dma_start", out=fb8[:, 0:1], in_=fox_fb_d[l, :].rearrange("(p o) -> p o", o=1)),
                 writes=[r_fb8], dma_chan=r_fb8)
            P.op("pool", I("dma_start", out=Wf, in_=w_in_d[l, :, OFF["fox_f"]:OFF["fox_f"] + 8].rearrange("(c p) n -> p c n", p=128)),
                 writes=[r_Wf], dma_chan=r_Wf)
            P.op("dve", I("scalar_tensor_tensor", out=junkf[:, 0:64], in0=dl[:, 0:64], scalar=1.0, in1=dl[:, 64:128],
                                                         op0=ALU.mult, op1=ALU.mult, accum_out=lsum[:, 0:1]),
                 reads=[r_dl], writes=[r_junkf, r_lam])
            P.op("dve", I("scalar_tensor_tensor", out=junkf[:, 0:64], in0=dl[:, 128:192], scalar=1.0, in1=dl[:, 192:256],
                                                         op0=ALU.mult, op1=ALU.mult, accum_out=lsum[:, 1:2]),
                 reads=[r_dl], writes=[r_junkf, r_lam])
            P.op("act", I("activation", out=lexp, in_=lsum, func=AF.Exp), reads=[r_lam], writes=[r_lam])
            P.op("dve", I("tensor_tensor", out=ltmp, in0=lexp[:, 0:1], in1=lexp[:, 1:2], op=ALU.subtract), reads=[r_lam], writes=[r_lam])
            P.op("dve", I("tensor_scalar", out=neglam, in0=ltmp, scalar1=-1.0, scalar2=-lam_init, op0=ALU.mult, op1=ALU.add),
                 reads=[r_lam], writes=[r_lam])
            P.op("dve", I("tensor_scalar", out=gnp, in0=gn, scalar1=1.0 - lam_init, scalar2=None, op0=ALU.mult),
                 reads=[r_gn], writes=[r_gnp])
            P.op("dve", I("tensor_scalar", out=fb8[:, 1:2], in0=fb8[:, 0:1], scalar1=-1.0, scalar2=None, op0=ALU.mult),
                 reads=[r_fb8], writes=[r_fb8])
            P.op("dve", I("memset", ones8, 1.0), writes=[r_ones8])

            prevC = None
            for ch in range(NG):
                b = misc_bank()
                for c in range(8):
                    P.op("pe", I("matmul", banks[b][0:8, :], lhsT=Wf[:, c, :], rhs=hT[:, c, ch * 512:(ch + 1) * 512],
                                                                   start=(c == 0), stop=(c == 7)),
                         reads=[r_Wf] + r_hT[ch * 4:ch * 4 + 4], writes=[r_bank[b]])
                for hf in range(2):
                    i = ch * 2 + hf
                    s = i % 2
                    P.op("act", I("activation", out=fe, in_=banks[b][0:8, hf * 256:(hf + 1) * 256], func=AF.Exp,
                                                                   scale=-1.0, bias=fb8[:, 1:2]),
                         reads=[r_bank[b], r_fb8], writes=[r_fe])
                    P.op("act", I("activation", out=fsp, in_=fe, func=AF.Ln, bias=1.0, scale=1.0), reads=[r_fe], writes=[r_fsp])
                    init = 0.0 if prevC is None else prevC[:, 255:256]
                    P.op("dve", I("tensor_tensor_scan", out=fC[s], data0=ones8, data1=fsp, initial=init,
                                                                              op0=ALU.mult, op1=ALU.add),
                         reads=[r_ones8, r_fsp, r_fC[1 - s]], writes=[r_fC[s]])
                    prevC = fC[s]
                    a6 = aug6[s]
                    P.op("dve", I("tensor_copy", out=a6[:, 0, :], in_=fC[s]), reads=[r_fC[s]], writes=[r_aug6[s]])
                    P.op("dve", I("tensor_tensor", out=fr, in0=fC[s], in1=a6[:, 0, :], op=ALU.subtract),
                         reads=[r_fC[s], r_aug6[s]], writes=[r_fr])
                    P.op("dve", I("tensor_copy", out=a6[:, 1, :], in_=fr), reads=[r_fr], writes=[r_aug6[s]])
                    P.op("dve", I("tensor_tensor", out=fr, in0=fr, in1=a6[:, 1, :], op=ALU.subtract),
                         reads=[r_fr, r_aug6[s]], writes=[r_fr])
                    P.op("dve", I("tensor_copy", out=a6[:, 2, :], in_=fr), reads=[r_fr], writes=[r_aug6[s]])
                    P.op("dve", I("tensor_scalar", out=a6[:, 3:6, :], in0=a6[:, 0:3, :], scalar1=-1.0, scalar2=None, op0=ALU.mult),
                         reads=[r_aug6[s]], writes=[r_aug6[s]])
                    P.op("sp", I("dma_start", out=caug_d[:, :, i * 256:(i + 1) * 256], in_=a6),
                         reads=[r_aug6[s]], writes=[r_caug[i]], dma_chan=r_aug6[s])

            maps = []
            for a in range(4):
                for c in range(2):
                    maps.append(dict(br=0, unit=a, sub=c, qoff=OFF["diff_q"] + a * 128 + c * 64, koff=OFF["diff_k"] + a * 128 + c * 64,
                                     kind="alibi", slope=2 * (a + 1) - 1, Kc=68, band=None, E=128, vh=a, full=True))
            for h in range(8):
                maps.append(dict(br=1, unit=4 + h // 2, sub=h % 2, qoff=OFF["fox_q"] + h * 64, koff=OFF["fox_k"] + h * 64,
                                 kind="fox", head=h, Kc=70, band=None, E=64, vh=h, full=True))
            for h in range(8):
                maps.append(dict(br=2, unit=8 + h // 2, sub=h % 2, qoff=OFF["dil_q"] + h * 64, koff=OFF["dil_k"] + h * 64,
                                 kind="alibi", slope=h, Kc=68, band=17, E=64, vh=h, full=False))
            voff = [OFF["diff_v"], OFF["fox_v"], OFF["dil_v"]]
            zoff = [OFF["diff_z"], OFF["fox_z"], OFF["dil_z"]]

            def Vview(br):
                if br == 0:
                    return Vaug[:, 0:NT * 516].rearrange("p (t h e) -> p t h e", t=NT, h=4)
                return Vaug.rearrange("p (t h e) -> p t h e", t=NT, h=8)

            def load_wqk(mi):
                m = maps[mi]
                s = mi % 2
                P.op("pool", I("dma_start", out=Wq[s], in_=w_in_d[l, :, m["qoff"]:m["qoff"] + 64].rearrange("(c p) n -> p c n", p=128)),
                     writes=[r_Wq[s]], dma_chan=r_Wq[s])
                P.op("pool", I("dma_start", out=Wk[s], in_=w_in_d[l, :, m["koff"]:m["koff"] + 64].rearrange("(c p) n -> p c n", p=128)),
                     writes=[r_Wk[s]], dma_chan=r_Wk[s])
                P.op("pool", I("memset", QT[s][64:128, :], 0.0), writes=[r_QT[s]])
                P.op("pool", I("memset", KT[s][64:128, :], 0.0), writes=[r_KT[s]])
                if m["kind"] == "alibi":
                    P.op("sp", I("dma_start", out=QT[s][64:68, :], in_=qaug_d[:, :]), writes=[r_QT[s]], dma_chan=r_QT[s])
                    P.op("sp", I("dma_start", out=KT[s][64:68, :], in_=kaug_d[m["slope"], :, :]), writes=[r_KT[s]], dma_chan=r_KT[s])
                else:
                    h = m["head"]
                    P.op("sp", I("dma_start", out=QT[s][64:67, :], in_=ones3_d[:, :]), writes=[r_QT[s]], dma_chan=r_QT[s])
                    P.op("sp", I("dma_start", out=QT[s][67:70, :], in_=caug_d[h, 3:6, :]), reads=r_caug, writes=[r_QT[s]], dma_chan=r_QT[s])
                    P.op("sp", I("dma_start", out=KT[s][64:67, :], in_=caug_d[h, 0:3, :]), reads=r_caug, writes=[r_KT[s]], dma_chan=r_KT[s])
                    P.op("sp", I("dma_start", out=KT[s][67:70, :], in_=ones3_d[:, :]), writes=[r_KT[s]], dma_chan=r_KT[s])

            def proj_chunk(mi, ch):
                s = mi % 2
                hts = r_hT[ch * 4:ch * 4 + 4]
                b = misc_bank()
                for c in range(8):
                    P.op("pe", I("matmul", banks[b][0:64, :], lhsT=Wq[s][:, c, :], rhs=hT[:, c, ch * 512:(ch + 1) * 512],
                                                       start=(c == 0), stop=(c == 7)),
                         reads=[r_Wq[s]] + hts, writes=[r_bank[b]])
                P.op("dve", I("tensor_scalar", out=QT[s][0:64, ch * 512:(ch + 1) * 512], in0=banks[b][0:64, :], scalar1=0.125,
                                                      scalar2=None, op0=ALU.mult),
                     reads=[r_bank[b]], writes=[r_QT[s]])
                b2 = misc_bank()
                for c in range(8):
                    P.op("pe", I("matmul", banks[b2][0:64, :], lhsT=Wk[s][:, c, :], rhs=hT[:, c, ch * 512:(ch + 1) * 512],
                                                       start=(c == 0), stop=(c == 7)),
                         reads=[r_Wk[s]] + hts, writes=[r_bank[b2]])
                P.op("dve", I("tensor_copy", out=KT[s][0:64, ch * 512:(ch + 1) * 512], in_=banks[b2][0:64, :]),
                     reads=[r_bank[b2]], writes=[r_KT[s]])

            def load_wz(u):
                s = u % 2
                br, j = u // 4, u % 4
                o = zoff[br] + j * 128
                P.op("pool", I("dma_start", out=Wz[s], in_=w_in_d[l, :, o:o + 128].rearrange("(c p) n -> p c n", p=128)),
                     writes=[r_Wz[s]], dma_chan=r_Wz[s])

            def z_chunk(u, ch):
                s = u % 2
                b = misc_bank()
                for c in range(8):
                    P.op("pe", I("matmul", banks[b][:, :], lhsT=Wz[s][:, c, :], rhs=hT[:, c, ch * 512:(ch + 1) * 512],
                                                       start=(c == 0), stop=(c == 7)),
                         reads=[r_Wz[s]] + r_hT[ch * 4:ch * 4 + 4], writes=[r_bank[b]])
                zs = nxt("ze", 2)
                P.op("act", I("activation", out=ze[zs], in_=banks[b][:, :], func=AF.Exp, scale=-1.0), reads=[r_bank[b]], writes=[r_ze[zs]])
                P.op("dve", I("tensor_scalar", out=ze[zs], in0=ze[zs], scalar1=1.0, scalar2=None, op0=ALU.add), reads=[r_ze[zs]], writes=[r_ze[zs]])
                P.op("dve", I("reciprocal", out=ze[zs], in_=ze[zs]), reads=[r_ze[zs]], writes=[r_ze[zs]])
                P.op("dve", I("tensor_tensor", out=siluT[:, ch * 512:(ch + 1) * 512], in0=banks[b][:, :], in1=ze[zs], op=ALU.mult),
                     reads=[r_bank[b], r_ze[zs]], writes=[r_silu[ch]])

            def load_wv(br):
                o = voff[br]
                for hh in range(2):
                    P.op("pool", I("dma_start", out=Wv[:, :, hh * 256:(hh + 1) * 256],
                                                              in_=w_in_d[l, :, o + hh * 256:o + (hh + 1) * 256].rearrange("(c p) n -> p c n", p=128)),
                         writes=[r_Wv], dma_chan=r_Wv)

            def branch_setup(br):
                Vv = Vview(br)
                H, E = (4, 128) if br == 0 else (8, 64)
                P.op("pool", I("memset", Vv[:, :, :, E:E + 1], 1.0), writes=[r_V])
                for t in range(NT):
                    b = misc_bank()
                    for c in range(8):
                        P.op("pe", I("matmul", banks[b][:, :], lhsT=hT[:, c, t * 128:(t + 1) * 128], rhs=Wv[:, c, :],
                                                                     start=(c == 0), stop=(c == 7)),
                             reads=[r_Wv, r_hT[t]], writes=[r_bank[b]])
                    src = banks[b][:, :].rearrange("p (h e) -> p h e", h=H)
                    if t % 2 == 0:
                        P.op("act", I("activation", out=Vv[:, t, :, 0:E], in_=src, func=AF.Copy),
                             reads=[r_bank[b]], writes=[r_V])
                    else:
                        P.op("dve", I("tensor_copy", out=Vv[:, t, :, 0:E], in_=src), reads=[r_bank[b]], writes=[r_V])

            ACCB = (2, 3, 4, 5)

            def attention(mi, filler):
                m = maps[mi]
                s = mi % 2
                E = m["E"]
                Kc = m["Kc"]
                Vv = Vview(m["br"])
                nbk = 2 if E == 128 else 1
                steps = []

                def make_evac(g, b0, accv, bkof):
                    def evac():
                        rs = nxt("rden", 2)
                        rd = rden[rs]
                        if E == 64:
                            denv = banks[b0][:, 0:260].rearrange("p (j e) -> p j e", j=4)[:, :, 64]
                            P.op("dve", I("reciprocal", out=rd, in_=denv), reads=[r_bank[b0]], writes=[r_rden[rs]])
                            for j in range(4):
                                P.op("dve", I("tensor_scalar", out=O[:, 4 * g + j, m["sub"] * 64:(m["sub"] + 1) * 64], in0=accv[j][:, 0:64],
                                              scalar1=rd[:, j:j + 1], scalar2=None, op0=ALU.mult),
                                     reads=[r_bank[b0], r_rden[rs]], writes=[r_O[g]])
                        else:
                            for j in range(4):
                                P.op("dve", I("reciprocal", out=rd[:, j:j + 1], in_=accv[j][:, 128:129]),
                                     reads=[r_bank[bkof[j]]], writes=[r_rden[rs]])
                            if m["sub"] == 0:
                                for j in range(4):
                                    P.op("dve", I("tensor_scalar", out=O[:, 4 * g + j, :], in0=accv[j][:, 0:128], scalar1=rd[:, j:j + 1],
                                                  scalar2=None, op0=ALU.mult),
                                         reads=[r_bank[bkof[j]], r_rden[rs]], writes=[r_O[g]])
                            else:
                                P.op("dve", I("tensor_scalar", out=rd, in0=rd, scalar1=neglam, scalar2=None, op0=ALU.mult),
                                     reads=[r_rden[rs], r_lam], writes=[r_rden[rs]])
                                for j in range(4):
                                    P.op("dve", I("scalar_tensor_tensor", out=O[:, 4 * g + j, :], in0=accv[j][:, 0:128], scalar=rd[:, j:j + 1],
                                                  in1=O[:, 4 * g + j, :], op0=ALU.mult, op1=ALU.add),
                                         reads=[r_bank[bkof[j]], r_rden[rs], r_O[g]], writes=[r_O[g]])
                                    P.op("dve", I("scalar_tensor_tensor", out=junkf, in0=O[:, 4 * g + j, :], scalar=1.0, in1=O[:, 4 * g + j, :],
                                                  op0=ALU.mult, op1=ALU.mult, accum_out=ssq[:, j:j + 1]),
                                         reads=[r_O[g]], writes=[r_junkf, r_rs])
                                P.op("act", I("activation", out=lnv, in_=ssq, func=AF.Ln, scale=1.0 / 128, bias=EPS), reads=[r_rs], writes=[r_rs])
                                P.op("act", I("activation", out=rstd, in_=lnv, func=AF.Exp, scale=-0.5), reads=[r_rs], writes=[r_rs])
                                for j in range(4):
                                    P.op("dve", I("tensor_scalar", out=O[:, 4 * g + j, :], in0=O[:, 4 * g + j, :], scalar1=rstd[:, j:j + 1],
                                                  scalar2=None, op0=ALU.mult),
                                         reads=[r_O[g], r_rs], writes=[r_O[g]])
                    return evac

                STB = (0, 1) if nbk == 2 else (0, 1, 4, 5)
                LOOK = 1 if nbk == 2 else 2
                for g in range(NG):
                    if nbk == 2:
                        b0 = ACCB[2 * nxt("acc2", 2)]
                        ring["acc1"] = 0
                        bks = [b0, b0 + 1]
                        accv = [banks[bks[j // 2]][:, (j % 2) * 129:(j % 2) * 129 + 129] for j in range(4)]
                        bkof = [bks[j // 2] for j in range(4)]
                    else:
                        b0 = ACCB[nxt("acc1", 2)]
                        ring["acc2"] = 0
                        bks = [b0]
                        accv = [banks[b0][:, j * 65:j * 65 + 65] for j in range(4)]
                        bkof = [b0] * 4
                    first = {b: True for b in bks}
                    kb_lo = 0 if m["full"] else max(0, 4 * g - 16)
                    for kb in range(kb_lo, 4 * g + 4):
                        jlo = max(0, kb - 4 * g)
                        jhi = 3 if m["full"] else min(3, kb + 16 - 4 * g)
                        pvs = []
                        for j in range(jlo, jhi + 1):
                            bk = bkof[j]
                            pvs.append((j, bk, first[bk], 4 * g + j, accv[j]))
                            first[bk] = False
                        steps.append(dict(g=g, kb=kb, jlo=jlo, N=(jhi - jlo + 1) * 128, q0=(4 * g + jlo) * 128, sb=STB[nxt("st", len(STB))], p=nxt("pt", 4),
                                          pvs=pvs, last=(kb == 4 * g + 3), evac=make_evac(g, b0, accv, bkof)))

                def do_qk(t):
                    kb, N, q0, sb_ = t["kb"], t["N"], t["q0"], t["sb"]
                    P.op("pe", I("matmul", banks[sb_][:, 0:N], lhsT=KT[s][:, kb * 128:(kb + 1) * 128],
                                 rhs=QT[s][:, q0:q0 + N], start=True, stop=True),
                         reads=[r_KT[s], r_QT[s]], writes=[r_bank[sb_]])

                def do_exp(t):
                    kb, N, sb_, p, g, jlo = t["kb"], t["N"], t["sb"], t["p"], t["g"], t["jlo"]
                    P.op("act", I("activation", out=PT[p][:, 0:N], in_=banks[sb_][:, 0:N], func=AF.Exp),
                         reads=[r_bank[sb_]], writes=[r_PT[p]])
                    if m["full"]:
                        if kb >= 4 * g:
                            P.op("dve", I("tensor_tensor", out=PT[p][:, 0:128], in0=PT[p][:, 0:128], in1=cmask, op=ALU.mult),
                                 reads=[r_PT[p], r_cm], writes=[r_PT[p]])
                    else:
                        dlo = 4 * g + jlo - kb
                        P.op("dve", I("tensor_tensor", out=PT[p][:, 0:N], in0=PT[p][:, 0:N],
                                      in1=dmask[:, dlo * 128:dlo * 128 + N], op=ALU.mult),
                             reads=[r_PT[p], r_dm], writes=[r_PT[p]])

                def do_pv(t):
                    kb, p, jlo = t["kb"], t["p"], t["jlo"]
                    for (j, bk, stf, qb, av) in t["pvs"]:
                        P.op("pe", I("matmul", av, lhsT=PT[p][:, (j - jlo) * 128:(j - jlo + 1) * 128], rhs=Vv[:, kb, m["vh"], :],
                                     start=stf, stop=(kb == qb), skip_group_check=True),
                             reads=[r_PT[p], r_V], writes=[r_bank[bk]])

                n = len(steps)
                for i in range(min(LOOK, n)):
                    do_qk(steps[i])
                for i, t in enumerate(steps):
                    if i + LOOK < n:
                        do_qk(steps[i + LOOK])
                    do_exp(t)
                    do_pv(t)
                    if t["last"]:
                        t["evac"]()
                        filler(t["g"])

            def finalize_unit(u):
                br = u // 4
                for g in range(NG):
                    ob = nxt("obf", 2)
                    P.op("pool", I("tensor_copy", out=Obf[ob], in_=O[:, 4 * g:4 * g + 4, :]), reads=[r_O[g]], writes=[r_Obf[ob]])
                    b = misc_bank()
                    tpv = banks[b][:, :].bitcast(BF16)
                    for j in range(4):
                        P.op("pe", I("transpose", out=tpv[:, j * 128:(j + 1) * 128], in_=Obf[ob][:, j, :], identity=identb[:, :]),
                             reads=[r_Obf[ob], R("identb")], writes=[r_bank[b]])
                    ys = nxt("ys", 2)
                    if br == 0:
                        a = u % 4
                        P.op("dve", I("scalar_tensor_tensor", out=ystage[ys], in0=tpv[:, 0:512], scalar=gnp[:, a:a + 1],
                                      in1=siluT[:, g * 512:(g + 1) * 512], op0=ALU.mult, op1=ALU.mult),
                             reads=[r_bank[b], r_gnp, r_silu[g]], writes=[r_ys[ys]])
                    else:
                        P.op("dve", I("tensor_tensor", out=ystage[ys], in0=tpv[:, 0:512], in1=siluT[:, g * 512:(g + 1) * 512], op=ALU.mult),
                             reads=[r_bank[b], r_silu[g]], writes=[r_ys[ys]])
                    P.op("sp", I("dma_start", out=yT_d[u, :, g * 512:(g + 1) * 512], in_=ystage[ys]),
                         reads=[r_ys[ys]], dma_chan=r_ys[ys])

            nm = len(maps)
            load_wqk(0)
            load_wv(0)
            for ch in range(NG):
                proj_chunk(0, ch)
            for mi, m in enumerate(maps):
                u = m["unit"]
                if mi + 1 < nm:
                    load_wqk(mi + 1)
                if m["sub"] == 0:
                    load_wz(u)
                if mi % 8 == 0:
                    branch_setup(m["br"])
                    if m["br"] < 2:
                        load_wv(m["br"] + 1)

                def filler(g, mi=mi, m=m, u=u):
                    if mi + 1 < nm:
                        proj_chunk(mi + 1, g)
                    if m["sub"] == 1:
                        z_chunk(u, g)
                attention(mi, filler)
                if m["sub"] == 1:
                    finalize_unit(u)

        def phaseC(l):
            P.epoch += 1
            A.reset()
            last = (l == DEPTH - 1)
            Wb = A.alloc([128, 12, D], BF16); r_Wb = [R("Wb%d" % f) for f in range(8)]
            Wg = A.alloc([128, 8, 3072], BF16); r_Wg = [R("Wg%d" % f) for f in range(8)]
            Wo = A.alloc([128, 8, D], BF16); r_Wo = R("Wo")
            yc = A.alloc([128, 12, 512], BF16); r_yc = R("yc")
            mT = A.alloc([128, 8, 512], BF16); r_mT = [R("mT%d" % f) for f in range(8)]
            sg = [A.alloc([128, 512], F32) for _ in range(2)]; r_sg = [R("sg%d" % i) for i in range(2)]
            macc = A.alloc([128, 512], F32); r_macc = R("macc")
            tmp = A.alloc([128, 512], F32); r_tmp = R("tmp")
            xt = [A.alloc([128, D], F32) for _ in range(2)]; r_xt = [R("cxt%d" % i) for i in range(2)]
            hb = [A.alloc([128, D], BF16) for _ in range(2)]; r_hb = [R("chb%d" % i) for i in range(2)]
            junk = A.alloc([128, D], BF16); r_junk = R("cjunk")
            gbc = A.alloc([128, D], F32); r_gbc = R("cgbc")
            load_gbc(gbc, r_gbc, final_g_d[0:1, :] if last else norm_g_d[l + 1:l + 2, :])
            wbv = w_br_d[l].rearrange("(u p) f -> p u f", p=128)
            wgv = w_in_d[l, :, OFF["merge_g"]:OFF["merge_g"] + 3072].rearrange("(c p) (n q) -> p c n q", p=128, n=3)
            Wg4 = Wg.rearrange("p c (n q) -> p c n q", n=3)
            for f in range(8):
                P.op("pool", I("dma_start", out=Wb[:, :, f * 128:(f + 1) * 128], in_=wbv[:, :, f * 128:(f + 1) * 128]),
                     writes=[r_Wb[f]], dma_chan=r_Wb[f])
                for n in range(3):
                    P.op("pool", I("dma_start", out=Wg4[:, :, n, f * 128:(f + 1) * 128], in_=wgv[:, :, n, f * 128:(f + 1) * 128]),
                         writes=[r_Wg[f]], dma_chan=r_Wg[f])
            wov = w_out_d[l].rearrange("(f p) o -> p f o", p=128)
            for hh in range(2):
                P.op("pool", I("dma_start", out=Wo[:, :, hh * 512:(hh + 1) * 512], in_=wov[:, :, hh * 512:(hh + 1) * 512]),
                     writes=[r_Wo], dma_chan=r_Wo)
            xsrc = x_d if l == 0 else xs_d
            CB = (0, 1, 2, 3)
            OB = (4, 5)
            for tc in range(NG):
                P.op("sp", I("dma_start", out=yc, in_=yT_d[:, :, tc * 512:(tc + 1) * 512].rearrange("u p s -> p u s")),
                     writes=[r_yc], dma_chan=r_yc)
                hts = r_hT[tc * 4:tc * 4 + 4]
                for f in range(8):
                    for n in range(3):
                        bb = CB[nxt("cb", 4)]
                        for j in range(4):
                            P.op("pe", I("matmul", banks[bb][:, :], lhsT=Wb[:, n * 4 + j, f * 128:(f + 1) * 128], rhs=yc[:, n * 4 + j, :],
                                                                                start=(j == 0), stop=(j == 3)),
                                 reads=[r_Wb[f], r_yc], writes=[r_bank[bb]])
                        gb = CB[nxt("cb", 4)]
                        for c in range(8):
                            P.op("pe", I("matmul", banks[gb][:, :], lhsT=Wg[:, c, n * 1024 + f * 128:n * 1024 + (f + 1) * 128],
                                                                                       rhs=hT[:, c, tc * 512:(tc + 1) * 512], start=(c == 0), stop=(c == 7)),
                                 reads=[r_Wg[f]] + hts, writes=[r_bank[gb]])
                        sgi = nxt("sg", 2)
                        P.op("act", I("activation", out=sg[sgi], in_=banks[gb][:, :], func=AF.Sigmoid),
                             reads=[r_bank[gb]], writes=[r_sg[sgi]])
                        if n == 0:
                            P.op("dve", I("tensor_tensor", out=macc, in0=banks[bb][:, :], in1=sg[sgi], op=ALU.mult),
                                 reads=[r_bank[bb], r_sg[sgi]], writes=[r_macc])
                        else:
                            P.op("dve", I("tensor_tensor", out=tmp, in0=banks[bb][:, :], in1=sg[sgi], op=ALU.mult),
                                 reads=[r_bank[bb], r_sg[sgi]], writes=[r_tmp])
                            if n == 1:
                                P.op("dve", I("tensor_tensor", out=macc, in0=macc, in1=tmp, op=ALU.add), reads=[r_macc, r_tmp], writes=[r_macc])
                            else:
                                P.op("dve", I("tensor_tensor", out=mT[:, f, :], in0=macc, in1=tmp, op=ALU.add),
                                     reads=[r_macc, r_tmp], writes=[r_mT[f]])
                for tt in range(4):
                    t = tc * 4 + tt
                    s = nxt("cxt", 2)
                    P.op("sp", I("dma_start", out=xt[s], in_=xsrc[t * 128:(t + 1) * 128, :]), writes=[r_xt[s]], dma_chan=r_xt[s])
                    for hh in range(2):
                        ob = OB[nxt("ob", 2)]
                        for f in range(8):
                            P.op("pe", I("matmul", banks[ob][:, :], lhsT=mT[:, f, tt * 128:(tt + 1) * 128],
                                                                                    rhs=Wo[:, f, hh * 512:(hh + 1) * 512], start=(f == 0), stop=(f == 7)),
                                 reads=r_mT + [r_Wo], writes=[r_bank[ob]])
                        P.op("dve", I("tensor_tensor", out=xt[s][:, hh * 512:(hh + 1) * 512], in0=banks[ob][:, :],
                                                                                 in1=xt[s][:, hh * 512:(hh + 1) * 512], op=ALU.add),
                             reads=[r_bank[ob], r_xt[s]], writes=[r_xt[s]])
                    if not last:
                        P.op("sp", I("dma_start", out=xs_d[t * 128:(t + 1) * 128, :], in_=xt[s]), reads=[r_xt[s]], dma_chan=r_xt[s])
                        norm_tile(xt[s], r_xt[s], t, gbc, r_gbc, (hb, r_hb, junk, r_junk))
                    else:
                        fs = nxt("fss", 2)
                        ss = small[:, 8 + 4 * fs:12 + 4 * fs]
                        r_ss = R("sm_fss%d" % fs)
                        P.op("act", I("activation", out=junk, in_=xt[s], func=AF.Square, accum_out=ss[:, 0:1]),
                             reads=[r_xt[s]], writes=[r_junk, r_ss])
                        P.op("act", I("activation", out=ss[:, 1:2], in_=ss[:, 0:1], func=AF.Sqrt, scale=1.0 / D, bias=EPS),
                             reads=[r_ss], writes=[r_ss])
                        P.op("dve", I("reciprocal", out=ss[:, 2:3], in_=ss[:, 1:2]), reads=[r_ss], writes=[r_ss])
                        P.op("dve", I("scalar_tensor_tensor", out=xt[s], in0=xt[s], scalar=ss[:, 2:3], in1=gbc, op0=ALU.mult, op1=ALU.mult),
                             reads=[r_xt[s], r_ss, r_gbc], writes=[r_xt[s]])
                        P.op("sp", I("dma_start", out=out_d[t * 128:(t + 1) * 128, :], in_=xt[s]), reads=[r_xt[s]], dma_chan=r_xt[s])

        phaseA()
        for l in range(DEPTH):
            P.barrier()
            phaseB(l)
            P.barrier()
            phaseC(l)
        stats = P.emit(nc, st)
        stats["arena_peak"] = A.peak
        build.stats = stats
    return nc


def make_consts(S):
    bf = ml_dtypes.bfloat16
    t = np.arange(S)
    qaug = np.stack([(t // 64) * 64, t % 64, np.ones(S), np.ones(S)]).astype(np.float32)
    kaug = np.zeros((8, 4, S), np.float32)
    for j in range(8):
        sl = 2.0 ** -(j + 1)
        kaug[j, 0] = -sl
        kaug[j, 1] = -sl
        kaug[j, 2] = sl * ((t // 64) * 64)
        kaug[j, 3] = sl * (t % 64)
    k = np.arange(128)[:, None]
    q = np.arange(128)[None, :]
    cmask = (q >= k).astype(np.float32)
    dm = np.zeros((128, 17, 128), np.float32)
    for dlt in range(17):
        d = 128 * dlt + q - k
        mult = ((d >= 0) & (d <= 128)).astype(np.float32) + ((d >= 0) & (d % 4 == 0) & (d <= 512)) + ((d >= 0) & (d % 16 == 0) & (d <= 2048))
        dm[:, dlt, :] = mult
    return dict(ident_bf=np.eye(128).astype(bf), ident_f=np.eye(128).astype(np.float32), qaug=qaug.astype(bf), kaug=kaug.astype(bf),
                ones3=np.ones((3, S)).astype(bf), cmask=cmask.astype(bf), dmask=dm.reshape(128, 17 * 128).astype(bf))


def kernel(x, norm_g, w_in, fox_fb, diff_lam, diff_norm_g, w_branch, w_out, final_g):
    x = np.asarray(x, np.float32)
    B, S, _ = x.shape
    DEPTH = norm_g.shape[0]
    nc = build(S, DEPTH)
    shared = dict(norm_g=np.ascontiguousarray(norm_g, np.float32), w_in=np.ascontiguousarray(w_in, np.float32),
                  fox_fb=np.ascontiguousarray(fox_fb, np.float32),
                  diff_lam=np.ascontiguousarray(np.asarray(diff_lam, np.float32).reshape(DEPTH, 256)),
                  diff_norm_g=np.ascontiguousarray(diff_norm_g, np.float32),
                  w_branch=np.ascontiguousarray(np.asarray(w_branch, np.float32).reshape(DEPTH, 1536, D)),
                  w_out=np.ascontiguousarray(w_out, np.float32),
                  final_g=np.ascontiguousarray(np.asarray(final_g, np.float32).reshape(1, D)))
    shared.update(make_consts(S))
    in_maps = [dict(shared, x=np.ascontiguousarray(x[b])) for b in range(B)]
    res = run_bass_kernel_spmd(nc, in_maps, core_ids=list(range(B)))
    return np.stack([np.asarray(r["out"], np.float32) for r in res.results], axis=0)
```

```python
import math
import numpy as np
import ml_dtypes
from contextlib import ExitStack
import concourse.bass as bass
import concourse.mybir as mybir
from concourse.bass_utils import run_bass_kernel_spmd

F32 = mybir.dt.float32
BF16 = mybir.dt.bfloat16
AF = mybir.ActivationFunctionType
ALU = mybir.AluOpType

D = 1024
INW = 9224
OFF = dict(diff_q=0, diff_k=512, diff_v=1024, diff_z=1536, fox_q=2048, fox_k=2560, fox_v=3072,
           fox_f=3584, fox_z=3592, dil_q=4104, dil_k=4616, dil_v=5128, dil_z=5640, merge_g=6152)
EPS = 1e-6
ENGS = ("pe", "act", "dve", "pool", "sp")
SWDGE_DEPTH = 3
SAME_ENG_WINDOW = 4


def I(meth, *args, **kw):
    f = lambda e: getattr(e, meth)(*args, **kw)
    f.multi = (meth in ("matmul", "transpose")) or (kw.get("accum_out") is not None)
    return f


class Res:
    __slots__ = ("name", "last_w", "readers", "dma_readers", "dsem", "dcount")

    def __init__(self, name):
        self.name = name
        self.last_w = None
        self.readers = {}
        self.dma_readers = []
        self.dsem = None
        self.dcount = 0


class Op:
    __slots__ = ("eng", "emit", "deps", "signal", "sigsem", "sigval", "is_dma", "epoch", "seq")

    def __init__(self, eng, emit, is_dma, epoch):
        self.seq = 0
        self.eng = eng
        self.emit = emit
        self.deps = []
        self.signal = False
        self.sigsem = None
        self.sigval = 0
        self.is_dma = is_dma
        self.epoch = epoch


class Prog:
    def __init__(self):
        self.ops = {e: [] for e in ENGS}
        self.epoch = 0
        self.chans = []
        self.chan_last = {}
        self.last_op = {}
        self.pending_bar = {e: [] for e in ENGS}
        self.res = {}
        self.pool_dmas = []

    def R(self, name):
        r = self.res.get(name)
        if r is None:
            r = self.res[name] = Res(name)
        return r

    def op(self, eng, emit, reads=(), writes=(), dma_chan=None):
        is_dma = dma_chan is not None
        o = Op(eng, emit, is_dma, self.epoch)
        o.seq = len(self.ops[eng])
        deps = list(self.pending_bar[eng])
        self.pending_bar[eng] = []
        for r in reads:
            if r.last_w is not None:
                deps.append(r.last_w)
        for r in writes:
            if r.last_w is not None:
                deps.append(r.last_w)
            deps.extend(r.readers.values())
            deps.extend(r.dma_readers)
        seen = set()
        for d in deps:
            if id(d) in seen:
                continue
            seen.add(id(d))
            if (not d.is_dma) and d.eng == eng and (eng == "pe" or o.seq - d.seq > SAME_ENG_WINDOW):
                continue
            o.deps.append(d)
            d.signal = True
        if is_dma and eng == "pool":
            self.pool_dmas.append(o)
            if len(self.pool_dmas) > SWDGE_DEPTH:
                d = self.pool_dmas[-1 - SWDGE_DEPTH]
                if id(d) not in seen:
                    o.deps.append(d)
        for r in reads:
            if is_dma:
                r.dma_readers.append(o)
            else:
                r.readers[eng] = o
        for r in writes:
            r.last_w = o
            r.readers = {}
            r.dma_readers = []
        if is_dma:
            if dma_chan.dsem is None:
                dma_chan.dsem = "ch%d" % len(self.chans)
                self.chans.append(dma_chan)
            dma_chan.dcount += 16
            o.sigsem = dma_chan.dsem
            o.sigval = dma_chan.dcount
            o.signal = True
            self.chan_last[dma_chan.dsem] = o
        else:
            self.last_op[eng] = o
        self.ops[eng].append(o)
        return o

    def barrier(self):
        deps = list(self.last_op.values()) + list(self.chan_last.values())
        for d in deps:
            d.signal = True
        for e in ENGS:
            self.pending_bar[e] = list(deps)

    def emit(self, nc, stack):
        semkeys = set()
        for e in ENGS:
            cnt = {}
            for o in self.ops[e]:
                if o.is_dma:
                    semkeys.add(o.sigsem)
                elif o.signal:
                    k = "%s_%d" % (e, o.epoch)
                    cnt[k] = cnt.get(k, 0) + 1
                    o.sigsem = k
                    o.sigval = cnt[k]
                    semkeys.add(k)
        sems = {k: stack.enter_context(nc.semaphore(k)) for k in sorted(semkeys)}
        block = stack.enter_context(nc.Block())
        engmap = {"pe": block.tensor, "act": block.scalar, "dve": block.vector,
                  "pool": block.gpsimd, "sp": block.sync}
        stats = {}
        chans = self.chans
        for e in ENGS:
            def body(eng, ops=self.ops[e], e=e):
                waited = {}
                nw = 0
                for o in ops:
                    need = {}
                    for d in o.deps:
                        if waited.get(d.sigsem, 0) >= d.sigval:
                            continue
                        need[d.sigsem] = max(need.get(d.sigsem, 0), d.sigval)
                    need = list(need.items())
                    attach = None
                    if need and e != "pe" and not getattr(o.emit, "multi", True):
                        attach = need.pop()
                    for k, v in need:
                        eng.wait_ge(sems[k], v)
                        waited[k] = v
                        nw += 1
                    ins = o.emit(eng)
                    if attach is not None:
                        ins._wait_ge(sems[attach[0]], attach[1])
                        waited[attach[0]] = attach[1]
                    if o.signal:
                        ins.then_inc(sems[o.sigsem], 16 if o.is_dma else 1)
                if e == "sp":
                    for r in chans:
                        eng.wait_ge(sems[r.dsem], r.dcount)
                stats[e] = (len(ops), nw)
            engmap[e](body)
        stats["nsem"] = len(sems)
        return stats


class Arena:
    def __init__(self, ap_f32):
        self.base = ap_f32
        self.cap = ap_f32.shape[1] * 4
        self.off = 0
        self.peak = 0

    def reset(self):
        self.off = 0

    def overlay(self, off):
        old = self.off
        self.off = off
        return old

    def alloc(self, shape, dtype):
        esz = 2 if dtype == BF16 else 4
        n = 1
        for s in shape[1:]:
            n *= s
        nbytes = (n * esz + 31) // 32 * 32
        assert self.off + nbytes <= self.cap, "arena overflow %d + %d > %d" % (self.off, nbytes, self.cap)
        v = self.base[:, self.off // 4:(self.off + nbytes) // 4]
        if dtype == BF16:
            v = v.bitcast(BF16)
        v = v[0:shape[0], 0:n]
        if len(shape) == 3:
            v = v.rearrange("p (a b) -> p a b", a=shape[1])
        elif len(shape) == 4:
            v = v.rearrange("p (a b c) -> p a b c", a=shape[1], b=shape[2])
        self.off += nbytes
        self.peak = max(self.peak, self.off)
        return v


def build(S, DEPTH):
    NT = S // 128
    NG = S // 512
    nc = bass.Bass("TRN2", target_bir_lowering=False)
    dram = lambda name, shape, dt, kind: nc.dram_tensor(name, shape, dt, kind=kind).ap()
    x_d = dram("x", [S, D], F32, "ExternalInput")
    norm_g_d = dram("norm_g", [DEPTH, D], F32, "ExternalInput")
    w_in_d = dram("w_in", [DEPTH, D, INW], F32, "ExternalInput")
    fox_fb_d = dram("fox_fb", [DEPTH, 8], F32, "ExternalInput")
    diff_lam_d = dram("diff_lam", [DEPTH, 256], F32, "ExternalInput")
    diff_ng_d = dram("diff_norm_g", [DEPTH, 512], F32, "ExternalInput")
    w_br_d = dram("w_branch", [DEPTH, 1536, D], F32, "ExternalInput")
    w_out_d = dram("w_out", [DEPTH, D, D], F32, "ExternalInput")
    final_g_d = dram("final_g", [1, D], F32, "ExternalInput")
    identb_d = dram("ident_bf", [128, 128], BF16, "ExternalInput")
    identf_d = dram("ident_f", [128, 128], F32, "ExternalInput")
    qaug_d = dram("qaug", [4, S], BF16, "ExternalInput")
    kaug_d = dram("kaug", [8, 4, S], BF16, "ExternalInput")
    ones3_d = dram("ones3", [3, S], BF16, "ExternalInput")
    cmask_d = dram("cmask", [128, 128], BF16, "ExternalInput")
    dmask_d = dram("dmask", [128, 17 * 128], BF16, "ExternalInput")
    out_d = dram("out", [S, D], F32, "ExternalOutput")
    xs_d = dram("xs_scr", [S, D], F32, "Internal")
    yT_d = dram("yT_scr", [12, 128, S], BF16, "Internal")
    caug_d = dram("caug_scr", [8, 6, S], BF16, "Internal")

    P = Prog()
    R = P.R
    st = ExitStack()
    with st:
        hT = st.enter_context(nc.sbuf_tensor("hT", [128, 8, S], BF16))
        identb = st.enter_context(nc.sbuf_tensor("identb", [128, 128], BF16))
        identf = st.enter_context(nc.sbuf_tensor("identf", [128, 128], F32))
        small = st.enter_context(nc.sbuf_tensor("small", [128, 64], F32))
        PERS = 8 * S * 2 + 256 + 512 + 256
        arena_t = st.enter_context(nc.sbuf_tensor("arena", [128, (212400 - PERS) // 4 - 64], F32))
        A = Arena(arena_t[:, :])
        banks = [st.enter_context(nc.psum_tensor("bank%d" % i, [128, 512], F32)) for i in range(8)]
        r_bank = [R("bank%d" % i) for i in range(8)]
        r_hT = [R("hT%d" % t) for t in range(NT)]
        r_small = {}

        def sm(name, lo, hi):
            r_small[name] = R("sm_" + name)
            return small[:, lo:hi]

        ring = {}

        def nxt(name, n):
            v = ring.get(name, 0)
            ring[name] = v + 1
            return v % n

        MISC = (6, 7)

        def misc_bank():
            return MISC[nxt("misc", 2)]

        P.op("sp", I("dma_start", out=identb[:, :], in_=identb_d[:, :]), writes=[R("identb")], dma_chan=R("identb"))
        P.op("sp", I("dma_start", out=identf[:, :], in_=identf_d[:, :]), writes=[R("identf")], dma_chan=R("identf"))

        def norm_tile(src, r_src, t, gbc, r_gbc, bufs):
            hb, r_hb, junk, r_junk = bufs
            s = nxt("hb", 2)
            ss = small[:, 0 + 4 * s:4 + 4 * s]
            r_ss = R("sm_ss%d" % s)
            P.op("act", I("activation", out=junk, in_=src, func=AF.Square, accum_out=ss[:, 0:1]),
                 reads=[r_src], writes=[r_junk, r_ss])
            P.op("act", I("activation", out=ss[:, 1:2], in_=ss[:, 0:1], func=AF.Sqrt, scale=1.0 / D, bias=EPS),
                 reads=[r_ss], writes=[r_ss])
            P.op("dve", I("reciprocal", out=ss[:, 2:3], in_=ss[:, 1:2]), reads=[r_ss], writes=[r_ss])
            P.op("dve", I("scalar_tensor_tensor", out=hb[s], in0=src, scalar=ss[:, 2:3], in1=gbc,
                                                         op0=ALU.mult, op1=ALU.mult),
                 reads=[r_src, r_ss, r_gbc], writes=[r_hb[s]])
            b = misc_bank()
            tpv = banks[b][:, :].bitcast(BF16)
            for c in range(8):
                P.op("pe", I("transpose", out=tpv[:, c * 128:(c + 1) * 128], in_=hb[s][:, c * 128:(c + 1) * 128],
                                                      identity=identb[:, :]),
                     reads=[r_hb[s], R("identb")], writes=[r_bank[b]])
            P.op("dve", I("tensor_copy", out=hT[:, :, t * 128:(t + 1) * 128],
                                                in_=tpv.rearrange("p (c k) -> p c k", c=8)),
                 reads=[r_bank[b]], writes=[r_hT[t]])
            return ss, r_ss

        def load_gbc(gbc, r_gbc, src_row):
            P.op("sp", I("dma_start", out=gbc, in_=src_row.to_broadcast([128, D])), writes=[r_gbc], dma_chan=r_gbc)

        def phaseA():
            P.epoch += 1
            A.reset()
            gbc = A.alloc([128, D], F32); r_gbc = R("gbc")
            xt = [A.alloc([128, D], F32) for _ in range(2)]; r_xt = [R("xt%d" % i) for i in range(2)]
            hb = [A.alloc([128, D], BF16) for _ in range(2)]; r_hb = [R("hb%d" % i) for i in range(2)]
            junk = A.alloc([128, D], BF16); r_junk = R("junk")
            load_gbc(gbc, r_gbc, norm_g_d[0:1, :])
            for t in range(NT):
                s = t % 2
                P.op("sp", I("dma_start", out=xt[s], in_=x_d[t * 128:(t + 1) * 128, :]),
                     writes=[r_xt[s]], dma_chan=r_xt[s])
                norm_tile(xt[s], r_xt[s], t, gbc, r_gbc, (hb, r_hb, junk, r_junk))

        def phaseB(l):
            P.epoch += 1
            A.reset()
            lam_init = 0.8 - 0.6 * math.exp(-0.3 * l)
            Vaug = A.alloc([128, NT * 520], BF16); r_V = R("Vaug")
            QT = [A.alloc([128, S], BF16) for _ in range(3)]; r_QT = [R("QT%d" % i) for i in range(3)]
            KT = [A.alloc([128, S], BF16) for _ in range(3)]; r_KT = [R("KT%d" % i) for i in range(3)]
            O_off = A.off
            O = A.alloc([128, NT, 128], F32); r_O = [R("O%d" % g) for g in range(NG)]
            siluT = A.alloc([128, S], BF16); r_silu = [R("silu%d" % g) for g in range(NG)]
            PT = [A.alloc([128, 512], BF16) for _ in range(3)]; r_PT = [R("PT%d" % i) for i in range(3)]
            Wv = A.alloc([128, 8, 512], BF16); r_Wv = R("Wv")
            Wz = [A.alloc([128, 8, 128], BF16) for _ in range(2)]; r_Wz = [R("Wz%d" % i) for i in range(2)]
            Wq = [A.alloc([128, 8, 128], BF16) for _ in range(2)]; r_Wq = [R("Wq%d" % i) for i in range(2)]
            Wk = [A.alloc([128, 8, 128], BF16) for _ in range(2)]; r_Wk = [R("Wk%d" % i) for i in range(2)]
            ystage = [A.alloc([128, 512], BF16) for _ in range(2)]; r_ys = [R("ys%d" % i) for i in range(2)]
            ze = [A.alloc([128, 512], F32) for _ in range(1)]; r_ze = [R("ze%d" % i) for i in range(1)]
            Obf = [A.alloc([128, 4, 128], BF16) for _ in range(2)]; r_Obf = [R("Obf%d" % i) for i in range(2)]
            cmask = A.alloc([128, 128], BF16); r_cm = R("cmask")
            dmask = A.alloc([128, 17 * 128], BF16); r_dm = R("dmask")
            junkf = A.alloc([128, 128], F32); r_junkf = R("junkf")
            dl = A.alloc([128, 256], F32); r_dl = R("dl")
            gn = A.alloc([128, 4], F32); r_gn = R("gn")
            Wf = A.alloc([128, 8, 8], BF16); r_Wf = R("Wf")
            use_ov = NT * 128 * 4 >= 9216
            _save = A.overlay(O_off) if use_ov else None
            fe = A.alloc([8, 256], F32); r_fe = R("fe")
            fsp = A.alloc([8, 256], F32); r_fsp = R("fsp")
            fC = [A.alloc([8, 256], F32) for _ in range(2)]; r_fC = [R("fC%d" % i) for i in range(2)]
            fr = A.alloc([8, 256], F32); r_fr = R("fr")
            ones8 = A.alloc([8, 256], F32); r_ones8 = R("ones8")
            aug6 = [A.alloc([8, 6, 256], BF16)] * 2; r_aug6 = [R("aug6_0")] * 2
            if use_ov:
                assert A.off <= O_off + NT * 128 * 4
                A.overlay(_save)
            fb8 = A.alloc([8, 2], F32); r_fb8 = R("fb8")
            r_caug = [R("caug%d" % i) for i in range(S // 256)]
            lsum = small[:, 16:18]; lexp = small[:, 18:20]; ltmp = small[:, 20:21]; neglam = small[:, 21:22]
            r_lam = R("sm_lam")
            gnp = small[:, 24:28]; r_gnp = R("sm_gnp")
            rden = [small[:, 32 + 4 * i:36 + 4 * i] for i in range(2)]; r_rden = [R("sm_rden%d" % i) for i in range(2)]
            ssq = small[:, 40:44]; lnv = small[:, 44:48]; rstd = small[:, 48:52]; r_rs = R("sm_rs")

            P.op("sp", I("dma_start", out=cmask, in_=cmask_d[:, :]), writes=[r_cm], dma_chan=r_cm)
            P.op("sp", I("dma_start", out=dmask, in_=dmask_d[:, :]), writes=[r_dm], dma_chan=r_dm)
            P.op("sp", I("dma_start", out=dl, in_=diff_lam_d[l:l + 1, :].to_broadcast([128, 256])), writes=[r_dl], dma_chan=r_dl)
            for a in range(4):
                P.op("sp", I("dma_start", out=gn[:, a:a + 1],
                                                      in_=diff_ng_d[l, a * 128:(a + 1) * 128].rearrange("(p o) -> p o", o=1)),
                     writes=[r_gn], dma_chan=r_gn)
            P.op("sp", I("dma_start", out=fb8[:, 0:1], in_=fox_fb_d[l, :].rearrange("(p o) -> p o", o=1)),
                 writes=[r_fb8], dma_chan=r_fb8)
            P.op("pool", I("dma_start", out=Wf, in_=w_in_d[l, :, OFF["fox_f"]:OFF["fox_f"] + 8].rearrange("(c p) n -> p c n", p=128)),
                 writes=[r_Wf], dma_chan=r_Wf)
            P.op("dve", I("scalar_tensor_tensor", out=junkf[:, 0:64], in0=dl[:, 0:64], scalar=1.0, in1=dl[:, 64:128],
                                                         op0=ALU.mult, op1=ALU.mult, accum_out=lsum[:, 0:1]),
                 reads=[r_dl], writes=[r_junkf, r_lam])
            P.op("dve", I("scalar_tensor_tensor", out=junkf[:, 0:64], in0=dl[:, 128:192], scalar=1.0, in1=dl[:, 192:256],
                                                         op0=ALU.mult, op1=ALU.mult, accum_out=lsum[:, 1:2]),
                 reads=[r_dl], writes=[r_junkf, r_lam])
            P.op("act", I("activation", out=lexp, in_=lsum, func=AF.Exp), reads=[r_lam], writes=[r_lam])
            P.op("dve", I("tensor_tensor", out=ltmp, in0=lexp[:, 0:1], in1=lexp[:, 1:2], op=ALU.subtract), reads=[r_lam], writes=[r_lam])
            P.op("dve", I("tensor_scalar", out=neglam, in0=ltmp, scalar1=-1.0, scalar2=-lam_init, op0=ALU.mult, op1=ALU.add),
                 reads=[r_lam], writes=[r_lam])
            P.op("dve", I("tensor_scalar", out=gnp, in0=gn, scalar1=1.0 - lam_init, scalar2=None, op0=ALU.mult),
                 reads=[r_gn], writes=[r_gnp])
            P.op("dve", I("tensor_scalar", out=fb8[:, 1:2], in0=fb8[:, 0:1], scalar1=-1.0, scalar2=None, op0=ALU.mult),
                 reads=[r_fb8], writes=[r_fb8])
            P.op("dve", I("memset", ones8, 1.0), reads=r_O, writes=[r_ones8])

            def Pf(eng, emit, reads=(), writes=(), dma_chan=None):
                return P.op(eng, emit, reads=list(reads) + r_O, writes=writes, dma_chan=dma_chan)

            prevC = None
            for ch in range(NG):
                b = misc_bank()
                for c in range(8):
                    Pf("pe", I("matmul", banks[b][0:8, :], lhsT=Wf[:, c, :], rhs=hT[:, c, ch * 512:(ch + 1) * 512],
                                                                   start=(c == 0), stop=(c == 7)),
                         reads=[r_Wf] + r_hT[ch * 4:ch * 4 + 4], writes=[r_bank[b]])
                for hf in range(2):
                    i = ch * 2 + hf
                    s = i % 2
                    Pf("act", I("activation", out=fe, in_=banks[b][0:8, hf * 256:(hf + 1) * 256], func=AF.Exp,
                                                                   scale=-1.0, bias=fb8[:, 1:2]),
                         reads=[r_bank[b], r_fb8], writes=[r_fe])
                    Pf("act", I("activation", out=fsp, in_=fe, func=AF.Ln, bias=1.0, scale=1.0), reads=[r_fe], writes=[r_fsp])
                    init = 0.0 if prevC is None else prevC[:, 255:256]
                    Pf("dve", I("tensor_tensor_scan", out=fC[s], data0=ones8, data1=fsp, initial=init,
                                                                              op0=ALU.mult, op1=ALU.add),
                         reads=[r_ones8, r_fsp, r_fC[1 - s]], writes=[r_fC[s]])
                    prevC = fC[s]
                    a6 = aug6[s]
                    Pf("dve", I("tensor_copy", out=a6[:, 0, :], in_=fC[s]), reads=[r_fC[s]], writes=[r_aug6[s]])
                    Pf("dve", I("tensor_tensor", out=fr, in0=fC[s], in1=a6[:, 0, :], op=ALU.subtract),
                         reads=[r_fC[s], r_aug6[s]], writes=[r_fr])
                    Pf("dve", I("tensor_copy", out=a6[:, 1, :], in_=fr), reads=[r_fr], writes=[r_aug6[s]])
                    Pf("dve", I("tensor_tensor", out=fr, in0=fr, in1=a6[:, 1, :], op=ALU.subtract),
                         reads=[r_fr, r_aug6[s]], writes=[r_fr])
                    Pf("dve", I("tensor_copy", out=a6[:, 2, :], in_=fr), reads=[r_fr], writes=[r_aug6[s]])
                    Pf("dve", I("tensor_scalar", out=a6[:, 3:6, :], in0=a6[:, 0:3, :], scalar1=-1.0, scalar2=None, op0=ALU.mult),
                         reads=[r_aug6[s]], writes=[r_aug6[s]])
                    Pf("sp", I("dma_start", out=caug_d[:, :, i * 256:(i + 1) * 256], in_=a6),
                         reads=[r_aug6[s]], writes=[r_caug[i]], dma_chan=r_aug6[s])

            maps = []
            for a in range(4):
                for c in range(2):
                    maps.append(dict(br=0, unit=a, sub=c, qoff=OFF["diff_q"] + a * 128 + c * 64, koff=OFF["diff_k"] + a * 128 + c * 64,
                                     kind="alibi", slope=2 * (a + 1) - 1, Kc=68, band=None, E=128, vh=a, full=True))
            for h in range(8):
                maps.append(dict(br=1, unit=4 + h // 2, sub=h % 2, qoff=OFF["fox_q"] + h * 64, koff=OFF["fox_k"] + h * 64,
                                 kind="fox", head=h, Kc=70, band=None, E=64, vh=h, full=True))
            for h in range(8):
                maps.append(dict(br=2, unit=8 + h // 2, sub=h % 2, qoff=OFF["dil_q"] + h * 64, koff=OFF["dil_k"] + h * 64,
                                 kind="alibi", slope=h, Kc=68, band=17, E=64, vh=h, full=False))
            voff = [OFF["diff_v"], OFF["fox_v"], OFF["dil_v"]]
            zoff = [OFF["diff_z"], OFF["fox_z"], OFF["dil_z"]]

            def Vview(br):
                if br == 0:
                    return Vaug[:, 0:NT * 516].rearrange("p (t h e) -> p t h e", t=NT, h=4)
                return Vaug.rearrange("p (t h e) -> p t h e", t=NT, h=8)

            def load_wqk(ui):
                mA, mB = maps[2 * ui], maps[2 * ui + 1]
                sA, sB = (2 * ui) % 3, (2 * ui + 1) % 3
                w = ui % 2
                P.op("pool", I("dma_start", out=Wq[w], in_=w_in_d[l, :, mA["qoff"]:mA["qoff"] + 128].rearrange("(c p) n -> p c n", p=128)),
                     writes=[r_Wq[w]], dma_chan=r_Wq[w])
                P.op("pool", I("dma_start", out=Wk[w], in_=w_in_d[l, :, mA["koff"]:mA["koff"] + 128].rearrange("(c p) n -> p c n", p=128)),
                     writes=[r_Wk[w]], dma_chan=r_Wk[w])
                for (m, s, zlo, a0) in ((mA, sA, 64, 64), (mB, sB, 0, 0)):
                    P.op("pool", I("memset", QT[s][zlo:zlo + 64, :], 0.0), writes=[r_QT[s]])
                    P.op("pool", I("memset", KT[s][zlo:zlo + 64, :], 0.0), writes=[r_KT[s]])
                    if m["kind"] == "alibi":
                        P.op("sp", I("dma_start", out=QT[s][a0:a0 + 4, :], in_=qaug_d[:, :]), writes=[r_QT[s]], dma_chan=r_QT[s])
                        P.op("sp", I("dma_start", out=KT[s][a0:a0 + 4, :], in_=kaug_d[m["slope"], :, :]), writes=[r_KT[s]], dma_chan=r_KT[s])
                    else:
                        h = m["head"]
                        P.op("sp", I("dma_start", out=QT[s][a0:a0 + 3, :], in_=ones3_d[:, :]), writes=[r_QT[s]], dma_chan=r_QT[s])
                        P.op("sp", I("dma_start", out=QT[s][a0 + 3:a0 + 6, :], in_=caug_d[h, 3:6, :]), reads=r_caug, writes=[r_QT[s]], dma_chan=r_QT[s])
                        P.op("sp", I("dma_start", out=KT[s][a0:a0 + 3, :], in_=caug_d[h, 0:3, :]), reads=r_caug, writes=[r_KT[s]], dma_chan=r_KT[s])
                        P.op("sp", I("dma_start", out=KT[s][a0 + 3:a0 + 6, :], in_=ones3_d[:, :]), writes=[r_KT[s]], dma_chan=r_KT[s])

            def proj_chunk(ui, ch):
                sA, sB = (2 * ui) % 3, (2 * ui + 1) % 3
                w = ui % 2
                hts = r_hT[ch * 4:ch * 4 + 4]
                cs = slice(ch * 512, (ch + 1) * 512)
                b = misc_bank()
                for c in range(8):
                    P.op("pe", I("matmul", banks[b][:, :], lhsT=Wq[w][:, c, :], rhs=hT[:, c, cs], start=(c == 0), stop=(c == 7)),
                         reads=[r_Wq[w]] + hts, writes=[r_bank[b]])
                P.op("dve", I("tensor_scalar", out=QT[sA][0:64, cs], in0=banks[b][0:64, :], scalar1=0.125, scalar2=None, op0=ALU.mult),
                     reads=[r_bank[b]], writes=[r_QT[sA]])
                P.op("dve", I("tensor_scalar", out=QT[sB][64:128, cs], in0=banks[b][64:128, :], scalar1=0.125, scalar2=None, op0=ALU.mult),
                     reads=[r_bank[b]], writes=[r_QT[sB]])
                b2 = misc_bank()
                for c in range(8):
                    P.op("pe", I("matmul", banks[b2][:, :], lhsT=Wk[w][:, c, :], rhs=hT[:, c, cs], start=(c == 0), stop=(c == 7)),
                         reads=[r_Wk[w]] + hts, writes=[r_bank[b2]])
                P.op("dve", I("tensor_copy", out=KT[sA][0:64, cs], in_=banks[b2][0:64, :]), reads=[r_bank[b2]], writes=[r_KT[sA]])
                P.op("dve", I("tensor_copy", out=KT[sB][64:128, cs], in_=banks[b2][64:128, :]), reads=[r_bank[b2]], writes=[r_KT[sB]])

            def load_wz(u):
                s = u % 2
                br, j = u // 4, u % 4
                o = zoff[br] + j * 128
                P.op("pool", I("dma_start", out=Wz[s], in_=w_in_d[l, :, o:o + 128].rearrange("(c p) n -> p c n", p=128)),
                     writes=[r_Wz[s]], dma_chan=r_Wz[s])

            def z_chunk(u, ch):
                s = u % 2
                b = misc_bank()
                for c in range(8):
                    P.op("pe", I("matmul", banks[b][:, :], lhsT=Wz[s][:, c, :], rhs=hT[:, c, ch * 512:(ch + 1) * 512],
                                                       start=(c == 0), stop=(c == 7)),
                         reads=[r_Wz[s]] + r_hT[ch * 4:ch * 4 + 4], writes=[r_bank[b]])
                zs = 0
                P.op("act", I("activation", out=ze[zs], in_=banks[b][:, :], func=AF.Exp, scale=-1.0), reads=[r_bank[b]], writes=[r_ze[zs]])
                P.op("dve", I("tensor_scalar", out=ze[zs], in0=ze[zs], scalar1=1.0, scalar2=None, op0=ALU.add), reads=[r_ze[zs]], writes=[r_ze[zs]])
                P.op("dve", I("reciprocal", out=ze[zs], in_=ze[zs]), reads=[r_ze[zs]], writes=[r_ze[zs]])
                P.op("dve", I("tensor_tensor", out=siluT[:, ch * 512:(ch + 1) * 512], in0=banks[b][:, :], in1=ze[zs], op=ALU.mult),
                     reads=[r_bank[b], r_ze[zs]], writes=[r_silu[ch]])

            def load_wv(br):
                o = voff[br]
                for hh in range(2):
                    P.op("pool", I("dma_start", out=Wv[:, :, hh * 256:(hh + 1) * 256],
                                                              in_=w_in_d[l, :, o + hh * 256:o + (hh + 1) * 256].rearrange("(c p) n -> p c n", p=128)),
                         writes=[r_Wv], dma_chan=r_Wv)

            def branch_setup(br):
                Vv = Vview(br)
                H, E = (4, 128) if br == 0 else (8, 64)
                P.op("pool", I("memset", Vv[:, :, :, E:E + 1], 1.0), writes=[r_V])
                for t in range(NT):
                    b = misc_bank()
                    for c in range(8):
                        P.op("pe", I("matmul", banks[b][:, :], lhsT=hT[:, c, t * 128:(t + 1) * 128], rhs=Wv[:, c, :],
                                                                     start=(c == 0), stop=(c == 7)),
                             reads=[r_Wv, r_hT[t]], writes=[r_bank[b]])
                    src = banks[b][:, :].rearrange("p (h e) -> p h e", h=H)
                    if t % 2 == 0:
                        P.op("act", I("activation", out=Vv[:, t, :, 0:E], in_=src, func=AF.Copy),
                             reads=[r_bank[b]], writes=[r_V])
                    else:
                        P.op("dve", I("tensor_copy", out=Vv[:, t, :, 0:E], in_=src), reads=[r_bank[b]], writes=[r_V])

            ACCB = (2, 3, 4, 5)

            def attention(mi, filler):
                m = maps[mi]
                s = mi % 3
                E = m["E"]
                Kc = m["Kc"]
                Vv = Vview(m["br"])
                nbk = 2 if E == 128 else 1
                steps = []

                def make_evac(g, b0, accv, bkof):
                    def evac():
                        rs = nxt("rden", 2)
                        rd = rden[rs]
                        if E == 64:
                            denv = banks[b0][:, 0:260].rearrange("p (j e) -> p j e", j=4)[:, :, 64]
                            P.op("dve", I("reciprocal", out=rd, in_=denv), reads=[r_bank[b0]], writes=[r_rden[rs]])
                            for j in range(4):
                                P.op("dve", I("tensor_scalar", out=O[:, 4 * g + j, m["sub"] * 64:(m["sub"] + 1) * 64], in0=accv[j][:, 0:64],
                                              scalar1=rd[:, j:j + 1], scalar2=None, op0=ALU.mult),
                                     reads=[r_bank[b0], r_rden[rs]], writes=[r_O[g]])
                        else:
                            for j in range(4):
                                P.op("dve", I("reciprocal", out=rd[:, j:j + 1], in_=accv[j][:, 128:129]),
                                     reads=[r_bank[bkof[j]]], writes=[r_rden[rs]])
                            if m["sub"] == 0:
                                for j in range(4):
                                    P.op("dve", I("tensor_scalar", out=O[:, 4 * g + j, :], in0=accv[j][:, 0:128], scalar1=rd[:, j:j + 1],
                                                  scalar2=None, op0=ALU.mult),
                                         reads=[r_bank[bkof[j]], r_rden[rs]], writes=[r_O[g]])
                            else:
                                P.op("dve", I("tensor_scalar", out=rd, in0=rd, scalar1=neglam, scalar2=None, op0=ALU.mult),
                                     reads=[r_rden[rs], r_lam], writes=[r_rden[rs]])
                                for j in range(4):
                                    P.op("dve", I("scalar_tensor_tensor", out=O[:, 4 * g + j, :], in0=accv[j][:, 0:128], scalar=rd[:, j:j + 1],
                                                  in1=O[:, 4 * g + j, :], op0=ALU.mult, op1=ALU.add),
                                         reads=[r_bank[bkof[j]], r_rden[rs], r_O[g]], writes=[r_O[g]])
                                    P.op("dve", I("scalar_tensor_tensor", out=junkf, in0=O[:, 4 * g + j, :], scalar=1.0, in1=O[:, 4 * g + j, :],
                                                  op0=ALU.mult, op1=ALU.mult, accum_out=ssq[:, j:j + 1]),
                                         reads=[r_O[g]], writes=[r_junkf, r_rs])
                                P.op("act", I("activation", out=lnv, in_=ssq, func=AF.Ln, scale=1.0 / 128, bias=EPS), reads=[r_rs], writes=[r_rs])
                                P.op("act", I("activation", out=rstd, in_=lnv, func=AF.Exp, scale=-0.5), reads=[r_rs], writes=[r_rs])
                                for j in range(4):
                                    P.op("dve", I("tensor_scalar", out=O[:, 4 * g + j, :], in0=O[:, 4 * g + j, :], scalar1=rstd[:, j:j + 1],
                                                  scalar2=None, op0=ALU.mult),
                                         reads=[r_O[g], r_rs], writes=[r_O[g]])
                    return evac

                STB = (0, 1) if nbk == 2 else (0, 1, 4, 5)
                LOOK = 1 if nbk == 2 else 2
                for g in range(NG):
                    if nbk == 2:
                        b0 = ACCB[2 * nxt("acc2", 2)]
                        ring["acc1"] = 0
                        bks = [b0, b0 + 1]
                        accv = [banks[bks[j // 2]][:, (j % 2) * 129:(j % 2) * 129 + 129] for j in range(4)]
                        bkof = [bks[j // 2] for j in range(4)]
                    else:
                        b0 = ACCB[nxt("acc1", 2)]
                        ring["acc2"] = 0
                        bks = [b0]
                        accv = [banks[b0][:, j * 65:j * 65 + 65] for j in range(4)]
                        bkof = [b0] * 4
                    first = {b: True for b in bks}
                    kb_lo = 0 if m["full"] else max(0, 4 * g - 16)
                    for kb in range(kb_lo, 4 * g + 4):
                        jlo = max(0, kb - 4 * g)
                        jhi = 3 if m["full"] else min(3, kb + 16 - 4 * g)
                        pvs = []
                        for j in range(jlo, jhi + 1):
                            bk = bkof[j]
                            pvs.append((j, bk, first[bk], 4 * g + j, accv[j]))
                            first[bk] = False
                        steps.append(dict(g=g, kb=kb, jlo=jlo, N=(jhi - jlo + 1) * 128, q0=(4 * g + jlo) * 128, sb=STB[nxt("st", len(STB))], p=nxt("pt", 3),
                                          pvs=pvs, last=(kb == 4 * g + 3), evac=make_evac(g, b0, accv, bkof)))

                def do_qk(t):
                    kb, N, q0, sb_ = t["kb"], t["N"], t["q0"], t["sb"]
                    P.op("pe", I("matmul", banks[sb_][:, 0:N], lhsT=KT[s][:, kb * 128:(kb + 1) * 128],
                                 rhs=QT[s][:, q0:q0 + N], start=True, stop=True),
                         reads=[r_KT[s], r_QT[s]], writes=[r_bank[sb_]])

                def do_exp(t):
                    kb, N, sb_, p, g, jlo = t["kb"], t["N"], t["sb"], t["p"], t["g"], t["jlo"]
                    P.op("act", I("activation", out=PT[p][:, 0:N], in_=banks[sb_][:, 0:N], func=AF.Exp),
                         reads=[r_bank[sb_]], writes=[r_PT[p]])
                    if m["full"]:
                        if kb >= 4 * g:
                            P.op("dve", I("tensor_tensor", out=PT[p][:, 0:128], in0=PT[p][:, 0:128], in1=cmask, op=ALU.mult),
                                 reads=[r_PT[p], r_cm], writes=[r_PT[p]])
                    else:
                        dlo = 4 * g + jlo - kb
                        P.op("dve", I("tensor_tensor", out=PT[p][:, 0:N], in0=PT[p][:, 0:N],
                                      in1=dmask[:, dlo * 128:dlo * 128 + N], op=ALU.mult),
                             reads=[r_PT[p], r_dm], writes=[r_PT[p]])

                def do_pv(t):
                    kb, p, jlo = t["kb"], t["p"], t["jlo"]
                    for (j, bk, stf, qb, av) in t["pvs"]:
                        P.op("pe", I("matmul", av, lhsT=PT[p][:, (j - jlo) * 128:(j - jlo + 1) * 128], rhs=Vv[:, kb, m["vh"], :],
                                     start=stf, stop=(kb == qb), skip_group_check=True),
                             reads=[r_PT[p], r_V], writes=[r_bank[bk]])

                n = len(steps)
                for i in range(min(LOOK, n)):
                    do_qk(steps[i])
                for i, t in enumerate(steps):
                    if i + LOOK < n:
                        do_qk(steps[i + LOOK])
                    do_exp(t)
                    do_pv(t)
                    if t["last"]:
                        t["evac"]()
                        filler(t["g"])

            def finalize_unit(u):
                br = u // 4
                for g in range(NG):
                    ob = nxt("obf", 2)
                    P.op("pool", I("tensor_copy", out=Obf[ob], in_=O[:, 4 * g:4 * g + 4, :]), reads=[r_O[g]], writes=[r_Obf[ob]])
                    b = misc_bank()
                    tpv = banks[b][:, :].bitcast(BF16)
                    for j in range(4):
                        P.op("pe", I("transpose", out=tpv[:, j * 128:(j + 1) * 128], in_=Obf[ob][:, j, :], identity=identb[:, :]),
                             reads=[r_Obf[ob], R("identb")], writes=[r_bank[b]])
                    ys = nxt("ys", 2)
                    if br == 0:
                        a = u % 4
                        P.op("dve", I("scalar_tensor_tensor", out=ystage[ys], in0=tpv[:, 0:512], scalar=gnp[:, a:a + 1],
                                      in1=siluT[:, g * 512:(g + 1) * 512], op0=ALU.mult, op1=ALU.mult),
                             reads=[r_bank[b], r_gnp, r_silu[g]], writes=[r_ys[ys]])
                    else:
                        P.op("dve", I("tensor_tensor", out=ystage[ys], in0=tpv[:, 0:512], in1=siluT[:, g * 512:(g + 1) * 512], op=ALU.mult),
                             reads=[r_bank[b], r_silu[g]], writes=[r_ys[ys]])
                    P.op("sp", I("dma_start", out=yT_d[u, :, g * 512:(g + 1) * 512], in_=ystage[ys]),
                         reads=[r_ys[ys]], dma_chan=r_ys[ys])

            nm = len(maps)
            nu = nm // 2
            load_wqk(0)
            load_wv(0)
            for ch in range(NG):
                proj_chunk(0, ch)
            for mi, m in enumerate(maps):
                u = m["unit"]
                ui = mi // 2
                if m["sub"] == 0:
                    load_wz(u)
                elif ui + 1 < nu:
                    load_wqk(ui + 1)
                if mi % 8 == 0:
                    branch_setup(m["br"])
                    if m["br"] < 2:
                        load_wv(m["br"] + 1)

                def filler(g, m=m, u=u, ui=ui):
                    if m["sub"] == 1:
                        if ui + 1 < nu:
                            proj_chunk(ui + 1, g)
                        z_chunk(u, g)
                attention(mi, filler)
                if m["sub"] == 1:
                    finalize_unit(u)
            build.phaseB_bytes = A.off

        def phaseC(l):
            P.epoch += 1
            A.reset()
            last = (l == DEPTH - 1)
            Wb = A.alloc([128, 12, D], BF16); r_Wb = [R("Wb%d" % f) for f in range(8)]
            Wg = A.alloc([128, 8, 3072], BF16); r_Wg = [R("Wg%d" % f) for f in range(8)]
            Wo = A.alloc([128, 8, D], BF16); r_Wo = R("Wo")
            yc = A.alloc([128, 12, 512], BF16); r_yc = R("yc")
            mT = A.alloc([128, 8, 512], BF16); r_mT = [R("mT%d" % f) for f in range(8)]
            sg = [A.alloc([128, 512], F32) for _ in range(2)]; r_sg = [R("sg%d" % i) for i in range(2)]
            macc = A.alloc([128, 512], F32); r_macc = R("macc")
            tmp = A.alloc([128, 512], F32); r_tmp = R("tmp")
            xt = [A.alloc([128, D], F32) for _ in range(2)]; r_xt = [R("cxt%d" % i) for i in range(2)]
            hb = [A.alloc([128, D], BF16) for _ in range(2)]; r_hb = [R("chb%d" % i) for i in range(2)]
            junk = A.alloc([128, D], BF16); r_junk = R("cjunk")
            gbc = A.alloc([128, D], F32); r_gbc = R("cgbc")
            load_gbc(gbc, r_gbc, final_g_d[0:1, :] if last else norm_g_d[l + 1:l + 2, :])
            wbv = w_br_d[l].rearrange("(u p) f -> p u f", p=128)
            wgv = w_in_d[l, :, OFF["merge_g"]:OFF["merge_g"] + 3072].rearrange("(c p) (n q) -> p c n q", p=128, n=3)
            Wg4 = Wg.rearrange("p c (n q) -> p c n q", n=3)
            for f in range(8):
                P.op("pool", I("dma_start", out=Wb[:, :, f * 128:(f + 1) * 128], in_=wbv[:, :, f * 128:(f + 1) * 128]),
                     writes=[r_Wb[f]], dma_chan=r_Wb[f])
                for n in range(3):
                    P.op("pool", I("dma_start", out=Wg4[:, :, n, f * 128:(f + 1) * 128], in_=wgv[:, :, n, f * 128:(f + 1) * 128]),
                         writes=[r_Wg[f]], dma_chan=r_Wg[f])
            wov = w_out_d[l].rearrange("(f p) o -> p f o", p=128)
            for hh in range(2):
                P.op("pool", I("dma_start", out=Wo[:, :, hh * 512:(hh + 1) * 512], in_=wov[:, :, hh * 512:(hh + 1) * 512]),
                     writes=[r_Wo], dma_chan=r_Wo)
            xsrc = x_d if l == 0 else xs_d
            CB = (0, 1, 2, 3)
            OB = (4, 5)
            for tc in range(NG):
                P.op("sp", I("dma_start", out=yc, in_=yT_d[:, :, tc * 512:(tc + 1) * 512].rearrange("u p s -> p u s")),
                     writes=[r_yc], dma_chan=r_yc)
                hts = r_hT[tc * 4:tc * 4 + 4]
                for f in range(8):
                    for n in range(3):
                        bb = CB[nxt("cb", 4)]
                        for j in range(4):
                            P.op("pe", I("matmul", banks[bb][:, :], lhsT=Wb[:, n * 4 + j, f * 128:(f + 1) * 128], rhs=yc[:, n * 4 + j, :],
                                                                                start=(j == 0), stop=(j == 3)),
                                 reads=[r_Wb[f], r_yc], writes=[r_bank[bb]])
                        gb = CB[nxt("cb", 4)]
                        for c in range(8):
                            P.op("pe", I("matmul", banks[gb][:, :], lhsT=Wg[:, c, n * 1024 + f * 128:n * 1024 + (f + 1) * 128],
                                                                                       rhs=hT[:, c, tc * 512:(tc + 1) * 512], start=(c == 0), stop=(c == 7)),
                                 reads=[r_Wg[f]] + hts, writes=[r_bank[gb]])
                        sgi = nxt("sg", 2)
                        P.op("act", I("activation", out=sg[sgi], in_=banks[gb][:, :], func=AF.Sigmoid),
                             reads=[r_bank[gb]], writes=[r_sg[sgi]])
                        if n == 0:
                            P.op("dve", I("tensor_tensor", out=macc, in0=banks[bb][:, :], in1=sg[sgi], op=ALU.mult),
                                 reads=[r_bank[bb], r_sg[sgi]], writes=[r_macc])
                        else:
                            P.op("dve", I("tensor_tensor", out=tmp, in0=banks[bb][:, :], in1=sg[sgi], op=ALU.mult),
                                 reads=[r_bank[bb], r_sg[sgi]], writes=[r_tmp])
                            if n == 1:
                                P.op("dve", I("tensor_tensor", out=macc, in0=macc, in1=tmp, op=ALU.add), reads=[r_macc, r_tmp], writes=[r_macc])
                            else:
                                P.op("dve", I("tensor_tensor", out=mT[:, f, :], in0=macc, in1=tmp, op=ALU.add),
                                     reads=[r_macc, r_tmp], writes=[r_mT[f]])
                for tt in range(4):
                    t = tc * 4 + tt
                    s = nxt("cxt", 2)
                    P.op("sp", I("dma_start", out=xt[s], in_=xsrc[t * 128:(t + 1) * 128, :]), writes=[r_xt[s]], dma_chan=r_xt[s])
                    for hh in range(2):
                        ob = OB[nxt("ob", 2)]
                        for f in range(8):
                            P.op("pe", I("matmul", banks[ob][:, :], lhsT=mT[:, f, tt * 128:(tt + 1) * 128],
                                                                                    rhs=Wo[:, f, hh * 512:(hh + 1) * 512], start=(f == 0), stop=(f == 7)),
                                 reads=r_mT + [r_Wo], writes=[r_bank[ob]])
                        P.op("dve", I("tensor_tensor", out=xt[s][:, hh * 512:(hh + 1) * 512], in0=banks[ob][:, :],
                                                                                 in1=xt[s][:, hh * 512:(hh + 1) * 512], op=ALU.add),
                             reads=[r_bank[ob], r_xt[s]], writes=[r_xt[s]])
                    if not last:
                        P.op("sp", I("dma_start", out=xs_d[t * 128:(t + 1) * 128, :], in_=xt[s]), reads=[r_xt[s]], dma_chan=r_xt[s])
                        norm_tile(xt[s], r_xt[s], t, gbc, r_gbc, (hb, r_hb, junk, r_junk))
                    else:
                        fs = nxt("fss", 2)
                        ss = small[:, 8 + 4 * fs:12 + 4 * fs]
                        r_ss = R("sm_fss%d" % fs)
                        P.op("act", I("activation", out=junk, in_=xt[s], func=AF.Square, accum_out=ss[:, 0:1]),
                             reads=[r_xt[s]], writes=[r_junk, r_ss])
                        P.op("act", I("activation", out=ss[:, 1:2], in_=ss[:, 0:1], func=AF.Sqrt, scale=1.0 / D, bias=EPS),
                             reads=[r_ss], writes=[r_ss])
                        P.op("dve", I("reciprocal", out=ss[:, 2:3], in_=ss[:, 1:2]), reads=[r_ss], writes=[r_ss])
                        P.op("dve", I("scalar_tensor_tensor", out=xt[s], in0=xt[s], scalar=ss[:, 2:3], in1=gbc, op0=ALU.mult, op1=ALU.mult),
                             reads=[r_xt[s], r_ss, r_gbc], writes=[r_xt[s]])
                        P.op("sp", I("dma_start", out=out_d[t * 128:(t + 1) * 128, :], in_=xt[s]), reads=[r_xt[s]], dma_chan=r_xt[s])

        phaseA()
        for l in range(DEPTH):
            P.barrier()
            phaseB(l)
            P.barrier()
            phaseC(l)
        stats = P.emit(nc, st)
        stats["arena_peak"] = A.peak
        build.stats = stats
    return nc


def make_consts(S):
    bf = ml_dtypes.bfloat16
    t = np.arange(S)
    qaug = np.stack([(t // 64) * 64, t % 64, np.ones(S), np.ones(S)]).astype(np.float32)
    kaug = np.zeros((8, 4, S), np.float32)
    for j in range(8):
        sl = 2.0 ** -(j + 1)
        kaug[j, 0] = -sl
        kaug[j, 1] = -sl
        kaug[j, 2] = sl * ((t // 64) * 64)
        kaug[j, 3] = sl * (t % 64)
    k = np.arange(128)[:, None]
    q = np.arange(128)[None, :]
    cmask = (q >= k).astype(np.float32)
    dm = np.zeros((128, 17, 128), np.float32)
    for dlt in range(17):
        d = 128 * dlt + q - k
        mult = ((d >= 0) & (d <= 128)).astype(np.float32) + ((d >= 0) & (d % 4 == 0) & (d <= 512)) + ((d >= 0) & (d % 16 == 0) & (d <= 2048))
        dm[:, dlt, :] = mult
    return dict(ident_bf=np.eye(128).astype(bf), ident_f=np.eye(128).astype(np.float32), qaug=qaug.astype(bf), kaug=kaug.astype(bf),
                ones3=np.ones((3, S)).astype(bf), cmask=cmask.astype(bf), dmask=dm.reshape(128, 17 * 128).astype(bf))


def kernel(x, norm_g, w_in, fox_fb, diff_lam, diff_norm_g, w_branch, w_out, final_g):
    x = np.asarray(x, np.float32)
    B, S, _ = x.shape
    DEPTH = norm_g.shape[0]
    nc = build(S, DEPTH)
    shared = dict(norm_g=np.ascontiguousarray(norm_g, np.float32), w_in=np.ascontiguousarray(w_in, np.float32),
                  fox_fb=np.ascontiguousarray(fox_fb, np.float32),
                  diff_lam=np.ascontiguousarray(np.asarray(diff_lam, np.float32).reshape(DEPTH, 256)),
                  diff_norm_g=np.ascontiguousarray(diff_norm_g, np.float32),
                  w_branch=np.ascontiguousarray(np.asarray(w_branch, np.float32).reshape(DEPTH, 1536, D)),
                  w_out=np.ascontiguousarray(w_out, np.float32),
                  final_g=np.ascontiguousarray(np.asarray(final_g, np.float32).reshape(1, D)))
    shared.update(make_consts(S))
    in_maps = [dict(shared, x=np.ascontiguousarray(x[b])) for b in range(B)]
    res = run_bass_kernel_spmd(nc, in_maps, core_ids=list(range(B)))
    return np.stack([np.asarray(r["out"], np.float32) for r in res.results], axis=0)
```

```python
import math
import numpy as np
import ml_dtypes
from contextlib import ExitStack
import concourse.bass as bass
import concourse.mybir as mybir
from concourse.bass_utils import run_bass_kernel_spmd

F32 = mybir.dt.float32
BF16 = mybir.dt.bfloat16
AF = mybir.ActivationFunctionType
ALU = mybir.AluOpType

D = 1024
INW = 9224
OFF = dict(diff_q=0, diff_k=512, diff_v=1024, diff_z=1536, fox_q=2048, fox_k=2560, fox_v=3072,
           fox_f=3584, fox_z=3592, dil_q=4104, dil_k=4616, dil_v=5128, dil_z=5640, merge_g=6152)
EPS = 1e-6
ENGS = ("pe", "act", "dve", "pool", "sp")
SWDGE_DEPTH = 3
SAME_ENG_WINDOW = 4


def I(meth, *args, **kw):
    f = lambda e: getattr(e, meth)(*args, **kw)
    f.multi = (meth in ("matmul", "transpose")) or (kw.get("accum_out") is not None)
    return f


class Res:
    __slots__ = ("name", "last_w", "readers", "dma_readers", "dsem", "dcount")

    def __init__(self, name):
        self.name = name
        self.last_w = None
        self.readers = {}
        self.dma_readers = []
        self.dsem = None
        self.dcount = 0


class Op:
    __slots__ = ("eng", "emit", "deps", "signal", "sigsem", "sigval", "is_dma", "epoch", "seq")

    def __init__(self, eng, emit, is_dma, epoch):
        self.seq = 0
        self.eng = eng
        self.emit = emit
        self.deps = []
        self.signal = False
        self.sigsem = None
        self.sigval = 0
        self.is_dma = is_dma
        self.epoch = epoch


class Prog:
    def __init__(self):
        self.ops = {e: [] for e in ENGS}
        self.epoch = 0
        self.chans = []
        self.chan_last = {}
        self.last_op = {}
        self.pending_bar = {e: [] for e in ENGS}
        self.res = {}
        self.pool_dmas = []

    def R(self, name):
        r = self.res.get(name)
        if r is None:
            r = self.res[name] = Res(name)
        return r

    def op(self, eng, emit, reads=(), writes=(), dma_chan=None):
        is_dma = dma_chan is not None
        o = Op(eng, emit, is_dma, self.epoch)
        o.seq = len(self.ops[eng])
        deps = list(self.pending_bar[eng])
        self.pending_bar[eng] = []
        for r in reads:
            if r.last_w is not None:
                deps.append(r.last_w)
        for r in writes:
            if r.last_w is not None:
                deps.append(r.last_w)
            deps.extend(r.readers.values())
            deps.extend(r.dma_readers)
        seen = set()
        for d in deps:
            if id(d) in seen:
                continue
            seen.add(id(d))
            if (not d.is_dma) and d.eng == eng and (eng == "pe" or o.seq - d.seq > SAME_ENG_WINDOW):
                continue
            o.deps.append(d)
            d.signal = True
        if is_dma and eng == "pool":
            self.pool_dmas.append(o)
            if len(self.pool_dmas) > SWDGE_DEPTH:
                d = self.pool_dmas[-1 - SWDGE_DEPTH]
                if id(d) not in seen:
                    o.deps.append(d)
        for r in reads:
            if is_dma:
                r.dma_readers.append(o)
            else:
                r.readers[eng] = o
        for r in writes:
            r.last_w = o
            r.readers = {}
            r.dma_readers = []
        if is_dma:
            if dma_chan.dsem is None:
                dma_chan.dsem = "ch%d" % len(self.chans)
                self.chans.append(dma_chan)
            dma_chan.dcount += 16
            o.sigsem = dma_chan.dsem
            o.sigval = dma_chan.dcount
            o.signal = True
            self.chan_last[dma_chan.dsem] = o
        else:
            self.last_op[eng] = o
        self.ops[eng].append(o)
        return o

    def barrier(self):
        deps = list(self.last_op.values()) + list(self.chan_last.values())
        for d in deps:
            d.signal = True
        for e in ENGS:
            self.pending_bar[e] = list(deps)

    def emit(self, nc, stack):
        semkeys = set()
        for e in ENGS:
            cnt = {}
            for o in self.ops[e]:
                if o.is_dma:
                    semkeys.add(o.sigsem)
                elif o.signal:
                    k = "%s_%d" % (e, o.epoch)
                    cnt[k] = cnt.get(k, 0) + 1
                    o.sigsem = k
                    o.sigval = cnt[k]
                    semkeys.add(k)
        sems = {k: stack.enter_context(nc.semaphore(k)) for k in sorted(semkeys)}
        block = stack.enter_context(nc.Block())
        engmap = {"pe": block.tensor, "act": block.scalar, "dve": block.vector,
                  "pool": block.gpsimd, "sp": block.sync}
        stats = {}
        chans = self.chans
        for e in ENGS:
            def body(eng, ops=self.ops[e], e=e):
                waited = {}
                nw = 0
                for o in ops:
                    need = {}
                    for d in o.deps:
                        if waited.get(d.sigsem, 0) >= d.sigval:
                            continue
                        need[d.sigsem] = max(need.get(d.sigsem, 0), d.sigval)
                    need = list(need.items())
                    attach = None
                    if need and e != "pe" and not getattr(o.emit, "multi", True):
                        attach = need.pop()
                    for k, v in need:
                        eng.wait_ge(sems[k], v)
                        waited[k] = v
                        nw += 1
                    ins = o.emit(eng)
                    if attach is not None:
                        ins._wait_ge(sems[attach[0]], attach[1])
                        waited[attach[0]] = attach[1]
                    if o.signal:
                        ins.then_inc(sems[o.sigsem], 16 if o.is_dma else 1)
                if e == "sp":
                    for r in chans:
                        eng.wait_ge(sems[r.dsem], r.dcount)
                stats[e] = (len(ops), nw)
            engmap[e](body)
        stats["nsem"] = len(sems)
        return stats


class Arena:
    def __init__(self, ap_f32):
        self.base = ap_f32
        self.cap = ap_f32.shape[1] * 4
        self.off = 0
        self.peak = 0

    def reset(self):
        self.off = 0

    def overlay(self, off):
        old = self.off
        self.off = off
        return old

    def alloc(self, shape, dtype):
        esz = 2 if dtype == BF16 else 4
        n = 1
        for s in shape[1:]:
            n *= s
        nbytes = (n * esz + 31) // 32 * 32
        assert self.off + nbytes <= self.cap, "arena overflow %d + %d > %d" % (self.off, nbytes, self.cap)
        v = self.base[:, self.off // 4:(self.off + nbytes) // 4]
        if dtype == BF16:
            v = v.bitcast(BF16)
        v = v[0:shape[0], 0:n]
        if len(shape) == 3:
            v = v.rearrange("p (a b) -> p a b", a=shape[1])
        elif len(shape) == 4:
            v = v.rearrange("p (a b c) -> p a b c", a=shape[1], b=shape[2])
        self.off += nbytes
        self.peak = max(self.peak, self.off)
        return v


def build(S, DEPTH):
    NT = S // 128
    NG = S // 512
    nc = bass.Bass("TRN2", target_bir_lowering=False)
    dram = lambda name, shape, dt, kind: nc.dram_tensor(name, shape, dt, kind=kind).ap()
    x_d = dram("x", [S, D], F32, "ExternalInput")
    norm_g_d = dram("norm_g", [DEPTH, D], F32, "ExternalInput")
    w_in_d = dram("w_in", [DEPTH, D, INW], F32, "ExternalInput")
    fox_fb_d = dram("fox_fb", [DEPTH, 8], F32, "ExternalInput")
    diff_lam_d = dram("diff_lam", [DEPTH, 256], F32, "ExternalInput")
    diff_ng_d = dram("diff_norm_g", [DEPTH, 512], F32, "ExternalInput")
    w_br_d = dram("w_branch", [DEPTH, 1536, D], F32, "ExternalInput")
    w_out_d = dram("w_out", [DEPTH, D, D], F32, "ExternalInput")
    final_g_d = dram("final_g", [1, D], F32, "ExternalInput")
    identb_d = dram("ident_bf", [128, 128], BF16, "ExternalInput")
    identf_d = dram("ident_f", [128, 128], F32, "ExternalInput")
    qaug_d = dram("qaug", [4, S], BF16, "ExternalInput")
    kaug_d = dram("kaug", [8, 4, S], BF16, "ExternalInput")
    ones3_d = dram("ones3", [3, S], BF16, "ExternalInput")
    cmask_d = dram("cmask", [128, 128], BF16, "ExternalInput")
    dmask_d = dram("dmask", [128, 17 * 128], BF16, "ExternalInput")
    out_d = dram("out", [S, D], F32, "ExternalOutput")
    xs_d = dram("xs_scr", [S, D], F32, "Internal")
    yT_d = dram("yT_scr", [12, 128, S], BF16, "Internal")
    caug_d = dram("caug_scr", [8, 6, S], BF16, "Internal")

    P = Prog()
    R = P.R
    st = ExitStack()
    with st:
        hT = st.enter_context(nc.sbuf_tensor("hT", [128, 8, S], BF16))
        identb = st.enter_context(nc.sbuf_tensor("identb", [128, 128], BF16))
        identf = st.enter_context(nc.sbuf_tensor("identf", [128, 128], F32))
        small = st.enter_context(nc.sbuf_tensor("small", [128, 64], F32))
        PERS = 8 * S * 2 + 256 + 512 + 256
        arena_t = st.enter_context(nc.sbuf_tensor("arena", [128, (212400 - PERS) // 4 - 64], F32))
        A = Arena(arena_t[:, :])
        banks = [st.enter_context(nc.psum_tensor("bank%d" % i, [128, 512], F32)) for i in range(8)]
        r_bank = [R("bank%d" % i) for i in range(8)]
        r_hT = [R("hT%d" % t) for t in range(NT)]
        r_small = {}

        def sm(name, lo, hi):
            r_small[name] = R("sm_" + name)
            return small[:, lo:hi]

        ring = {}

        def nxt(name, n):
            v = ring.get(name, 0)
            ring[name] = v + 1
            return v % n

        MISC = (6, 7)

        def misc_bank():
            return MISC[nxt("misc", 2)]

        P.op("sp", I("dma_start", out=identb[:, :], in_=identb_d[:, :]), writes=[R("identb")], dma_chan=R("identb"))
        P.op("sp", I("dma_start", out=identf[:, :], in_=identf_d[:, :]), writes=[R("identf")], dma_chan=R("identf"))

        def norm_tile(src, r_src, t, gbc, r_gbc, bufs, defer=False):
            hb, r_hb, junk, r_junk = bufs
            s = nxt("hb", 2)
            ss = small[:, 0 + 4 * s:4 + 4 * s]
            r_ss = R("sm_ss%d" % s)
            P.op("act", I("activation", out=junk, in_=src, func=AF.Square, accum_out=ss[:, 0:1]),
                 reads=[r_src], writes=[r_junk, r_ss])
            P.op("act", I("activation", out=ss[:, 1:2], in_=ss[:, 0:1], func=AF.Sqrt, scale=1.0 / D, bias=EPS),
                 reads=[r_ss], writes=[r_ss])
            P.op("dve", I("reciprocal", out=ss[:, 2:3], in_=ss[:, 1:2]), reads=[r_ss], writes=[r_ss])
            P.op("dve", I("scalar_tensor_tensor", out=hb[s], in0=src, scalar=ss[:, 2:3], in1=gbc,
                                                         op0=ALU.mult, op1=ALU.mult),
                 reads=[r_src, r_ss, r_gbc], writes=[r_hb[s]])
            def post():
                b = misc_bank()
                tpv = banks[b][:, :].bitcast(BF16)
                for c in range(8):
                    P.op("pe", I("transpose", out=tpv[:, c * 128:(c + 1) * 128], in_=hb[s][:, c * 128:(c + 1) * 128],
                                 identity=identb[:, :]),
                         reads=[r_hb[s], R("identb")], writes=[r_bank[b]])
                P.op("dve", I("tensor_copy", out=hT[:, :, t * 128:(t + 1) * 128],
                              in_=tpv.rearrange("p (c k) -> p c k", c=8)),
                     reads=[r_bank[b]], writes=[r_hT[t]])
            if defer:
                return post
            post()
            return None

        def load_gbc(gbc, r_gbc, src_row):
            P.op("sp", I("dma_start", out=gbc, in_=src_row.to_broadcast([128, D])), writes=[r_gbc], dma_chan=r_gbc)

        def phaseA():
            P.epoch += 1
            A.reset()
            gbc = A.alloc([128, D], F32); r_gbc = R("gbc")
            xt = [A.alloc([128, D], F32) for _ in range(2)]; r_xt = [R("xt%d" % i) for i in range(2)]
            hb = [A.alloc([128, D], BF16) for _ in range(2)]; r_hb = [R("hb%d" % i) for i in range(2)]
            junk = A.alloc([128, D], BF16); r_junk = R("junk")
            load_gbc(gbc, r_gbc, norm_g_d[0:1, :])
            for t in range(NT):
                s = t % 2
                P.op("sp", I("dma_start", out=xt[s], in_=x_d[t * 128:(t + 1) * 128, :]),
                     writes=[r_xt[s]], dma_chan=r_xt[s])
                norm_tile(xt[s], r_xt[s], t, gbc, r_gbc, (hb, r_hb, junk, r_junk))

        def phaseB(l):
            P.epoch += 1
            A.reset()
            lam_init = 0.8 - 0.6 * math.exp(-0.3 * l)
            Vaug = A.alloc([128, NT * 520], BF16); r_V = R("Vaug")
            QT = [A.alloc([128, S], BF16) for _ in range(3)]; r_QT = [R("QT%d" % i) for i in range(3)]
            KT = [A.alloc([128, S], BF16) for _ in range(3)]; r_KT = [R("KT%d" % i) for i in range(3)]
            O_off = A.off
            O = A.alloc([128, NT, 128], F32); r_O = [R("O%d" % g) for g in range(NG)]
            siluT = A.alloc([128, S], BF16); r_silu = [R("silu%d" % g) for g in range(NG)]
            PT = [A.alloc([128, 512], BF16) for _ in range(3)]; r_PT = [R("PT%d" % i) for i in range(3)]
            Wv = A.alloc([128, 8, 512], BF16); r_Wv = R("Wv")
            Wz = [A.alloc([128, 8, 128], BF16) for _ in range(2)]; r_Wz = [R("Wz%d" % i) for i in range(2)]
            Wq = [A.alloc([128, 8, 128], BF16) for _ in range(2)]; r_Wq = [R("Wq%d" % i) for i in range(2)]
            Wk = [A.alloc([128, 8, 128], BF16) for _ in range(2)]; r_Wk = [R("Wk%d" % i) for i in range(2)]
            ystage = [A.alloc([128, 512], BF16) for _ in range(2)]; r_ys = [R("ys%d" % i) for i in range(2)]
            ze = [A.alloc([128, 512], F32) for _ in range(1)]; r_ze = [R("ze%d" % i) for i in range(1)]
            Obf = [A.alloc([128, 4, 128], BF16) for _ in range(2)]; r_Obf = [R("Obf%d" % i) for i in range(2)]
            cmask = A.alloc([128, 128], BF16); r_cm = R("cmask")
            dmask = A.alloc([128, 17 * 128], BF16); r_dm = R("dmask")
            junkf = A.alloc([128, 128], F32); r_junkf = R("junkf")
            dl = A.alloc([128, 256], F32); r_dl = R("dl")
            gn = A.alloc([128, 4], F32); r_gn = R("gn")
            Wf = A.alloc([128, 8, 8], BF16); r_Wf = R("Wf")
            use_ov = NT * 128 * 4 >= 9216
            _save = A.overlay(O_off) if use_ov else None
            fe = A.alloc([8, 256], F32); r_fe = R("fe")
            fsp = A.alloc([8, 256], F32); r_fsp = R("fsp")
            fC = [A.alloc([8, 256], F32) for _ in range(2)]; r_fC = [R("fC%d" % i) for i in range(2)]
            fr = A.alloc([8, 256], F32); r_fr = R("fr")
            ones8 = A.alloc([8, 256], F32); r_ones8 = R("ones8")
            aug6 = [A.alloc([8, 6, 256], BF16)] * 2; r_aug6 = [R("aug6_0")] * 2
            if use_ov:
                assert A.off <= O_off + NT * 128 * 4
                A.overlay(_save)
            fb8 = A.alloc([8, 2], F32); r_fb8 = R("fb8")
            r_caug = [R("caug%d" % i) for i in range(S // 256)]
            lsum = small[:, 16:18]; lexp = small[:, 18:20]; ltmp = small[:, 20:21]; neglam = small[:, 21:22]
            r_lam = R("sm_lam")
            gnp = small[:, 24:28]; r_gnp = R("sm_gnp")
            rden = [small[:, 32 + 4 * i:36 + 4 * i] for i in range(2)]; r_rden = [R("sm_rden%d" % i) for i in range(2)]
            ssq = small[:, 40:44]; lnv = small[:, 44:48]; rstd = small[:, 48:52]; r_rs = R("sm_rs")

            P.op("sp", I("dma_start", out=cmask, in_=cmask_d[:, :]), writes=[r_cm], dma_chan=r_cm)
            P.op("sp", I("dma_start", out=dmask, in_=dmask_d[:, :]), writes=[r_dm], dma_chan=r_dm)
            P.op("sp", I("dma_start", out=dl, in_=diff_lam_d[l:l + 1, :].to_broadcast([128, 256])), writes=[r_dl], dma_chan=r_dl)
            for a in range(4):
                P.op("sp", I("dma_start", out=gn[:, a:a + 1],
                                                      in_=diff_ng_d[l, a * 128:(a + 1) * 128].rearrange("(p o) -> p o", o=1)),
                     writes=[r_gn], dma_chan=r_gn)
            P.op("sp", I("dma_start", out=fb8[:, 0:1], in_=fox_fb_d[l, :].rearrange("(p o) -> p o", o=1)),
                 writes=[r_fb8], dma_chan=r_fb8)
            P.op("pool", I("dma_start", out=Wf, in_=w_in_d[l, :, OFF["fox_f"]:OFF["fox_f"] + 8].rearrange("(c p) n -> p c n", p=128)),
                 writes=[r_Wf], dma_chan=r_Wf)
            P.op("dve", I("scalar_tensor_tensor", out=junkf[:, 0:64], in0=dl[:, 0:64], scalar=1.0, in1=dl[:, 64:128],
                                                         op0=ALU.mult, op1=ALU.mult, accum_out=lsum[:, 0:1]),
                 reads=[r_dl], writes=[r_junkf, r_lam])
            P.op("dve", I("scalar_tensor_tensor", out=junkf[:, 0:64], in0=dl[:, 128:192], scalar=1.0, in1=dl[:, 192:256],
                                                         op0=ALU.mult, op1=ALU.mult, accum_out=lsum[:, 1:2]),
                 reads=[r_dl], writes=[r_junkf, r_lam])
            P.op("act", I("activation", out=lexp, in_=lsum, func=AF.Exp), reads=[r_lam], writes=[r_lam])
            P.op("dve", I("tensor_tensor", out=ltmp, in0=lexp[:, 0:1], in1=lexp[:, 1:2], op=ALU.subtract), reads=[r_lam], writes=[r_lam])
            P.op("dve", I("tensor_scalar", out=neglam, in0=ltmp, scalar1=-1.0, scalar2=-lam_init, op0=ALU.mult, op1=ALU.add),
                 reads=[r_lam], writes=[r_lam])
            P.op("dve", I("tensor_scalar", out=gnp, in0=gn, scalar1=1.0 - lam_init, scalar2=None, op0=ALU.mult),
                 reads=[r_gn], writes=[r_gnp])
            P.op("dve", I("tensor_scalar", out=fb8[:, 1:2], in0=fb8[:, 0:1], scalar1=-1.0, scalar2=None, op0=ALU.mult),
                 reads=[r_fb8], writes=[r_fb8])
            P.op("dve", I("memset", ones8, 1.0), reads=r_O, writes=[r_ones8])

            def Pf(eng, emit, reads=(), writes=(), dma_chan=None):
                return P.op(eng, emit, reads=list(reads) + r_O, writes=writes, dma_chan=dma_chan)

            prevC = None
            for ch in range(NG):
                b = misc_bank()
                for c in range(8):
                    Pf("pe", I("matmul", banks[b][0:8, :], lhsT=Wf[:, c, :], rhs=hT[:, c, ch * 512:(ch + 1) * 512],
                                                                   start=(c == 0), stop=(c == 7)),
                         reads=[r_Wf] + r_hT[ch * 4:ch * 4 + 4], writes=[r_bank[b]])
                for hf in range(2):
                    i = ch * 2 + hf
                    s = i % 2
                    Pf("act", I("activation", out=fe, in_=banks[b][0:8, hf * 256:(hf + 1) * 256], func=AF.Exp,
                                                                   scale=-1.0, bias=fb8[:, 1:2]),
                         reads=[r_bank[b], r_fb8], writes=[r_fe])
                    Pf("act", I("activation", out=fsp, in_=fe, func=AF.Ln, bias=1.0, scale=1.0), reads=[r_fe], writes=[r_fsp])
                    init = 0.0 if prevC is None else prevC[:, 255:256]
                    Pf("dve", I("tensor_tensor_scan", out=fC[s], data0=ones8, data1=fsp, initial=init,
                                                                              op0=ALU.mult, op1=ALU.add),
                         reads=[r_ones8, r_fsp, r_fC[1 - s]], writes=[r_fC[s]])
                    prevC = fC[s]
                    a6 = aug6[s]
                    Pf("dve", I("tensor_copy", out=a6[:, 0, :], in_=fC[s]), reads=[r_fC[s]], writes=[r_aug6[s]])
                    Pf("dve", I("tensor_tensor", out=fr, in0=fC[s], in1=a6[:, 0, :], op=ALU.subtract),
                         reads=[r_fC[s], r_aug6[s]], writes=[r_fr])
                    Pf("dve", I("tensor_copy", out=a6[:, 1, :], in_=fr), reads=[r_fr], writes=[r_aug6[s]])
                    Pf("dve", I("tensor_tensor", out=fr, in0=fr, in1=a6[:, 1, :], op=ALU.subtract),
                         reads=[r_fr, r_aug6[s]], writes=[r_fr])
                    Pf("dve", I("tensor_copy", out=a6[:, 2, :], in_=fr), reads=[r_fr], writes=[r_aug6[s]])
                    Pf("dve", I("tensor_scalar", out=a6[:, 3:6, :], in0=a6[:, 0:3, :], scalar1=-1.0, scalar2=None, op0=ALU.mult),
                         reads=[r_aug6[s]], writes=[r_aug6[s]])
                    Pf("sp", I("dma_start", out=caug_d[:, :, i * 256:(i + 1) * 256], in_=a6),
                         reads=[r_aug6[s]], writes=[r_caug[i]], dma_chan=r_aug6[s])

            maps = []
            for a in range(4):
                for c in range(2):
                    maps.append(dict(br=0, unit=a, sub=c, qoff=OFF["diff_q"] + a * 128 + c * 64, koff=OFF["diff_k"] + a * 128 + c * 64,
                                     kind="alibi", slope=2 * (a + 1) - 1, Kc=68, band=None, E=128, vh=a, full=True))
            for h in range(8):
                maps.append(dict(br=1, unit=4 + h // 2, sub=h % 2, qoff=OFF["fox_q"] + h * 64, koff=OFF["fox_k"] + h * 64,
                                 kind="fox", head=h, Kc=70, band=None, E=64, vh=h, full=True))
            for h in range(8):
                maps.append(dict(br=2, unit=8 + h // 2, sub=h % 2, qoff=OFF["dil_q"] + h * 64, koff=OFF["dil_k"] + h * 64,
                                 kind="alibi", slope=h, Kc=68, band=17, E=64, vh=h, full=False))
            voff = [OFF["diff_v"], OFF["fox_v"], OFF["dil_v"]]
            zoff = [OFF["diff_z"], OFF["fox_z"], OFF["dil_z"]]

            def Vview(br):
                if br == 0:
                    return Vaug[:, 0:NT * 516].rearrange("p (t h e) -> p t h e", t=NT, h=4)
                return Vaug.rearrange("p (t h e) -> p t h e", t=NT, h=8)

            def load_wqk(ui):
                mA, mB = maps[2 * ui], maps[2 * ui + 1]
                sA, sB = (2 * ui) % 3, (2 * ui + 1) % 3
                w = ui % 2
                P.op("pool", I("dma_start", out=Wq[w], in_=w_in_d[l, :, mA["qoff"]:mA["qoff"] + 128].rearrange("(c p) n -> p c n", p=128)),
                     writes=[r_Wq[w]], dma_chan=r_Wq[w])
                P.op("pool", I("dma_start", out=Wk[w], in_=w_in_d[l, :, mA["koff"]:mA["koff"] + 128].rearrange("(c p) n -> p c n", p=128)),
                     writes=[r_Wk[w]], dma_chan=r_Wk[w])
                for (m, s, zlo, a0) in ((mA, sA, 64, 64), (mB, sB, 0, 0)):
                    P.op("pool", I("memset", QT[s][zlo:zlo + 64, :], 0.0), writes=[r_QT[s]])
                    P.op("pool", I("memset", KT[s][zlo:zlo + 64, :], 0.0), writes=[r_KT[s]])
                    if m["kind"] == "alibi":
                        P.op("sp", I("dma_start", out=QT[s][a0:a0 + 4, :], in_=qaug_d[:, :]), writes=[r_QT[s]], dma_chan=r_QT[s])
                        P.op("sp", I("dma_start", out=KT[s][a0:a0 + 4, :], in_=kaug_d[m["slope"], :, :]), writes=[r_KT[s]], dma_chan=r_KT[s])
                    else:
                        h = m["head"]
                        P.op("sp", I("dma_start", out=QT[s][a0:a0 + 3, :], in_=ones3_d[:, :]), writes=[r_QT[s]], dma_chan=r_QT[s])
                        P.op("sp", I("dma_start", out=QT[s][a0 + 3:a0 + 6, :], in_=caug_d[h, 3:6, :]), reads=r_caug, writes=[r_QT[s]], dma_chan=r_QT[s])
                        P.op("sp", I("dma_start", out=KT[s][a0:a0 + 3, :], in_=caug_d[h, 0:3, :]), reads=r_caug, writes=[r_KT[s]], dma_chan=r_KT[s])
                        P.op("sp", I("dma_start", out=KT[s][a0 + 3:a0 + 6, :], in_=ones3_d[:, :]), writes=[r_KT[s]], dma_chan=r_KT[s])

            def proj_chunk(ui, ch):
                sA, sB = (2 * ui) % 3, (2 * ui + 1) % 3
                w = ui % 2
                hts = r_hT[ch * 4:ch * 4 + 4]
                cs = slice(ch * 512, (ch + 1) * 512)
                b = misc_bank()
                for c in range(8):
                    P.op("pe", I("matmul", banks[b][:, :], lhsT=Wq[w][:, c, :], rhs=hT[:, c, cs], start=(c == 0), stop=(c == 7)),
                         reads=[r_Wq[w]] + hts, writes=[r_bank[b]])
                P.op("dve", I("tensor_scalar", out=QT[sA][0:64, cs], in0=banks[b][0:64, :], scalar1=0.125, scalar2=None, op0=ALU.mult),
                     reads=[r_bank[b]], writes=[r_QT[sA]])
                P.op("dve", I("tensor_scalar", out=QT[sB][64:128, cs], in0=banks[b][64:128, :], scalar1=0.125, scalar2=None, op0=ALU.mult),
                     reads=[r_bank[b]], writes=[r_QT[sB]])
                b2 = misc_bank()
                for c in range(8):
                    P.op("pe", I("matmul", banks[b2][:, :], lhsT=Wk[w][:, c, :], rhs=hT[:, c, cs], start=(c == 0), stop=(c == 7)),
                         reads=[r_Wk[w]] + hts, writes=[r_bank[b2]])
                P.op("dve", I("tensor_copy", out=KT[sA][0:64, cs], in_=banks[b2][0:64, :]), reads=[r_bank[b2]], writes=[r_KT[sA]])
                P.op("dve", I("tensor_copy", out=KT[sB][64:128, cs], in_=banks[b2][64:128, :]), reads=[r_bank[b2]], writes=[r_KT[sB]])

            def load_wz(u):
                s = u % 2
                br, j = u // 4, u % 4
                o = zoff[br] + j * 128
                P.op("pool", I("dma_start", out=Wz[s], in_=w_in_d[l, :, o:o + 128].rearrange("(c p) n -> p c n", p=128)),
                     writes=[r_Wz[s]], dma_chan=r_Wz[s])

            def z_chunk(u, ch):
                s = u % 2
                b = misc_bank()
                for c in range(8):
                    P.op("pe", I("matmul", banks[b][:, :], lhsT=Wz[s][:, c, :], rhs=hT[:, c, ch * 512:(ch + 1) * 512],
                                                       start=(c == 0), stop=(c == 7)),
                         reads=[r_Wz[s]] + r_hT[ch * 4:ch * 4 + 4], writes=[r_bank[b]])
                zs = 0
                P.op("act", I("activation", out=ze[zs], in_=banks[b][:, :], func=AF.Exp, scale=-1.0), reads=[r_bank[b]], writes=[r_ze[zs]])
                P.op("dve", I("tensor_scalar", out=ze[zs], in0=ze[zs], scalar1=1.0, scalar2=None, op0=ALU.add), reads=[r_ze[zs]], writes=[r_ze[zs]])
                P.op("dve", I("reciprocal", out=ze[zs], in_=ze[zs]), reads=[r_ze[zs]], writes=[r_ze[zs]])
                P.op("dve", I("tensor_tensor", out=siluT[:, ch * 512:(ch + 1) * 512], in0=banks[b][:, :], in1=ze[zs], op=ALU.mult),
                     reads=[r_bank[b], r_ze[zs]], writes=[r_silu[ch]])

            def load_wv(br):
                o = voff[br]
                for hh in range(2):
                    P.op("pool", I("dma_start", out=Wv[:, :, hh * 256:(hh + 1) * 256],
                                                              in_=w_in_d[l, :, o + hh * 256:o + (hh + 1) * 256].rearrange("(c p) n -> p c n", p=128)),
                         writes=[r_Wv], dma_chan=r_Wv)

            def branch_setup(br):
                Vv = Vview(br)
                H, E = (4, 128) if br == 0 else (8, 64)
                P.op("pool", I("memset", Vv[:, :, :, E:E + 1], 1.0), writes=[r_V])
                for t in range(NT):
                    b = misc_bank()
                    for c in range(8):
                        P.op("pe", I("matmul", banks[b][:, :], lhsT=hT[:, c, t * 128:(t + 1) * 128], rhs=Wv[:, c, :],
                                                                     start=(c == 0), stop=(c == 7)),
                             reads=[r_Wv, r_hT[t]], writes=[r_bank[b]])
                    src = banks[b][:, :].rearrange("p (h e) -> p h e", h=H)
                    if t % 2 == 0:
                        P.op("act", I("activation", out=Vv[:, t, :, 0:E], in_=src, func=AF.Copy),
                             reads=[r_bank[b]], writes=[r_V])
                    else:
                        P.op("dve", I("tensor_copy", out=Vv[:, t, :, 0:E], in_=src), reads=[r_bank[b]], writes=[r_V])

            ACCB = (2, 3, 4, 5)

            def attention(mi, filler):
                m = maps[mi]
                s = mi % 3
                E = m["E"]
                Kc = m["Kc"]
                Vv = Vview(m["br"])
                nbk = 2 if E == 128 else 1
                steps = []

                def make_evac(g, b0, accv, bkof):
                    def evac():
                        rs = nxt("rden", 2)
                        rd = rden[rs]
                        if E == 64:
                            denv = banks[b0][:, 0:260].rearrange("p (j e) -> p j e", j=4)[:, :, 64]
                            P.op("dve", I("reciprocal", out=rd, in_=denv), reads=[r_bank[b0]], writes=[r_rden[rs]])
                            for j in range(4):
                                P.op("dve", I("tensor_scalar", out=O[:, 4 * g + j, m["sub"] * 64:(m["sub"] + 1) * 64], in0=accv[j][:, 0:64],
                                              scalar1=rd[:, j:j + 1], scalar2=None, op0=ALU.mult),
                                     reads=[r_bank[b0], r_rden[rs]], writes=[r_O[g]])
                        else:
                            for j in range(4):
                                P.op("dve", I("reciprocal", out=rd[:, j:j + 1], in_=accv[j][:, 128:129]),
                                     reads=[r_bank[bkof[j]]], writes=[r_rden[rs]])
                            if m["sub"] == 0:
                                for j in range(4):
                                    P.op("dve", I("tensor_scalar", out=O[:, 4 * g + j, :], in0=accv[j][:, 0:128], scalar1=rd[:, j:j + 1],
                                                  scalar2=None, op0=ALU.mult),
                                         reads=[r_bank[bkof[j]], r_rden[rs]], writes=[r_O[g]])
                            else:
                                P.op("dve", I("tensor_scalar", out=rd, in0=rd, scalar1=neglam, scalar2=None, op0=ALU.mult),
                                     reads=[r_rden[rs], r_lam], writes=[r_rden[rs]])
                                for j in range(4):
                                    P.op("dve", I("scalar_tensor_tensor", out=O[:, 4 * g + j, :], in0=accv[j][:, 0:128], scalar=rd[:, j:j + 1],
                                                  in1=O[:, 4 * g + j, :], op0=ALU.mult, op1=ALU.add),
                                         reads=[r_bank[bkof[j]], r_rden[rs], r_O[g]], writes=[r_O[g]])
                                    P.op("dve", I("scalar_tensor_tensor", out=junkf, in0=O[:, 4 * g + j, :], scalar=1.0, in1=O[:, 4 * g + j, :],
                                                  op0=ALU.mult, op1=ALU.mult, accum_out=ssq[:, j:j + 1]),
                                         reads=[r_O[g]], writes=[r_junkf, r_rs])
                                P.op("act", I("activation", out=lnv, in_=ssq, func=AF.Ln, scale=1.0 / 128, bias=EPS), reads=[r_rs], writes=[r_rs])
                                P.op("act", I("activation", out=rstd, in_=lnv, func=AF.Exp, scale=-0.5), reads=[r_rs], writes=[r_rs])
                                for j in range(4):
                                    P.op("dve", I("tensor_scalar", out=O[:, 4 * g + j, :], in0=O[:, 4 * g + j, :], scalar1=rstd[:, j:j + 1],
                                                  scalar2=None, op0=ALU.mult),
                                         reads=[r_O[g], r_rs], writes=[r_O[g]])
                    return evac

                STB = (0, 1) if nbk == 2 else (0, 1, 4, 5)
                LOOK = 1 if nbk == 2 else 2
                for g in range(NG):
                    if nbk == 2:
                        b0 = ACCB[2 * nxt("acc2", 2)]
                        ring["acc1"] = 0
                        bks = [b0, b0 + 1]
                        accv = [banks[bks[j // 2]][:, (j % 2) * 129:(j % 2) * 129 + 129] for j in range(4)]
                        bkof = [bks[j // 2] for j in range(4)]
                    else:
                        b0 = ACCB[nxt("acc1", 2)]
                        ring["acc2"] = 0
                        bks = [b0]
                        accv = [banks[b0][:, j * 65:j * 65 + 65] for j in range(4)]
                        bkof = [b0] * 4
                    first = {b: True for b in bks}
                    kb_lo = 0 if m["full"] else max(0, 4 * g - 16)
                    for kb in range(kb_lo, 4 * g + 4):
                        jlo = max(0, kb - 4 * g)
                        jhi = 3 if m["full"] else min(3, kb + 16 - 4 * g)
                        pvs = []
                        for j in range(jlo, jhi + 1):
                            bk = bkof[j]
                            pvs.append((j, bk, first[bk], 4 * g + j, accv[j]))
                            first[bk] = False
                        steps.append(dict(g=g, kb=kb, jlo=jlo, N=(jhi - jlo + 1) * 128, q0=(4 * g + jlo) * 128, sb=STB[nxt("st", len(STB))], p=nxt("pt", 3),
                                          pvs=pvs, last=(kb == 4 * g + 3), evac=make_evac(g, b0, accv, bkof)))

                def do_qk(t):
                    kb, N, q0, sb_ = t["kb"], t["N"], t["q0"], t["sb"]
                    P.op("pe", I("matmul", banks[sb_][:, 0:N], lhsT=KT[s][:, kb * 128:(kb + 1) * 128],
                                 rhs=QT[s][:, q0:q0 + N], start=True, stop=True),
                         reads=[r_KT[s], r_QT[s]], writes=[r_bank[sb_]])

                def do_exp(t):
                    kb, N, sb_, p, g, jlo = t["kb"], t["N"], t["sb"], t["p"], t["g"], t["jlo"]
                    P.op("act", I("activation", out=PT[p][:, 0:N], in_=banks[sb_][:, 0:N], func=AF.Exp),
                         reads=[r_bank[sb_]], writes=[r_PT[p]])
                    if m["full"]:
                        if kb >= 4 * g:
                            P.op("dve", I("tensor_tensor", out=PT[p][:, 0:128], in0=PT[p][:, 0:128], in1=cmask, op=ALU.mult),
                                 reads=[r_PT[p], r_cm], writes=[r_PT[p]])
                    else:
                        dlo = 4 * g + jlo - kb
                        P.op("dve", I("tensor_tensor", out=PT[p][:, 0:N], in0=PT[p][:, 0:N],
                                      in1=dmask[:, dlo * 128:dlo * 128 + N], op=ALU.mult),
                             reads=[r_PT[p], r_dm], writes=[r_PT[p]])

                def do_pv(t):
                    kb, p, jlo = t["kb"], t["p"], t["jlo"]
                    for (j, bk, stf, qb, av) in t["pvs"]:
                        P.op("pe", I("matmul", av, lhsT=PT[p][:, (j - jlo) * 128:(j - jlo + 1) * 128], rhs=Vv[:, kb, m["vh"], :],
                                     start=stf, stop=(kb == qb), skip_group_check=True),
                             reads=[r_PT[p], r_V], writes=[r_bank[bk]])

                n = len(steps)
                for i in range(min(LOOK, n)):
                    do_qk(steps[i])
                for i, t in enumerate(steps):
                    if i + LOOK < n:
                        do_qk(steps[i + LOOK])
                    do_exp(t)
                    do_pv(t)
                    if t["last"]:
                        t["evac"]()
                        filler(t["g"])

            def finalize_unit(u):
                br = u // 4
                for g in range(NG):
                    ob = nxt("obf", 2)
                    P.op("pool", I("tensor_copy", out=Obf[ob], in_=O[:, 4 * g:4 * g + 4, :]), reads=[r_O[g]], writes=[r_Obf[ob]])
                    b = misc_bank()
                    tpv = banks[b][:, :].bitcast(BF16)
                    for j in range(4):
                        P.op("pe", I("transpose", out=tpv[:, j * 128:(j + 1) * 128], in_=Obf[ob][:, j, :], identity=identb[:, :]),
                             reads=[r_Obf[ob], R("identb")], writes=[r_bank[b]])
                    ys = nxt("ys", 2)
                    if br == 0:
                        a = u % 4
                        P.op("dve", I("scalar_tensor_tensor", out=ystage[ys], in0=tpv[:, 0:512], scalar=gnp[:, a:a + 1],
                                      in1=siluT[:, g * 512:(g + 1) * 512], op0=ALU.mult, op1=ALU.mult),
                             reads=[r_bank[b], r_gnp, r_silu[g]], writes=[r_ys[ys]])
                    else:
                        P.op("dve", I("tensor_tensor", out=ystage[ys], in0=tpv[:, 0:512], in1=siluT[:, g * 512:(g + 1) * 512], op=ALU.mult),
                             reads=[r_bank[b], r_silu[g]], writes=[r_ys[ys]])
                    P.op("sp", I("dma_start", out=yT_d[u, :, g * 512:(g + 1) * 512], in_=ystage[ys]),
                         reads=[r_ys[ys]], dma_chan=r_ys[ys])

            nm = len(maps)
            nu = nm // 2
            load_wqk(0)
            load_wv(0)
            for ch in range(NG):
                proj_chunk(0, ch)
            for mi, m in enumerate(maps):
                u = m["unit"]
                ui = mi // 2
                if m["sub"] == 0:
                    load_wz(u)
                elif ui + 1 < nu:
                    load_wqk(ui + 1)
                if mi % 8 == 0:
                    branch_setup(m["br"])
                    if m["br"] < 2:
                        load_wv(m["br"] + 1)

                def filler(g, m=m, u=u, ui=ui):
                    if m["sub"] == 1:
                        if ui + 1 < nu:
                            proj_chunk(ui + 1, g)
                        z_chunk(u, g)
                attention(mi, filler)
                if m["sub"] == 1:
                    finalize_unit(u)
            build.phaseB_bytes = A.off

        def phaseC(l):
            P.epoch += 1
            A.reset()
            last = (l == DEPTH - 1)
            Wb = A.alloc([128, 12, D], BF16); r_Wb = [R("Wb%d" % f) for f in range(8)]
            Wg = A.alloc([128, 8, 3072], BF16); r_Wg = [R("Wg%d" % f) for f in range(8)]
            Wo = A.alloc([128, 8, D], BF16); r_Wo = R("Wo")
            yc = A.alloc([128, 12, 512], BF16); r_yc = R("yc")
            mT = A.alloc([128, 8, 512], BF16); r_mT = [R("mT%d" % f) for f in range(8)]
            sg = [A.alloc([128, 512], F32) for _ in range(2)]; r_sg = [R("sg%d" % i) for i in range(2)]
            macc = A.alloc([128, 512], F32); r_macc = R("macc")
            tmp = A.alloc([128, 512], F32); r_tmp = R("tmp")
            xt = [A.alloc([128, D], F32) for _ in range(2)]; r_xt = [R("cxt%d" % i) for i in range(2)]
            hb = [A.alloc([128, D], BF16) for _ in range(2)]; r_hb = [R("chb%d" % i) for i in range(2)]
            junk = A.alloc([128, D], BF16); r_junk = R("cjunk")
            gbc = A.alloc([128, D], F32); r_gbc = R("cgbc")
            load_gbc(gbc, r_gbc, final_g_d[0:1, :] if last else norm_g_d[l + 1:l + 2, :])
            wbv = w_br_d[l].rearrange("(u p) f -> p u f", p=128)
            wgv = w_in_d[l, :, OFF["merge_g"]:OFF["merge_g"] + 3072].rearrange("(c p) (n q) -> p c n q", p=128, n=3)
            Wg4 = Wg.rearrange("p c (n q) -> p c n q", n=3)
            for f in range(8):
                P.op("pool", I("dma_start", out=Wb[:, :, f * 128:(f + 1) * 128], in_=wbv[:, :, f * 128:(f + 1) * 128]),
                     writes=[r_Wb[f]], dma_chan=r_Wb[f])
                for n in range(3):
                    P.op("pool", I("dma_start", out=Wg4[:, :, n, f * 128:(f + 1) * 128], in_=wgv[:, :, n, f * 128:(f + 1) * 128]),
                         writes=[r_Wg[f]], dma_chan=r_Wg[f])
            wov = w_out_d[l].rearrange("(f p) o -> p f o", p=128)
            for hh in range(2):
                P.op("pool", I("dma_start", out=Wo[:, :, hh * 512:(hh + 1) * 512], in_=wov[:, :, hh * 512:(hh + 1) * 512]),
                     writes=[r_Wo], dma_chan=r_Wo)
            xsrc = x_d if l == 0 else xs_d
            CB = (0, 1, 2, 3)
            OB = (4, 5)
            pending = []

            def flush():
                while pending:
                    pending.pop(0)()

            for tc in range(NG):
                P.op("sp", I("dma_start", out=yc, in_=yT_d[:, :, tc * 512:(tc + 1) * 512].rearrange("u p s -> p u s")),
                     writes=[r_yc], dma_chan=r_yc)
                hts = r_hT[tc * 4:tc * 4 + 4]
                for f in range(8):
                    for n in range(3):
                        bb = CB[nxt("cb", 4)]
                        for j in range(4):
                            P.op("pe", I("matmul", banks[bb][:, :], lhsT=Wb[:, n * 4 + j, f * 128:(f + 1) * 128], rhs=yc[:, n * 4 + j, :],
                                                                                start=(j == 0), stop=(j == 3)),
                                 reads=[r_Wb[f], r_yc], writes=[r_bank[bb]])
                        gb = CB[nxt("cb", 4)]
                        for c in range(8):
                            P.op("pe", I("matmul", banks[gb][:, :], lhsT=Wg[:, c, n * 1024 + f * 128:n * 1024 + (f + 1) * 128],
                                                                                       rhs=hT[:, c, tc * 512:(tc + 1) * 512], start=(c == 0), stop=(c == 7)),
                                 reads=[r_Wg[f]] + hts, writes=[r_bank[gb]])
                        if f == 0 and n == 1:
                            flush()
                        sgi = nxt("sg", 2)
                        P.op("act", I("activation", out=sg[sgi], in_=banks[gb][:, :], func=AF.Sigmoid),
                             reads=[r_bank[gb]], writes=[r_sg[sgi]])
                        if n == 0:
                            P.op("dve", I("tensor_tensor", out=macc, in0=banks[bb][:, :], in1=sg[sgi], op=ALU.mult),
                                 reads=[r_bank[bb], r_sg[sgi]], writes=[r_macc])
                        else:
                            P.op("dve", I("tensor_tensor", out=tmp, in0=banks[bb][:, :], in1=sg[sgi], op=ALU.mult),
                                 reads=[r_bank[bb], r_sg[sgi]], writes=[r_tmp])
                            if n == 1:
                                P.op("dve", I("tensor_tensor", out=macc, in0=macc, in1=tmp, op=ALU.add), reads=[r_macc, r_tmp], writes=[r_macc])
                            else:
                                P.op("dve", I("tensor_tensor", out=mT[:, f, :], in0=macc, in1=tmp, op=ALU.add),
                                     reads=[r_macc, r_tmp], writes=[r_mT[f]])
                for tt in range(4):
                    t = tc * 4 + tt
                    s = nxt("cxt", 2)
                    P.op("sp", I("dma_start", out=xt[s], in_=xsrc[t * 128:(t + 1) * 128, :]), writes=[r_xt[s]], dma_chan=r_xt[s])
                    for hh in range(2):
                        ob = OB[nxt("ob", 2)]
                        for f in range(8):
                            P.op("pe", I("matmul", banks[ob][:, :], lhsT=mT[:, f, tt * 128:(tt + 1) * 128],
                                                                                    rhs=Wo[:, f, hh * 512:(hh + 1) * 512], start=(f == 0), stop=(f == 7)),
                                 reads=r_mT + [r_Wo], writes=[r_bank[ob]])
                        P.op("dve", I("tensor_tensor", out=xt[s][:, hh * 512:(hh + 1) * 512], in0=banks[ob][:, :],
                                                                                 in1=xt[s][:, hh * 512:(hh + 1) * 512], op=ALU.add),
                             reads=[r_bank[ob], r_xt[s]], writes=[r_xt[s]])
                    if not last:
                        P.op("sp", I("dma_start", out=xs_d[t * 128:(t + 1) * 128, :], in_=xt[s]), reads=[r_xt[s]], dma_chan=r_xt[s])
                        pp = norm_tile(xt[s], r_xt[s], t, gbc, r_gbc, (hb, r_hb, junk, r_junk), defer=True)
                        flush()
                        pending.append(pp)
                    else:
                        fs = nxt("fss", 2)
                        ss = small[:, 8 + 4 * fs:12 + 4 * fs]
                        r_ss = R("sm_fss%d" % fs)
                        P.op("act", I("activation", out=junk, in_=xt[s], func=AF.Square, accum_out=ss[:, 0:1]),
                             reads=[r_xt[s]], writes=[r_junk, r_ss])
                        P.op("act", I("activation", out=ss[:, 1:2], in_=ss[:, 0:1], func=AF.Sqrt, scale=1.0 / D, bias=EPS),
                             reads=[r_ss], writes=[r_ss])
                        P.op("dve", I("reciprocal", out=ss[:, 2:3], in_=ss[:, 1:2]), reads=[r_ss], writes=[r_ss])
                        P.op("dve", I("scalar_tensor_tensor", out=xt[s], in0=xt[s], scalar=ss[:, 2:3], in1=gbc, op0=ALU.mult, op1=ALU.mult),
                             reads=[r_xt[s], r_ss, r_gbc], writes=[r_xt[s]])
                        P.op("sp", I("dma_start", out=out_d[t * 128:(t + 1) * 128, :], in_=xt[s]), reads=[r_xt[s]], dma_chan=r_xt[s])
            flush()

        phaseA()
        for l in range(DEPTH):
            P.barrier()
            phaseB(l)
            P.barrier()
            phaseC(l)
        stats = P.emit(nc, st)
        stats["arena_peak"] = A.peak
        build.stats = stats
    return nc


def make_consts(S):
    bf = ml_dtypes.bfloat16
    t = np.arange(S)
    qaug = np.stack([(t // 64) * 64, t % 64, np.ones(S), np.ones(S)]).astype(np.float32)
    kaug = np.zeros((8, 4, S), np.float32)
    for j in range(8):
        sl = 2.0 ** -(j + 1)
        kaug[j, 0] = -sl
        kaug[j, 1] = -sl
        kaug[j, 2] = sl * ((t // 64) * 64)
        kaug[j, 3] = sl * (t % 64)
    k = np.arange(128)[:, None]
    q = np.arange(128)[None, :]
    cmask = (q >= k).astype(np.float32)
    dm = np.zeros((128, 17, 128), np.float32)
    for dlt in range(17):
        d = 128 * dlt + q - k
        mult = ((d >= 0) & (d <= 128)).astype(np.float32) + ((d >= 0) & (d % 4 == 0) & (d <= 512)) + ((d >= 0) & (d % 16 == 0) & (d <= 2048))
        dm[:, dlt, :] = mult
    return dict(ident_bf=np.eye(128).astype(bf), ident_f=np.eye(128).astype(np.float32), qaug=qaug.astype(bf), kaug=kaug.astype(bf),
                ones3=np.ones((3, S)).astype(bf), cmask=cmask.astype(bf), dmask=dm.reshape(128, 17 * 128).astype(bf))


def kernel(x, norm_g, w_in, fox_fb, diff_lam, diff_norm_g, w_branch, w_out, final_g):
    x = np.asarray(x, np.float32)
    B, S, _ = x.shape
    DEPTH = norm_g.shape[0]
    nc = build(S, DEPTH)
    shared = dict(norm_g=np.ascontiguousarray(norm_g, np.float32), w_in=np.ascontiguousarray(w_in, np.float32),
                  fox_fb=np.ascontiguousarray(fox_fb, np.float32),
                  diff_lam=np.ascontiguousarray(np.asarray(diff_lam, np.float32).reshape(DEPTH, 256)),
                  diff_norm_g=np.ascontiguousarray(diff_norm_g, np.float32),
                  w_branch=np.ascontiguousarray(np.asarray(w_branch, np.float32).reshape(DEPTH, 1536, D)),
                  w_out=np.ascontiguousarray(w_out, np.float32),
                  final_g=np.ascontiguousarray(np.asarray(final_g, np.float32).reshape(1, D)))
    shared.update(make_consts(S))
    in_maps = [dict(shared, x=np.ascontiguousarray(x[b])) for b in range(B)]
    res = run_bass_kernel_spmd(nc, in_maps, core_ids=list(range(B)))
    return np.stack([np.asarray(r["out"], np.float32) for r in res.results], axis=0)
```

```python
import math
import numpy as np
import ml_dtypes
from contextlib import ExitStack
import concourse.bass as bass
import concourse.mybir as mybir
from concourse.bass_utils import run_bass_kernel_spmd

F32 = mybir.dt.float32
BF16 = mybir.dt.bfloat16
AF = mybir.ActivationFunctionType
ALU = mybir.AluOpType

D = 1024
INW = 9224
OFF = dict(diff_q=0, diff_k=512, diff_v=1024, diff_z=1536, fox_q=2048, fox_k=2560, fox_v=3072,
           fox_f=3584, fox_z=3592, dil_q=4104, dil_k=4616, dil_v=5128, dil_z=5640, merge_g=6152)
EPS = 1e-6
ENGS = ("pe", "act", "dve", "pool", "sp")
SWDGE_DEPTH = 3
SAME_ENG_WINDOW = 4


def I(meth, *args, **kw):
    f = lambda e: getattr(e, meth)(*args, **kw)
    f.multi = (meth in ("matmul", "transpose")) or (kw.get("accum_out") is not None)
    return f


class Res:
    __slots__ = ("name", "last_w", "readers", "dma_readers", "dsem", "dcount")

    def __init__(self, name):
        self.name = name
        self.last_w = None
        self.readers = {}
        self.dma_readers = []
        self.dsem = None
        self.dcount = 0


class Op:
    __slots__ = ("eng", "emit", "deps", "signal", "sigsem", "sigval", "is_dma", "epoch", "seq")

    def __init__(self, eng, emit, is_dma, epoch):
        self.seq = 0
        self.eng = eng
        self.emit = emit
        self.deps = []
        self.signal = False
        self.sigsem = None
        self.sigval = 0
        self.is_dma = is_dma
        self.epoch = epoch


class Prog:
    def __init__(self):
        self.ops = {e: [] for e in ENGS}
        self.epoch = 0
        self.chans = []
        self.chan_last = {}
        self.last_op = {}
        self.pending_bar = {e: [] for e in ENGS}
        self.res = {}
        self.pool_dmas = []

    def R(self, name):
        r = self.res.get(name)
        if r is None:
            r = self.res[name] = Res(name)
        return r

    def op(self, eng, emit, reads=(), writes=(), dma_chan=None):
        is_dma = dma_chan is not None
        o = Op(eng, emit, is_dma, self.epoch)
        o.seq = len(self.ops[eng])
        deps = list(self.pending_bar[eng])
        self.pending_bar[eng] = []
        for r in reads:
            if r.last_w is not None:
                deps.append(r.last_w)
        for r in writes:
            if r.last_w is not None:
                deps.append(r.last_w)
            deps.extend(r.readers.values())
            deps.extend(r.dma_readers)
        seen = set()
        for d in deps:
            if id(d) in seen:
                continue
            seen.add(id(d))
            if (not d.is_dma) and d.eng == eng and (eng == "pe" or o.seq - d.seq > SAME_ENG_WINDOW):
                continue
            o.deps.append(d)
            d.signal = True
        if is_dma and eng == "pool":
            self.pool_dmas.append(o)
            if len(self.pool_dmas) > SWDGE_DEPTH:
                d = self.pool_dmas[-1 - SWDGE_DEPTH]
                if id(d) not in seen:
                    o.deps.append(d)
        for r in reads:
            if is_dma:
                r.dma_readers.append(o)
            else:
                r.readers[eng] = o
        for r in writes:
            r.last_w = o
            r.readers = {}
            r.dma_readers = []
        if is_dma:
            if dma_chan.dsem is None:
                dma_chan.dsem = "ch%d" % len(self.chans)
                self.chans.append(dma_chan)
            dma_chan.dcount += 16
            o.sigsem = dma_chan.dsem
            o.sigval = dma_chan.dcount
            o.signal = True
            self.chan_last[dma_chan.dsem] = o
        else:
            self.last_op[eng] = o
        self.ops[eng].append(o)
        return o

    def barrier(self):
        deps = list(self.last_op.values()) + list(self.chan_last.values())
        for d in deps:
            d.signal = True
        for e in ENGS:
            self.pending_bar[e] = list(deps)

    def emit(self, nc, stack):
        semkeys = set()
        for e in ENGS:
            cnt = {}
            for o in self.ops[e]:
                if o.is_dma:
                    semkeys.add(o.sigsem)
                elif o.signal:
                    k = "%s_%d" % (e, o.epoch)
                    cnt[k] = cnt.get(k, 0) + 1
                    o.sigsem = k
                    o.sigval = cnt[k]
                    semkeys.add(k)
        sems = {k: stack.enter_context(nc.semaphore(k)) for k in sorted(semkeys)}
        block = stack.enter_context(nc.Block())
        engmap = {"pe": block.tensor, "act": block.scalar, "dve": block.vector,
                  "pool": block.gpsimd, "sp": block.sync}
        stats = {}
        chans = self.chans
        for e in ENGS:
            def body(eng, ops=self.ops[e], e=e):
                waited = {}
                nw = 0
                for o in ops:
                    need = {}
                    for d in o.deps:
                        if waited.get(d.sigsem, 0) >= d.sigval:
                            continue
                        need[d.sigsem] = max(need.get(d.sigsem, 0), d.sigval)
                    need = list(need.items())
                    attach = None
                    if need and e != "pe" and not getattr(o.emit, "multi", True):
                        attach = need.pop()
                    for k, v in need:
                        eng.wait_ge(sems[k], v)
                        waited[k] = v
                        nw += 1
                    ins = o.emit(eng)
                    if attach is not None:
                        ins._wait_ge(sems[attach[0]], attach[1])
                        waited[attach[0]] = attach[1]
                    if o.signal:
                        ins.then_inc(sems[o.sigsem], 16 if o.is_dma else 1)
                if e == "sp":
                    for r in chans:
                        eng.wait_ge(sems[r.dsem], r.dcount)
                stats[e] = (len(ops), nw)
            engmap[e](body)
        stats["nsem"] = len(sems)
        return stats


class Arena:
    def __init__(self, ap_f32):
        self.base = ap_f32
        self.cap = ap_f32.shape[1] * 4
        self.off = 0
        self.peak = 0

    def reset(self):
        self.off = 0

    def overlay(self, off):
        old = self.off
        self.off = off
        return old

    def alloc(self, shape, dtype):
        esz = 2 if dtype == BF16 else 4
        n = 1
        for s in shape[1:]:
            n *= s
        nbytes = (n * esz + 31) // 32 * 32
        assert self.off + nbytes <= self.cap, "arena overflow %d + %d > %d" % (self.off, nbytes, self.cap)
        v = self.base[:, self.off // 4:(self.off + nbytes) // 4]
        if dtype == BF16:
            v = v.bitcast(BF16)
        v = v[0:shape[0], 0:n]
        if len(shape) == 3:
            v = v.rearrange("p (a b) -> p a b", a=shape[1])
        elif len(shape) == 4:
            v = v.rearrange("p (a b c) -> p a b c", a=shape[1], b=shape[2])
        self.off += nbytes
        self.peak = max(self.peak, self.off)
        return v


def build(S, DEPTH):
    NT = S // 128
    NG = S // 512
    nc = bass.Bass("TRN2", target_bir_lowering=False)
    dram = lambda name, shape, dt, kind: nc.dram_tensor(name, shape, dt, kind=kind).ap()
    x_d = dram("x", [S, D], F32, "ExternalInput")
    norm_g_d = dram("norm_g", [DEPTH, D], F32, "ExternalInput")
    w_in_d = dram("w_in", [DEPTH, D, INW], F32, "ExternalInput")
    fox_fb_d = dram("fox_fb", [DEPTH, 8], F32, "ExternalInput")
    diff_lam_d = dram("diff_lam", [DEPTH, 256], F32, "ExternalInput")
    diff_ng_d = dram("diff_norm_g", [DEPTH, 512], F32, "ExternalInput")
    w_br_d = dram("w_branch", [DEPTH, 1536, D], F32, "ExternalInput")
    w_out_d = dram("w_out", [DEPTH, D, D], F32, "ExternalInput")
    final_g_d = dram("final_g", [1, D], F32, "ExternalInput")
    identb_d = dram("ident_bf", [128, 128], BF16, "ExternalInput")
    identf_d = dram("ident_f", [128, 128], F32, "ExternalInput")
    qaug_d = dram("qaug", [4, S], BF16, "ExternalInput")
    kaug_d = dram("kaug", [8, 4, S], BF16, "ExternalInput")
    ones3_d = dram("ones3", [3, S], BF16, "ExternalInput")
    cmask_d = dram("cmask", [128, 128], BF16, "ExternalInput")
    dmask_d = dram("dmask", [128, 17 * 128], BF16, "ExternalInput")
    out_d = dram("out", [S, D], F32, "ExternalOutput")
    xs_d = dram("xs_scr", [S, D], F32, "Internal")
    yT_d = dram("yT_scr", [12, 128, S], BF16, "Internal")
    caug_d = dram("caug_scr", [8, 6, S], BF16, "Internal")

    P = Prog()
    R = P.R
    st = ExitStack()
    with st:
        hT = st.enter_context(nc.sbuf_tensor("hT", [128, 8, S], BF16))
        identb = st.enter_context(nc.sbuf_tensor("identb", [128, 128], BF16))
        identf = st.enter_context(nc.sbuf_tensor("identf", [128, 128], F32))
        small = st.enter_context(nc.sbuf_tensor("small", [128, 64], F32))
        PERS = 8 * S * 2 + 256 + 512 + 256
        arena_t = st.enter_context(nc.sbuf_tensor("arena", [128, (212400 - PERS) // 4 - 64], F32))
        A = Arena(arena_t[:, :])
        banks = [st.enter_context(nc.psum_tensor("bank%d" % i, [128, 512], F32)) for i in range(8)]
        r_bank = [R("bank%d" % i) for i in range(8)]
        r_hT = [R("hT%d" % t) for t in range(NT)]
        r_small = {}

        def sm(name, lo, hi):
            r_small[name] = R("sm_" + name)
            return small[:, lo:hi]

        ring = {}

        def nxt(name, n):
            v = ring.get(name, 0)
            ring[name] = v + 1
            return v % n

        MISC = (6, 7)

        def misc_bank():
            return MISC[nxt("misc", 2)]

        P.op("sp", I("dma_start", out=identb[:, :], in_=identb_d[:, :]), writes=[R("identb")], dma_chan=R("identb"))
        P.op("sp", I("dma_start", out=identf[:, :], in_=identf_d[:, :]), writes=[R("identf")], dma_chan=R("identf"))

        def norm_tile(src, r_src, t, gbc, r_gbc, bufs, defer=False):
            hb, r_hb, junk, r_junk = bufs
            s = nxt("hb", 2)
            ss = small[:, 0 + 4 * s:4 + 4 * s]
            r_ss = R("sm_ss%d" % s)
            P.op("act", I("activation", out=junk, in_=src, func=AF.Square, accum_out=ss[:, 0:1]),
                 reads=[r_src], writes=[r_junk, r_ss])
            P.op("act", I("activation", out=ss[:, 1:2], in_=ss[:, 0:1], func=AF.Sqrt, scale=1.0 / D, bias=EPS),
                 reads=[r_ss], writes=[r_ss])
            P.op("dve", I("reciprocal", out=ss[:, 2:3], in_=ss[:, 1:2]), reads=[r_ss], writes=[r_ss])
            P.op("dve", I("scalar_tensor_tensor", out=hb[s], in0=src, scalar=ss[:, 2:3], in1=gbc,
                                                         op0=ALU.mult, op1=ALU.mult),
                 reads=[r_src, r_ss, r_gbc], writes=[r_hb[s]])
            def post():
                b = misc_bank()
                tpv = banks[b][:, :].bitcast(BF16)
                for c in range(8):
                    P.op("pe", I("transpose", out=tpv[:, c * 128:(c + 1) * 128], in_=hb[s][:, c * 128:(c + 1) * 128],
                                 identity=identb[:, :]),
                         reads=[r_hb[s], R("identb")], writes=[r_bank[b]])
                P.op("dve", I("tensor_copy", out=hT[:, :, t * 128:(t + 1) * 128],
                              in_=tpv.rearrange("p (c k) -> p c k", c=8)),
                     reads=[r_bank[b]], writes=[r_hT[t]])
            if defer:
                return post
            post()
            return None

        def load_gbc(gbc, r_gbc, src_row):
            P.op("sp", I("dma_start", out=gbc, in_=src_row.to_broadcast([128, D])), writes=[r_gbc], dma_chan=r_gbc)

        def phaseA():
            P.epoch += 1
            A.reset()
            gbc = A.alloc([128, D], F32); r_gbc = R("gbc")
            xt = [A.alloc([128, D], F32) for _ in range(2)]; r_xt = [R("xt%d" % i) for i in range(2)]
            hb = [A.alloc([128, D], BF16) for _ in range(2)]; r_hb = [R("hb%d" % i) for i in range(2)]
            junk = A.alloc([128, D], BF16); r_junk = R("junk")
            load_gbc(gbc, r_gbc, norm_g_d[0:1, :])
            for t in range(NT):
                s = t % 2
                P.op("sp", I("dma_start", out=xt[s], in_=x_d[t * 128:(t + 1) * 128, :]),
                     writes=[r_xt[s]], dma_chan=r_xt[s])
                norm_tile(xt[s], r_xt[s], t, gbc, r_gbc, (hb, r_hb, junk, r_junk))

        def phaseB(l):
            P.epoch += 1
            A.reset()
            lam_init = 0.8 - 0.6 * math.exp(-0.3 * l)
            Vaug = A.alloc([128, NT * 520], BF16); r_V = R("Vaug")
            QT = [A.alloc([128, S], BF16) for _ in range(3)]; r_QT = [R("QT%d" % i) for i in range(3)]
            KT = [A.alloc([128, S], BF16) for _ in range(3)]; r_KT = [R("KT%d" % i) for i in range(3)]
            O_off = A.off
            O = A.alloc([128, NT, 128], F32); r_O = [R("O%d" % g) for g in range(NG)]
            siluT = A.alloc([128, S], BF16); r_silu = [R("silu%d" % g) for g in range(NG)]
            PT = [A.alloc([128, 512], BF16) for _ in range(3)]; r_PT = [R("PT%d" % i) for i in range(3)]
            Wv = A.alloc([128, 8, 512], BF16); r_Wv = R("Wv")
            Wz = [A.alloc([128, 8, 128], BF16) for _ in range(2)]; r_Wz = [R("Wz%d" % i) for i in range(2)]
            Wq = [A.alloc([128, 8, 128], BF16) for _ in range(2)]; r_Wq = [R("Wq%d" % i) for i in range(2)]
            Wk = [A.alloc([128, 8, 128], BF16) for _ in range(2)]; r_Wk = [R("Wk%d" % i) for i in range(2)]
            ystage = [A.alloc([128, 512], BF16) for _ in range(2)]; r_ys = [R("ys%d" % i) for i in range(2)]
            ze = [A.alloc([128, 512], F32) for _ in range(1)]; r_ze = [R("ze%d" % i) for i in range(1)]
            Obf = [A.alloc([128, 4, 128], BF16) for _ in range(2)]; r_Obf = [R("Obf%d" % i) for i in range(2)]
            cmask = A.alloc([128, 128], BF16); r_cm = R("cmask")
            dmask = A.alloc([128, 17 * 128], BF16); r_dm = R("dmask")
            junkf = A.alloc([128, 128], F32); r_junkf = R("junkf")
            dl = A.alloc([128, 256], F32); r_dl = R("dl")
            gn = A.alloc([128, 4], F32); r_gn = R("gn")
            Wf = A.alloc([128, 8, 8], BF16); r_Wf = R("Wf")
            use_ov = NT * 128 * 4 >= 9216
            _save = A.overlay(O_off) if use_ov else None
            fe = A.alloc([8, 256], F32); r_fe = R("fe")
            fsp = A.alloc([8, 256], F32); r_fsp = R("fsp")
            fC = [A.alloc([8, 256], F32) for _ in range(2)]; r_fC = [R("fC%d" % i) for i in range(2)]
            fr = A.alloc([8, 256], F32); r_fr = R("fr")
            ones8 = A.alloc([8, 256], F32); r_ones8 = R("ones8")
            aug6 = [A.alloc([8, 6, 256], BF16)] * 2; r_aug6 = [R("aug6_0")] * 2
            if use_ov:
                assert A.off <= O_off + NT * 128 * 4
                A.overlay(_save)
            fb8 = A.alloc([8, 2], F32); r_fb8 = R("fb8")
            r_caug = [R("caug%d" % i) for i in range(S // 256)]
            lsum = small[:, 16:18]; lexp = small[:, 18:20]; ltmp = small[:, 20:21]; neglam = small[:, 21:22]
            r_lam = R("sm_lam")
            gnp = small[:, 24:28]; r_gnp = R("sm_gnp")
            rden = [small[:, 32 + 4 * i:36 + 4 * i] for i in range(2)]; r_rden = [R("sm_rden%d" % i) for i in range(2)]
            ssq = small[:, 40:44]; lnv = small[:, 44:48]; rstd = small[:, 48:52]; r_rs = R("sm_rs")

            P.op("sp", I("dma_start", out=cmask, in_=cmask_d[:, :]), writes=[r_cm], dma_chan=r_cm)
            P.op("sp", I("dma_start", out=dmask, in_=dmask_d[:, :]), writes=[r_dm], dma_chan=r_dm)
            P.op("sp", I("dma_start", out=dl, in_=diff_lam_d[l:l + 1, :].to_broadcast([128, 256])), writes=[r_dl], dma_chan=r_dl)
            for a in range(4):
                P.op("sp", I("dma_start", out=gn[:, a:a + 1],
                                                      in_=diff_ng_d[l, a * 128:(a + 1) * 128].rearrange("(p o) -> p o", o=1)),
                     writes=[r_gn], dma_chan=r_gn)
            P.op("sp", I("dma_start", out=fb8[:, 0:1], in_=fox_fb_d[l, :].rearrange("(p o) -> p o", o=1)),
                 writes=[r_fb8], dma_chan=r_fb8)
            P.op("pool", I("dma_start", out=Wf, in_=w_in_d[l, :, OFF["fox_f"]:OFF["fox_f"] + 8].rearrange("(c p) n -> p c n", p=128)),
                 writes=[r_Wf], dma_chan=r_Wf)
            P.op("dve", I("scalar_tensor_tensor", out=junkf[:, 0:64], in0=dl[:, 0:64], scalar=1.0, in1=dl[:, 64:128],
                                                         op0=ALU.mult, op1=ALU.mult, accum_out=lsum[:, 0:1]),
                 reads=[r_dl], writes=[r_junkf, r_lam])
            P.op("dve", I("scalar_tensor_tensor", out=junkf[:, 0:64], in0=dl[:, 128:192], scalar=1.0, in1=dl[:, 192:256],
                                                         op0=ALU.mult, op1=ALU.mult, accum_out=lsum[:, 1:2]),
                 reads=[r_dl], writes=[r_junkf, r_lam])
            P.op("act", I("activation", out=lexp, in_=lsum, func=AF.Exp), reads=[r_lam], writes=[r_lam])
            P.op("dve", I("tensor_tensor", out=ltmp, in0=lexp[:, 0:1], in1=lexp[:, 1:2], op=ALU.subtract), reads=[r_lam], writes=[r_lam])
            P.op("dve", I("tensor_scalar", out=neglam, in0=ltmp, scalar1=-1.0, scalar2=-lam_init, op0=ALU.mult, op1=ALU.add),
                 reads=[r_lam], writes=[r_lam])
            P.op("dve", I("tensor_scalar", out=gnp, in0=gn, scalar1=1.0 - lam_init, scalar2=None, op0=ALU.mult),
                 reads=[r_gn], writes=[r_gnp])
            P.op("dve", I("tensor_scalar", out=fb8[:, 1:2], in0=fb8[:, 0:1], scalar1=-1.0, scalar2=None, op0=ALU.mult),
                 reads=[r_fb8], writes=[r_fb8])
            P.op("dve", I("memset", ones8, 1.0), reads=r_O, writes=[r_ones8])

            def Pf(eng, emit, reads=(), writes=(), dma_chan=None):
                return P.op(eng, emit, reads=list(reads) + r_O, writes=writes, dma_chan=dma_chan)

            prevC = None
            for ch in range(NG):
                b = misc_bank()
                for c in range(8):
                    Pf("pe", I("matmul", banks[b][0:8, :], lhsT=Wf[:, c, :], rhs=hT[:, c, ch * 512:(ch + 1) * 512],
                                                                   start=(c == 0), stop=(c == 7)),
                         reads=[r_Wf] + r_hT[ch * 4:ch * 4 + 4], writes=[r_bank[b]])
                for hf in range(2):
                    i = ch * 2 + hf
                    s = i % 2
                    Pf("act", I("activation", out=fe, in_=banks[b][0:8, hf * 256:(hf + 1) * 256], func=AF.Exp,
                                                                   scale=-1.0, bias=fb8[:, 1:2]),
                         reads=[r_bank[b], r_fb8], writes=[r_fe])
                    Pf("act", I("activation", out=fsp, in_=fe, func=AF.Ln, bias=1.0, scale=1.0), reads=[r_fe], writes=[r_fsp])
                    init = 0.0 if prevC is None else prevC[:, 255:256]
                    Pf("dve", I("tensor_tensor_scan", out=fC[s], data0=ones8, data1=fsp, initial=init,
                                                                              op0=ALU.mult, op1=ALU.add),
                         reads=[r_ones8, r_fsp, r_fC[1 - s]], writes=[r_fC[s]])
                    prevC = fC[s]
                    a6 = aug6[s]
                    Pf("dve", I("tensor_copy", out=a6[:, 0, :], in_=fC[s]), reads=[r_fC[s]], writes=[r_aug6[s]])
                    Pf("dve", I("tensor_tensor", out=fr, in0=fC[s], in1=a6[:, 0, :], op=ALU.subtract),
                         reads=[r_fC[s], r_aug6[s]], writes=[r_fr])
                    Pf("dve", I("tensor_copy", out=a6[:, 1, :], in_=fr), reads=[r_fr], writes=[r_aug6[s]])
                    Pf("dve", I("tensor_tensor", out=fr, in0=fr, in1=a6[:, 1, :], op=ALU.subtract),
                         reads=[r_fr, r_aug6[s]], writes=[r_fr])
                    Pf("dve", I("tensor_copy", out=a6[:, 2, :], in_=fr), reads=[r_fr], writes=[r_aug6[s]])
                    Pf("dve", I("tensor_scalar", out=a6[:, 3:6, :], in0=a6[:, 0:3, :], scalar1=-1.0, scalar2=None, op0=ALU.mult),
                         reads=[r_aug6[s]], writes=[r_aug6[s]])
                    Pf("sp", I("dma_start", out=caug_d[:, :, i * 256:(i + 1) * 256], in_=a6),
                         reads=[r_aug6[s]], writes=[r_caug[i]], dma_chan=r_aug6[s])

            maps = []
            for a in range(4):
                for c in range(2):
                    maps.append(dict(br=0, unit=a, sub=c, qoff=OFF["diff_q"] + a * 128 + c * 64, koff=OFF["diff_k"] + a * 128 + c * 64,
                                     kind="alibi", slope=2 * (a + 1) - 1, Kc=68, band=None, E=128, vh=a, full=True))
            for h in range(8):
                maps.append(dict(br=1, unit=4 + h // 2, sub=h % 2, qoff=OFF["fox_q"] + h * 64, koff=OFF["fox_k"] + h * 64,
                                 kind="fox", head=h, Kc=70, band=None, E=64, vh=h, full=True))
            for h in range(8):
                maps.append(dict(br=2, unit=8 + h // 2, sub=h % 2, qoff=OFF["dil_q"] + h * 64, koff=OFF["dil_k"] + h * 64,
                                 kind="alibi", slope=h, Kc=68, band=17, E=64, vh=h, full=False))
            voff = [OFF["diff_v"], OFF["fox_v"], OFF["dil_v"]]
            zoff = [OFF["diff_z"], OFF["fox_z"], OFF["dil_z"]]

            def Vview(br):
                if br == 0:
                    return Vaug[:, 0:NT * 516].rearrange("p (t h e) -> p t h e", t=NT, h=4)
                return Vaug.rearrange("p (t h e) -> p t h e", t=NT, h=8)

            def load_wqk(ui):
                mA, mB = maps[2 * ui], maps[2 * ui + 1]
                sA, sB = (2 * ui) % 3, (2 * ui + 1) % 3
                w = ui % 2
                P.op("pool", I("dma_start", out=Wq[w], in_=w_in_d[l, :, mA["qoff"]:mA["qoff"] + 128].rearrange("(c p) n -> p c n", p=128)),
                     writes=[r_Wq[w]], dma_chan=r_Wq[w])
                P.op("pool", I("dma_start", out=Wk[w], in_=w_in_d[l, :, mA["koff"]:mA["koff"] + 128].rearrange("(c p) n -> p c n", p=128)),
                     writes=[r_Wk[w]], dma_chan=r_Wk[w])
                for (m, s, zlo, a0) in ((mA, sA, 64, 64), (mB, sB, 0, 0)):
                    P.op("pool", I("memset", QT[s][zlo:zlo + 64, :], 0.0), writes=[r_QT[s]])
                    P.op("pool", I("memset", KT[s][zlo:zlo + 64, :], 0.0), writes=[r_KT[s]])
                    if m["kind"] == "alibi":
                        P.op("sp", I("dma_start", out=QT[s][a0:a0 + 4, :], in_=qaug_d[:, :]), writes=[r_QT[s]], dma_chan=r_QT[s])
                        P.op("sp", I("dma_start", out=KT[s][a0:a0 + 4, :], in_=kaug_d[m["slope"], :, :]), writes=[r_KT[s]], dma_chan=r_KT[s])
                    else:
                        h = m["head"]
                        P.op("sp", I("dma_start", out=QT[s][a0:a0 + 3, :], in_=ones3_d[:, :]), writes=[r_QT[s]], dma_chan=r_QT[s])
                        P.op("sp", I("dma_start", out=QT[s][a0 + 3:a0 + 6, :], in_=caug_d[h, 3:6, :]), reads=r_caug, writes=[r_QT[s]], dma_chan=r_QT[s])
                        P.op("sp", I("dma_start", out=KT[s][a0:a0 + 3, :], in_=caug_d[h, 0:3, :]), reads=r_caug, writes=[r_KT[s]], dma_chan=r_KT[s])
                        P.op("sp", I("dma_start", out=KT[s][a0 + 3:a0 + 6, :], in_=ones3_d[:, :]), writes=[r_KT[s]], dma_chan=r_KT[s])

            def proj_chunk(ui, ch):
                sA, sB = (2 * ui) % 3, (2 * ui + 1) % 3
                w = ui % 2
                hts = r_hT[ch * 4:ch * 4 + 4]
                cs = slice(ch * 512, (ch + 1) * 512)
                b = misc_bank()
                for c in range(8):
                    P.op("pe", I("matmul", banks[b][:, :], lhsT=Wq[w][:, c, :], rhs=hT[:, c, cs], start=(c == 0), stop=(c == 7)),
                         reads=[r_Wq[w]] + hts, writes=[r_bank[b]])
                P.op("dve", I("tensor_scalar", out=QT[sA][0:64, cs], in0=banks[b][0:64, :], scalar1=0.125, scalar2=None, op0=ALU.mult),
                     reads=[r_bank[b]], writes=[r_QT[sA]])
                P.op("dve", I("tensor_scalar", out=QT[sB][64:128, cs], in0=banks[b][64:128, :], scalar1=0.125, scalar2=None, op0=ALU.mult),
                     reads=[r_bank[b]], writes=[r_QT[sB]])
                b2 = misc_bank()
                for c in range(8):
                    P.op("pe", I("matmul", banks[b2][:, :], lhsT=Wk[w][:, c, :], rhs=hT[:, c, cs], start=(c == 0), stop=(c == 7)),
                         reads=[r_Wk[w]] + hts, writes=[r_bank[b2]])
                P.op("dve", I("tensor_copy", out=KT[sA][0:64, cs], in_=banks[b2][0:64, :]), reads=[r_bank[b2]], writes=[r_KT[sA]])
                P.op("dve", I("tensor_copy", out=KT[sB][64:128, cs], in_=banks[b2][64:128, :]), reads=[r_bank[b2]], writes=[r_KT[sB]])

            def load_wz(u):
                s = u % 2
                br, j = u // 4, u % 4
                o = zoff[br] + j * 128
                P.op("pool", I("dma_start", out=Wz[s], in_=w_in_d[l, :, o:o + 128].rearrange("(c p) n -> p c n", p=128)),
                     writes=[r_Wz[s]], dma_chan=r_Wz[s])

            def z_chunk(u, ch):
                s = u % 2
                b = misc_bank()
                for c in range(8):
                    P.op("pe", I("matmul", banks[b][:, :], lhsT=Wz[s][:, c, :], rhs=hT[:, c, ch * 512:(ch + 1) * 512],
                                                       start=(c == 0), stop=(c == 7)),
                         reads=[r_Wz[s]] + r_hT[ch * 4:ch * 4 + 4], writes=[r_bank[b]])
                zs = 0
                P.op("act", I("activation", out=ze[zs], in_=banks[b][:, :], func=AF.Exp, scale=-1.0), reads=[r_bank[b]], writes=[r_ze[zs]])
                P.op("dve", I("tensor_scalar", out=ze[zs], in0=ze[zs], scalar1=1.0, scalar2=None, op0=ALU.add), reads=[r_ze[zs]], writes=[r_ze[zs]])
                P.op("dve", I("reciprocal", out=ze[zs], in_=ze[zs]), reads=[r_ze[zs]], writes=[r_ze[zs]])
                P.op("dve", I("tensor_tensor", out=siluT[:, ch * 512:(ch + 1) * 512], in0=banks[b][:, :], in1=ze[zs], op=ALU.mult),
                     reads=[r_bank[b], r_ze[zs]], writes=[r_silu[ch]])

            def load_wv(br):
                o = voff[br]
                for hh in range(2):
                    P.op("pool", I("dma_start", out=Wv[:, :, hh * 256:(hh + 1) * 256],
                                                              in_=w_in_d[l, :, o + hh * 256:o + (hh + 1) * 256].rearrange("(c p) n -> p c n", p=128)),
                         writes=[r_Wv], dma_chan=r_Wv)

            def branch_setup(br):
                Vv = Vview(br)
                H, E = (4, 128) if br == 0 else (8, 64)
                P.op("pool", I("memset", Vv[:, :, :, E:E + 1], 1.0), writes=[r_V])
                for t in range(NT):
                    b = misc_bank()
                    for c in range(8):
                        P.op("pe", I("matmul", banks[b][:, :], lhsT=hT[:, c, t * 128:(t + 1) * 128], rhs=Wv[:, c, :],
                                                                     start=(c == 0), stop=(c == 7)),
                             reads=[r_Wv, r_hT[t]], writes=[r_bank[b]])
                    src = banks[b][:, :].rearrange("p (h e) -> p h e", h=H)
                    if t % 2 == 0:
                        P.op("act", I("activation", out=Vv[:, t, :, 0:E], in_=src, func=AF.Copy),
                             reads=[r_bank[b]], writes=[r_V])
                    else:
                        P.op("dve", I("tensor_copy", out=Vv[:, t, :, 0:E], in_=src), reads=[r_bank[b]], writes=[r_V])

            ACCB = (2, 3, 4, 5)

            def attention(mi, filler):
                m = maps[mi]
                s = mi % 3
                E = m["E"]
                Kc = m["Kc"]
                Vv = Vview(m["br"])
                nbk = 2 if E == 128 else 1
                steps = []

                def make_evac(g, b0, accv, bkof):
                    def evac():
                        rs = nxt("rden", 2)
                        rd = rden[rs]
                        if E == 64:
                            denv = banks[b0][:, 0:260].rearrange("p (j e) -> p j e", j=4)[:, :, 64]
                            P.op("dve", I("reciprocal", out=rd, in_=denv), reads=[r_bank[b0]], writes=[r_rden[rs]])
                            for j in range(4):
                                P.op("dve", I("tensor_scalar", out=O[:, 4 * g + j, m["sub"] * 64:(m["sub"] + 1) * 64], in0=accv[j][:, 0:64],
                                              scalar1=rd[:, j:j + 1], scalar2=None, op0=ALU.mult),
                                     reads=[r_bank[b0], r_rden[rs]], writes=[r_O[g]])
                        else:
                            for j in range(4):
                                P.op("dve", I("reciprocal", out=rd[:, j:j + 1], in_=accv[j][:, 128:129]),
                                     reads=[r_bank[bkof[j]]], writes=[r_rden[rs]])
                            if m["sub"] == 0:
                                for j in range(4):
                                    P.op("dve", I("tensor_scalar", out=O[:, 4 * g + j, :], in0=accv[j][:, 0:128], scalar1=rd[:, j:j + 1],
                                                  scalar2=None, op0=ALU.mult),
                                         reads=[r_bank[bkof[j]], r_rden[rs]], writes=[r_O[g]])
                            else:
                                P.op("dve", I("tensor_scalar", out=rd, in0=rd, scalar1=neglam, scalar2=None, op0=ALU.mult),
                                     reads=[r_rden[rs], r_lam], writes=[r_rden[rs]])
                                for j in range(4):
                                    P.op("dve", I("scalar_tensor_tensor", out=O[:, 4 * g + j, :], in0=accv[j][:, 0:128], scalar=rd[:, j:j + 1],
                                                  in1=O[:, 4 * g + j, :], op0=ALU.mult, op1=ALU.add),
                                         reads=[r_bank[bkof[j]], r_rden[rs], r_O[g]], writes=[r_O[g]])
                                    P.op("dve", I("scalar_tensor_tensor", out=junkf, in0=O[:, 4 * g + j, :], scalar=1.0, in1=O[:, 4 * g + j, :],
                                                  op0=ALU.mult, op1=ALU.mult, accum_out=ssq[:, j:j + 1]),
                                         reads=[r_O[g]], writes=[r_junkf, r_rs])
                                P.op("act", I("activation", out=lnv, in_=ssq, func=AF.Ln, scale=1.0 / 128, bias=EPS), reads=[r_rs], writes=[r_rs])
                                P.op("act", I("activation", out=rstd, in_=lnv, func=AF.Exp, scale=-0.5), reads=[r_rs], writes=[r_rs])
                                for j in range(4):
                                    P.op("dve", I("tensor_scalar", out=O[:, 4 * g + j, :], in0=O[:, 4 * g + j, :], scalar1=rstd[:, j:j + 1],
                                                  scalar2=None, op0=ALU.mult),
                                         reads=[r_O[g], r_rs], writes=[r_O[g]])
                    return evac

                STB = (0, 1) if nbk == 2 else (0, 1, 4, 5)
                LOOK = 1 if nbk == 2 else 3
                for g in range(NG):
                    if nbk == 2:
                        b0 = ACCB[2 * nxt("acc2", 2)]
                        ring["acc1"] = 0
                        bks = [b0, b0 + 1]
                        accv = [banks[bks[j // 2]][:, (j % 2) * 129:(j % 2) * 129 + 129] for j in range(4)]
                        bkof = [bks[j // 2] for j in range(4)]
                    else:
                        b0 = ACCB[nxt("acc1", 2)]
                        ring["acc2"] = 0
                        bks = [b0]
                        accv = [banks[b0][:, j * 65:j * 65 + 65] for j in range(4)]
                        bkof = [b0] * 4
                    first = {b: True for b in bks}
                    kb_lo = 0 if m["full"] else max(0, 4 * g - 16)
                    for kb in range(kb_lo, 4 * g + 4):
                        jlo = max(0, kb - 4 * g)
                        jhi = 3 if m["full"] else min(3, kb + 16 - 4 * g)
                        pvs = []
                        for j in range(jlo, jhi + 1):
                            bk = bkof[j]
                            pvs.append((j, bk, first[bk], 4 * g + j, accv[j]))
                            first[bk] = False
                        steps.append(dict(g=g, kb=kb, jlo=jlo, N=(jhi - jlo + 1) * 128, q0=(4 * g + jlo) * 128, sb=STB[nxt("st", len(STB))], p=nxt("pt", 3),
                                          pvs=pvs, last=(kb == 4 * g + 3), evac=make_evac(g, b0, accv, bkof)))

                def do_qk(t):
                    kb, N, q0, sb_ = t["kb"], t["N"], t["q0"], t["sb"]
                    P.op("pe", I("matmul", banks[sb_][:, 0:N], lhsT=KT[s][:, kb * 128:(kb + 1) * 128],
                                 rhs=QT[s][:, q0:q0 + N], start=True, stop=True),
                         reads=[r_KT[s], r_QT[s]], writes=[r_bank[sb_]])

                def do_exp(t):
                    kb, N, sb_, p, g, jlo = t["kb"], t["N"], t["sb"], t["p"], t["g"], t["jlo"]
                    P.op("act", I("activation", out=PT[p][:, 0:N], in_=banks[sb_][:, 0:N], func=AF.Exp),
                         reads=[r_bank[sb_]], writes=[r_PT[p]])
                    if m["full"]:
                        if kb >= 4 * g:
                            P.op("dve", I("tensor_tensor", out=PT[p][:, 0:128], in0=PT[p][:, 0:128], in1=cmask, op=ALU.mult),
                                 reads=[r_PT[p], r_cm], writes=[r_PT[p]])
                    else:
                        dlo = 4 * g + jlo - kb
                        P.op("dve", I("tensor_tensor", out=PT[p][:, 0:N], in0=PT[p][:, 0:N],
                                      in1=dmask[:, dlo * 128:dlo * 128 + N], op=ALU.mult),
                             reads=[r_PT[p], r_dm], writes=[r_PT[p]])

                def do_pv(t):
                    kb, p, jlo = t["kb"], t["p"], t["jlo"]
                    for (j, bk, stf, qb, av) in t["pvs"]:
                        P.op("pe", I("matmul", av, lhsT=PT[p][:, (j - jlo) * 128:(j - jlo + 1) * 128], rhs=Vv[:, kb, m["vh"], :],
                                     start=stf, stop=(kb == qb), skip_group_check=True),
                             reads=[r_PT[p], r_V], writes=[r_bank[bk]])

                n = len(steps)
                for i in range(min(LOOK, n)):
                    do_qk(steps[i])
                for i, t in enumerate(steps):
                    if i + LOOK < n:
                        do_qk(steps[i + LOOK])
                    do_exp(t)
                    do_pv(t)
                    if t["last"]:
                        t["evac"]()
                        filler(t["g"])

            def finalize_unit(u):
                br = u // 4
                for g in range(NG):
                    ob = nxt("obf", 2)
                    P.op("pool", I("tensor_copy", out=Obf[ob], in_=O[:, 4 * g:4 * g + 4, :]), reads=[r_O[g]], writes=[r_Obf[ob]])
                    b = misc_bank()
                    tpv = banks[b][:, :].bitcast(BF16)
                    for j in range(4):
                        P.op("pe", I("transpose", out=tpv[:, j * 128:(j + 1) * 128], in_=Obf[ob][:, j, :], identity=identb[:, :]),
                             reads=[r_Obf[ob], R("identb")], writes=[r_bank[b]])
                    ys = nxt("ys", 2)
                    if br == 0:
                        a = u % 4
                        P.op("dve", I("scalar_tensor_tensor", out=ystage[ys], in0=tpv[:, 0:512], scalar=gnp[:, a:a + 1],
                                      in1=siluT[:, g * 512:(g + 1) * 512], op0=ALU.mult, op1=ALU.mult),
                             reads=[r_bank[b], r_gnp, r_silu[g]], writes=[r_ys[ys]])
                    else:
                        P.op("dve", I("tensor_tensor", out=ystage[ys], in0=tpv[:, 0:512], in1=siluT[:, g * 512:(g + 1) * 512], op=ALU.mult),
                             reads=[r_bank[b], r_silu[g]], writes=[r_ys[ys]])
                    P.op("sp", I("dma_start", out=yT_d[u, :, g * 512:(g + 1) * 512], in_=ystage[ys]),
                         reads=[r_ys[ys]], dma_chan=r_ys[ys])

            nm = len(maps)
            nu = nm // 2
            load_wqk(0)
            load_wv(0)
            for ch in range(NG):
                proj_chunk(0, ch)
            for mi, m in enumerate(maps):
                u = m["unit"]
                ui = mi // 2
                if m["sub"] == 0:
                    load_wz(u)
                elif ui + 1 < nu:
                    load_wqk(ui + 1)
                if mi % 8 == 0:
                    branch_setup(m["br"])
                    if m["br"] < 2:
                        load_wv(m["br"] + 1)

                def filler(g, m=m, u=u, ui=ui):
                    if m["sub"] == 1:
                        if ui + 1 < nu:
                            proj_chunk(ui + 1, g)
                        z_chunk(u, g)
                attention(mi, filler)
                if m["sub"] == 1:
                    finalize_unit(u)
            build.phaseB_bytes = A.off

        def phaseC(l):
            P.epoch += 1
            A.reset()
            last = (l == DEPTH - 1)
            Wb = A.alloc([128, 12, D], BF16); r_Wb = [R("Wb%d" % f) for f in range(8)]
            Wg = A.alloc([128, 8, 3072], BF16); r_Wg = [R("Wg%d" % f) for f in range(8)]
            Wo = A.alloc([128, 8, D], BF16); r_Wo = R("Wo")
            yc = A.alloc([128, 12, 512], BF16); r_yc = R("yc")
            mT = A.alloc([128, 8, 512], BF16); r_mT = [R("mT%d" % f) for f in range(8)]
            sg = [A.alloc([128, 512], F32) for _ in range(2)]; r_sg = [R("sg%d" % i) for i in range(2)]
            macc = A.alloc([128, 512], F32); r_macc = R("macc")
            tmp = A.alloc([128, 512], F32); r_tmp = R("tmp")
            xt = [A.alloc([128, D], F32) for _ in range(2)]; r_xt = [R("cxt%d" % i) for i in range(2)]
            hb = [A.alloc([128, D], BF16) for _ in range(2)]; r_hb = [R("chb%d" % i) for i in range(2)]
            junk = A.alloc([128, D], BF16); r_junk = R("cjunk")
            gbc = A.alloc([128, D], F32); r_gbc = R("cgbc")
            load_gbc(gbc, r_gbc, final_g_d[0:1, :] if last else norm_g_d[l + 1:l + 2, :])
            wbv = w_br_d[l].rearrange("(u p) f -> p u f", p=128)
            wgv = w_in_d[l, :, OFF["merge_g"]:OFF["merge_g"] + 3072].rearrange("(c p) (n q) -> p c n q", p=128, n=3)
            Wg4 = Wg.rearrange("p c (n q) -> p c n q", n=3)
            for f in range(8):
                P.op("pool", I("dma_start", out=Wb[:, :, f * 128:(f + 1) * 128], in_=wbv[:, :, f * 128:(f + 1) * 128]),
                     writes=[r_Wb[f]], dma_chan=r_Wb[f])
                for n in range(3):
                    P.op("pool", I("dma_start", out=Wg4[:, :, n, f * 128:(f + 1) * 128], in_=wgv[:, :, n, f * 128:(f + 1) * 128]),
                         writes=[r_Wg[f]], dma_chan=r_Wg[f])
            wov = w_out_d[l].rearrange("(f p) o -> p f o", p=128)
            for hh in range(2):
                P.op("pool", I("dma_start", out=Wo[:, :, hh * 512:(hh + 1) * 512], in_=wov[:, :, hh * 512:(hh + 1) * 512]),
                     writes=[r_Wo], dma_chan=r_Wo)
            xsrc = x_d if l == 0 else xs_d
            CB = (0, 1, 2, 3)
            OB = (4, 5)
            pending = []

            def flush():
                while pending:
                    pending.pop(0)()

            for tc in range(NG):
                P.op("sp", I("dma_start", out=yc, in_=yT_d[:, :, tc * 512:(tc + 1) * 512].rearrange("u p s -> p u s")),
                     writes=[r_yc], dma_chan=r_yc)
                hts = r_hT[tc * 4:tc * 4 + 4]
                for f in range(8):
                    for n in range(3):
                        bb = CB[nxt("cb", 4)]
                        for j in range(4):
                            P.op("pe", I("matmul", banks[bb][:, :], lhsT=Wb[:, n * 4 + j, f * 128:(f + 1) * 128], rhs=yc[:, n * 4 + j, :],
                                                                                start=(j == 0), stop=(j == 3)),
                                 reads=[r_Wb[f], r_yc], writes=[r_bank[bb]])
                        gb = CB[nxt("cb", 4)]
                        for c in range(8):
                            P.op("pe", I("matmul", banks[gb][:, :], lhsT=Wg[:, c, n * 1024 + f * 128:n * 1024 + (f + 1) * 128],
                                                                                       rhs=hT[:, c, tc * 512:(tc + 1) * 512], start=(c == 0), stop=(c == 7)),
                                 reads=[r_Wg[f]] + hts, writes=[r_bank[gb]])
                        if f == 0 and n == 1:
                            flush()
                        sgi = nxt("sg", 2)
                        P.op("act", I("activation", out=sg[sgi], in_=banks[gb][:, :], func=AF.Sigmoid),
                             reads=[r_bank[gb]], writes=[r_sg[sgi]])
                        if n == 0:
                            P.op("dve", I("tensor_tensor", out=macc, in0=banks[bb][:, :], in1=sg[sgi], op=ALU.mult),
                                 reads=[r_bank[bb], r_sg[sgi]], writes=[r_macc])
                        else:
                            P.op("dve", I("tensor_tensor", out=tmp, in0=banks[bb][:, :], in1=sg[sgi], op=ALU.mult),
                                 reads=[r_bank[bb], r_sg[sgi]], writes=[r_tmp])
                            if n == 1:
                                P.op("dve", I("tensor_tensor", out=macc, in0=macc, in1=tmp, op=ALU.add), reads=[r_macc, r_tmp], writes=[r_macc])
                            else:
                                P.op("dve", I("tensor_tensor", out=mT[:, f, :], in0=macc, in1=tmp, op=ALU.add),
                                     reads=[r_macc, r_tmp], writes=[r_mT[f]])
                for tt in range(4):
                    t = tc * 4 + tt
                    s = nxt("cxt", 2)
                    P.op("sp", I("dma_start", out=xt[s], in_=xsrc[t * 128:(t + 1) * 128, :]), writes=[r_xt[s]], dma_chan=r_xt[s])
                    for hh in range(2):
                        ob = OB[nxt("ob", 2)]
                        for f in range(8):
                            P.op("pe", I("matmul", banks[ob][:, :], lhsT=mT[:, f, tt * 128:(tt + 1) * 128],
                                                                                    rhs=Wo[:, f, hh * 512:(hh + 1) * 512], start=(f == 0), stop=(f == 7)),
                                 reads=r_mT + [r_Wo], writes=[r_bank[ob]])
                        P.op("dve", I("tensor_tensor", out=xt[s][:, hh * 512:(hh + 1) * 512], in0=banks[ob][:, :],
                                                                                 in1=xt[s][:, hh * 512:(hh + 1) * 512], op=ALU.add),
                             reads=[r_bank[ob], r_xt[s]], writes=[r_xt[s]])
                    if not last:
                        P.op("sp", I("dma_start", out=xs_d[t * 128:(t + 1) * 128, :], in_=xt[s]), reads=[r_xt[s]], dma_chan=r_xt[s])
                        pp = norm_tile(xt[s], r_xt[s], t, gbc, r_gbc, (hb, r_hb, junk, r_junk), defer=True)
                        flush()
                        pending.append(pp)
                    else:
                        fs = nxt("fss", 2)
                        ss = small[:, 8 + 4 * fs:12 + 4 * fs]
                        r_ss = R("sm_fss%d" % fs)
                        P.op("act", I("activation", out=junk, in_=xt[s], func=AF.Square, accum_out=ss[:, 0:1]),
                             reads=[r_xt[s]], writes=[r_junk, r_ss])
                        P.op("act", I("activation", out=ss[:, 1:2], in_=ss[:, 0:1], func=AF.Sqrt, scale=1.0 / D, bias=EPS),
                             reads=[r_ss], writes=[r_ss])
                        P.op("dve", I("reciprocal", out=ss[:, 2:3], in_=ss[:, 1:2]), reads=[r_ss], writes=[r_ss])
                        P.op("dve", I("scalar_tensor_tensor", out=xt[s], in0=xt[s], scalar=ss[:, 2:3], in1=gbc, op0=ALU.mult, op1=ALU.mult),
                             reads=[r_xt[s], r_ss, r_gbc], writes=[r_xt[s]])
                        P.op("sp", I("dma_start", out=out_d[t * 128:(t + 1) * 128, :], in_=xt[s]), reads=[r_xt[s]], dma_chan=r_xt[s])
            flush()

        phaseA()
        for l in range(DEPTH):
            P.barrier()
            phaseB(l)
            P.barrier()
            phaseC(l)
        stats = P.emit(nc, st)
        stats["arena_peak"] = A.peak
        build.stats = stats
    return nc


def make_consts(S):
    bf = ml_dtypes.bfloat16
    t = np.arange(S)
    qaug = np.stack([(t // 64) * 64, t % 64, np.ones(S), np.ones(S)]).astype(np.float32)
    kaug = np.zeros((8, 4, S), np.float32)
    for j in range(8):
        sl = 2.0 ** -(j + 1)
        kaug[j, 0] = -sl
        kaug[j, 1] = -sl
        kaug[j, 2] = sl * ((t // 64) * 64)
        kaug[j, 3] = sl * (t % 64)
    k = np.arange(128)[:, None]
    q = np.arange(128)[None, :]
    cmask = (q >= k).astype(np.float32)
    dm = np.zeros((128, 17, 128), np.float32)
    for dlt in range(17):
        d = 128 * dlt + q - k
        mult = ((d >= 0) & (d <= 128)).astype(np.float32) + ((d >= 0) & (d % 4 == 0) & (d <= 512)) + ((d >= 0) & (d % 16 == 0) & (d <= 2048))
        dm[:, dlt, :] = mult
    return dict(ident_bf=np.eye(128).astype(bf), ident_f=np.eye(128).astype(np.float32), qaug=qaug.astype(bf), kaug=kaug.astype(bf),
                ones3=np.ones((3, S)).astype(bf), cmask=cmask.astype(bf), dmask=dm.reshape(128, 17 * 128).astype(bf))


def kernel(x, norm_g, w_in, fox_fb, diff_lam, diff_norm_g, w_branch, w_out, final_g):
    x = np.asarray(x, np.float32)
    B, S, _ = x.shape
    DEPTH = norm_g.shape[0]
    nc = build(S, DEPTH)
    shared = dict(norm_g=np.ascontiguousarray(norm_g, np.float32), w_in=np.ascontiguousarray(w_in, np.float32),
                  fox_fb=np.ascontiguousarray(fox_fb, np.float32),
                  diff_lam=np.ascontiguousarray(np.asarray(diff_lam, np.float32).reshape(DEPTH, 256)),
                  diff_norm_g=np.ascontiguousarray(diff_norm_g, np.float32),
                  w_branch=np.ascontiguousarray(np.asarray(w_branch, np.float32).reshape(DEPTH, 1536, D)),
                  w_out=np.ascontiguousarray(w_out, np.float32),
                  final_g=np.ascontiguousarray(np.asarray(final_g, np.float32).reshape(1, D)))
    shared.update(make_consts(S))
    in_maps = [dict(shared, x=np.ascontiguousarray(x[b])) for b in range(B)]
    res = run_bass_kernel_spmd(nc, in_maps, core_ids=list(range(B)))
    return np.stack([np.asarray(r["out"], np.float32) for r in res.results], axis=0)
```

```python
import math
import numpy as np
import ml_dtypes
from contextlib import ExitStack
import concourse.bass as bass
import concourse.mybir as mybir
from concourse.bass_utils import run_bass_kernel_spmd

F32 = mybir.dt.float32
BF16 = mybir.dt.bfloat16
AF = mybir.ActivationFunctionType
ALU = mybir.AluOpType

D = 1024
INW = 9224
OFF = dict(diff_q=0, diff_k=512, diff_v=1024, diff_z=1536, fox_q=2048, fox_k=2560, fox_v=3072,
           fox_f=3584, fox_z=3592, dil_q=4104, dil_k=4616, dil_v=5128, dil_z=5640, merge_g=6152)
EPS = 1e-6
ENGS = ("pe", "act", "dve", "pool", "sp")
SWDGE_DEPTH = 3
SAME_ENG_WINDOW = 4


def I(meth, *args, **kw):
    f = lambda e: getattr(e, meth)(*args, **kw)
    f.multi = (meth in ("matmul", "transpose")) or (kw.get("accum_out") is not None)
    return f


class Res:
    __slots__ = ("name", "last_w", "readers", "dma_readers", "dsem", "dcount")

    def __init__(self, name):
        self.name = name
        self.last_w = None
        self.readers = {}
        self.dma_readers = []
        self.dsem = None
        self.dcount = 0


class Op:
    __slots__ = ("eng", "emit", "deps", "signal", "sigsem", "sigval", "is_dma", "epoch", "seq")

    def __init__(self, eng, emit, is_dma, epoch):
        self.seq = 0
        self.eng = eng
        self.emit = emit
        self.deps = []
        self.signal = False
        self.sigsem = None
        self.sigval = 0
        self.is_dma = is_dma
        self.epoch = epoch


class Prog:
    def __init__(self):
        self.ops = {e: [] for e in ENGS}
        self.epoch = 0
        self.chans = []
        self.chan_last = {}
        self.last_op = {}
        self.pending_bar = {e: [] for e in ENGS}
        self.res = {}
        self.pool_dmas = []

    def R(self, name):
        r = self.res.get(name)
        if r is None:
            r = self.res[name] = Res(name)
        return r

    def op(self, eng, emit, reads=(), writes=(), dma_chan=None):
        is_dma = dma_chan is not None
        o = Op(eng, emit, is_dma, self.epoch)
        o.seq = len(self.ops[eng])
        deps = list(self.pending_bar[eng])
        self.pending_bar[eng] = []
        for r in reads:
            if r.last_w is not None:
                deps.append(r.last_w)
        for r in writes:
            if r.last_w is not None:
                deps.append(r.last_w)
            deps.extend(r.readers.values())
            deps.extend(r.dma_readers)
        seen = set()
        for d in deps:
            if id(d) in seen:
                continue
            seen.add(id(d))
            if (not d.is_dma) and d.eng == eng and (eng == "pe" or o.seq - d.seq > SAME_ENG_WINDOW):
                continue
            o.deps.append(d)
            d.signal = True
        if is_dma and eng == "pool":
            self.pool_dmas.append(o)
            if len(self.pool_dmas) > SWDGE_DEPTH:
                d = self.pool_dmas[-1 - SWDGE_DEPTH]
                if id(d) not in seen:
                    o.deps.append(d)
        for r in reads:
            if is_dma:
                r.dma_readers.append(o)
            else:
                r.readers[eng] = o
        for r in writes:
            r.last_w = o
            r.readers = {}
            r.dma_readers = []
        if is_dma:
            if dma_chan.dsem is None:
                dma_chan.dsem = "ch%d" % len(self.chans)
                self.chans.append(dma_chan)
            dma_chan.dcount += 16
            o.sigsem = dma_chan.dsem
            o.sigval = dma_chan.dcount
            o.signal = True
            self.chan_last[dma_chan.dsem] = o
        else:
            self.last_op[eng] = o
        self.ops[eng].append(o)
        return o

    def barrier(self):
        deps = list(self.last_op.values()) + list(self.chan_last.values())
        for d in deps:
            d.signal = True
        for e in ENGS:
            self.pending_bar[e] = list(deps)

    def emit(self, nc, stack):
        semkeys = set()
        for e in ENGS:
            cnt = {}
            for o in self.ops[e]:
                if o.is_dma:
                    semkeys.add(o.sigsem)
                elif o.signal:
                    k = "%s_%d" % (e, o.epoch)
                    cnt[k] = cnt.get(k, 0) + 1
                    o.sigsem = k
                    o.sigval = cnt[k]
                    semkeys.add(k)
        sems = {k: stack.enter_context(nc.semaphore(k)) for k in sorted(semkeys)}
        block = stack.enter_context(nc.Block())
        engmap = {"pe": block.tensor, "act": block.scalar, "dve": block.vector,
                  "pool": block.gpsimd, "sp": block.sync}
        stats = {}
        chans = self.chans
        for e in ENGS:
            def body(eng, ops=self.ops[e], e=e):
                waited = {}
                nw = 0
                for o in ops:
                    need = {}
                    for d in o.deps:
                        if waited.get(d.sigsem, 0) >= d.sigval:
                            continue
                        need[d.sigsem] = max(need.get(d.sigsem, 0), d.sigval)
                    need = list(need.items())
                    attach = None
                    if need and e != "pe" and not getattr(o.emit, "multi", True):
                        attach = need.pop()
                    for k, v in need:
                        eng.wait_ge(sems[k], v)
                        waited[k] = v
                        nw += 1
                    ins = o.emit(eng)
                    if attach is not None:
                        ins._wait_ge(sems[attach[0]], attach[1])
                        waited[attach[0]] = attach[1]
                    if o.signal:
                        ins.then_inc(sems[o.sigsem], 16 if o.is_dma else 1)
                if e == "sp":
                    for r in chans:
                        eng.wait_ge(sems[r.dsem], r.dcount)
                stats[e] = (len(ops), nw)
            engmap[e](body)
        stats["nsem"] = len(sems)
        return stats


class Arena:
    def __init__(self, ap_f32):
        self.base = ap_f32
        self.cap = ap_f32.shape[1] * 4
        self.off = 0
        self.peak = 0

    def reset(self):
        self.off = 0

    def overlay(self, off):
        old = self.off
        self.off = off
        return old

    def alloc(self, shape, dtype):
        esz = 2 if dtype == BF16 else 4
        n = 1
        for s in shape[1:]:
            n *= s
        nbytes = (n * esz + 31) // 32 * 32
        assert self.off + nbytes <= self.cap, "arena overflow %d + %d > %d" % (self.off, nbytes, self.cap)
        v = self.base[:, self.off // 4:(self.off + nbytes) // 4]
        if dtype == BF16:
            v = v.bitcast(BF16)
        v = v[0:shape[0], 0:n]
        if len(shape) == 3:
            v = v.rearrange("p (a b) -> p a b", a=shape[1])
        elif len(shape) == 4:
            v = v.rearrange("p (a b c) -> p a b c", a=shape[1], b=shape[2])
        self.off += nbytes
        self.peak = max(self.peak, self.off)
        return v


def build(S, DEPTH):
    NT = S // 128
    NG = S // 512
    nc = bass.Bass("TRN2", target_bir_lowering=False)
    dram = lambda name, shape, dt, kind: nc.dram_tensor(name, shape, dt, kind=kind).ap()
    x_d = dram("x", [S, D], F32, "ExternalInput")
    norm_g_d = dram("norm_g", [DEPTH, D], F32, "ExternalInput")
    w_in_d = dram("w_in", [DEPTH, D, INW], F32, "ExternalInput")
    fox_fb_d = dram("fox_fb", [DEPTH, 8], F32, "ExternalInput")
    diff_lam_d = dram("diff_lam", [DEPTH, 256], F32, "ExternalInput")
    diff_ng_d = dram("diff_norm_g", [DEPTH, 512], F32, "ExternalInput")
    w_br_d = dram("w_branch", [DEPTH, 1536, D], F32, "ExternalInput")
    w_out_d = dram("w_out", [DEPTH, D, D], F32, "ExternalInput")
    final_g_d = dram("final_g", [1, D], F32, "ExternalInput")
    identb_d = dram("ident_bf", [128, 128], BF16, "ExternalInput")
    identf_d = dram("ident_f", [128, 128], F32, "ExternalInput")
    qaug_d = dram("qaug", [4, S], BF16, "ExternalInput")
    kaug_d = dram("kaug", [8, 4, S], BF16, "ExternalInput")
    ones3_d = dram("ones3", [3, S], BF16, "ExternalInput")
    cmask_d = dram("cmask", [128, 128], BF16, "ExternalInput")
    dmask_d = dram("dmask", [128, 17 * 128], BF16, "ExternalInput")
    out_d = dram("out", [S, D], F32, "ExternalOutput")
    xs_d = dram("xs_scr", [S, D], F32, "Internal")
    yT_d = dram("yT_scr", [12, 128, S], BF16, "Internal")
    caug_d = dram("caug_scr", [8, 6, S], BF16, "Internal")

    P = Prog()
    R = P.R
    st = ExitStack()
    with st:
        hT = st.enter_context(nc.sbuf_tensor("hT", [128, 8, S], BF16))
        identb = st.enter_context(nc.sbuf_tensor("identb", [128, 128], BF16))
        identf = st.enter_context(nc.sbuf_tensor("identf", [128, 128], F32))
        small = st.enter_context(nc.sbuf_tensor("small", [128, 64], F32))
        PERS = 8 * S * 2 + 256 + 512 + 256
        arena_t = st.enter_context(nc.sbuf_tensor("arena", [128, (212400 - PERS) // 4 - 64], F32))
        A = Arena(arena_t[:, :])
        banks = [st.enter_context(nc.psum_tensor("bank%d" % i, [128, 512], F32)) for i in range(8)]
        r_bank = [R("bank%d" % i) for i in range(8)]
        r_hT = [R("hT%d" % t) for t in range(NT)]
        r_small = {}

        def sm(name, lo, hi):
            r_small[name] = R("sm_" + name)
            return small[:, lo:hi]

        ring = {}

        def nxt(name, n):
            v = ring.get(name, 0)
            ring[name] = v + 1
            return v % n

        MISC = (6, 7)

        def misc_bank():
            return MISC[nxt("misc", 2)]

        P.op("sp", I("dma_start", out=identb[:, :], in_=identb_d[:, :]), writes=[R("identb")], dma_chan=R("identb"))
        P.op("sp", I("dma_start", out=identf[:, :], in_=identf_d[:, :]), writes=[R("identf")], dma_chan=R("identf"))

        def norm_tile(src, r_src, t, gbc, r_gbc, bufs, defer=False):
            hb, r_hb, junk, r_junk = bufs
            s = nxt("hb", 2)
            ss = small[:, 0 + 4 * s:4 + 4 * s]
            r_ss = R("sm_ss%d" % s)
            P.op("act", I("activation", out=junk, in_=src, func=AF.Square, accum_out=ss[:, 0:1]),
                 reads=[r_src], writes=[r_junk, r_ss])
            P.op("act", I("activation", out=ss[:, 1:2], in_=ss[:, 0:1], func=AF.Sqrt, scale=1.0 / D, bias=EPS),
                 reads=[r_ss], writes=[r_ss])
            P.op("dve", I("reciprocal", out=ss[:, 2:3], in_=ss[:, 1:2]), reads=[r_ss], writes=[r_ss])
            P.op("dve", I("scalar_tensor_tensor", out=hb[s], in0=src, scalar=ss[:, 2:3], in1=gbc,
                                                         op0=ALU.mult, op1=ALU.mult),
                 reads=[r_src, r_ss, r_gbc], writes=[r_hb[s]])
            def post():
                b = misc_bank()
                tpv = banks[b][:, :].bitcast(BF16)
                for c in range(8):
                    P.op("pe", I("transpose", out=tpv[:, c * 128:(c + 1) * 128], in_=hb[s][:, c * 128:(c + 1) * 128],
                                 identity=identb[:, :]),
                         reads=[r_hb[s], R("identb")], writes=[r_bank[b]])
                P.op("dve", I("tensor_copy", out=hT[:, :, t * 128:(t + 1) * 128],
                              in_=tpv.rearrange("p (c k) -> p c k", c=8)),
                     reads=[r_bank[b]], writes=[r_hT[t]])
            if defer:
                return post
            post()
            return None

        def load_gbc(gbc, r_gbc, src_row):
            P.op("sp", I("dma_start", out=gbc, in_=src_row.to_broadcast([128, D])), writes=[r_gbc], dma_chan=r_gbc)

        def phaseA():
            P.epoch += 1
            A.reset()
            gbc = A.alloc([128, D], F32); r_gbc = R("gbc")
            xt = [A.alloc([128, D], F32) for _ in range(2)]; r_xt = [R("xt%d" % i) for i in range(2)]
            hb = [A.alloc([128, D], BF16) for _ in range(2)]; r_hb = [R("hb%d" % i) for i in range(2)]
            junk = A.alloc([128, D], BF16); r_junk = R("junk")
            load_gbc(gbc, r_gbc, norm_g_d[0:1, :])
            for t in range(NT):
                s = t % 2
                P.op("sp", I("dma_start", out=xt[s], in_=x_d[t * 128:(t + 1) * 128, :]),
                     writes=[r_xt[s]], dma_chan=r_xt[s])
                norm_tile(xt[s], r_xt[s], t, gbc, r_gbc, (hb, r_hb, junk, r_junk))

        def phaseB(l):
            P.epoch += 1
            A.reset()
            lam_init = 0.8 - 0.6 * math.exp(-0.3 * l)
            Vaug = A.alloc([128, NT * 520], BF16); r_V = R("Vaug")
            QT = [A.alloc([128, S], BF16) for _ in range(3)]; r_QT = [R("QT%d" % i) for i in range(3)]
            KT = [A.alloc([128, S], BF16) for _ in range(3)]; r_KT = [R("KT%d" % i) for i in range(3)]
            O_off = A.off
            O = A.alloc([128, NT, 128], F32); r_O = [R("O%d" % g) for g in range(NG)]
            siluT = A.alloc([128, S], BF16); r_silu = [R("silu%d" % g) for g in range(NG)]
            PT = [A.alloc([128, 512], BF16) for _ in range(4)]; r_PT = [R("PT%d" % i) for i in range(4)]
            Wv = A.alloc([128, 8, 512], BF16); r_Wv = R("Wv")
            Wz = [A.alloc([128, 8, 128], BF16) for _ in range(2)]; r_Wz = [R("Wz%d" % i) for i in range(2)]
            Wq = [A.alloc([128, 8, 128], BF16) for _ in range(2)]; r_Wq = [R("Wq%d" % i) for i in range(2)]
            Wk = [A.alloc([128, 8, 128], BF16) for _ in range(2)]; r_Wk = [R("Wk%d" % i) for i in range(2)]
            ystage = [A.alloc([128, 512], BF16) for _ in range(2)]; r_ys = [R("ys%d" % i) for i in range(2)]
            ze = [A.alloc([128, 512], F32) for _ in range(1)]; r_ze = [R("ze%d" % i) for i in range(1)]
            Obf = [A.alloc([128, 4, 128], BF16) for _ in range(2)]; r_Obf = [R("Obf%d" % i) for i in range(2)]
            cmask = A.alloc([128, 128], BF16); r_cm = R("cmask")
            dmask = A.alloc([128, 17 * 128], BF16); r_dm = R("dmask")
            junkf = A.alloc([128, 128], F32); r_junkf = R("junkf")
            dl = A.alloc([128, 256], F32); r_dl = R("dl")
            gn = A.alloc([128, 4], F32); r_gn = R("gn")
            Wf = A.alloc([128, 8, 8], BF16); r_Wf = R("Wf")
            use_ov = NT * 128 * 4 >= 9216
            _save = A.overlay(O_off) if use_ov else None
            fe = A.alloc([8, 256], F32); r_fe = R("fe")
            fsp = A.alloc([8, 256], F32); r_fsp = R("fsp")
            fC = [A.alloc([8, 256], F32) for _ in range(2)]; r_fC = [R("fC%d" % i) for i in range(2)]
            fr = A.alloc([8, 256], F32); r_fr = R("fr")
            ones8 = A.alloc([8, 256], F32); r_ones8 = R("ones8")
            aug6 = [A.alloc([8, 6, 256], BF16)] * 2; r_aug6 = [R("aug6_0")] * 2
            if use_ov:
                assert A.off <= O_off + NT * 128 * 4
                A.overlay(_save)
            fb8 = A.alloc([8, 2], F32); r_fb8 = R("fb8")
            r_caug = [R("caug%d" % i) for i in range(S // 256)]
            lsum = small[:, 16:18]; lexp = small[:, 18:20]; ltmp = small[:, 20:21]; neglam = small[:, 21:22]
            r_lam = R("sm_lam")
            gnp = small[:, 24:28]; r_gnp = R("sm_gnp")
            rden = [small[:, 32 + 4 * i:36 + 4 * i] for i in range(2)]; r_rden = [R("sm_rden%d" % i) for i in range(2)]
            ssq = small[:, 40:44]; lnv = small[:, 44:48]; rstd = small[:, 48:52]; r_rs = R("sm_rs")

            P.op("sp", I("dma_start", out=cmask, in_=cmask_d[:, :]), writes=[r_cm], dma_chan=r_cm)
            P.op("sp", I("dma_start", out=dmask, in_=dmask_d[:, :]), writes=[r_dm], dma_chan=r_dm)
            P.op("sp", I("dma_start", out=dl, in_=diff_lam_d[l:l + 1, :].to_broadcast([128, 256])), writes=[r_dl], dma_chan=r_dl)
            for a in range(4):
                P.op("sp", I("dma_start", out=gn[:, a:a + 1],
                                                      in_=diff_ng_d[l, a * 128:(a + 1) * 128].rearrange("(p o) -> p o", o=1)),
                     writes=[r_gn], dma_chan=r_gn)
            P.op("sp", I("dma_start", out=fb8[:, 0:1], in_=fox_fb_d[l, :].rearrange("(p o) -> p o", o=1)),
                 writes=[r_fb8], dma_chan=r_fb8)
            P.op("pool", I("dma_start", out=Wf, in_=w_in_d[l, :, OFF["fox_f"]:OFF["fox_f"] + 8].rearrange("(c p) n -> p c n", p=128)),
                 writes=[r_Wf], dma_chan=r_Wf)
            P.op("dve", I("scalar_tensor_tensor", out=junkf[:, 0:64], in0=dl[:, 0:64], scalar=1.0, in1=dl[:, 64:128],
                                                         op0=ALU.mult, op1=ALU.mult, accum_out=lsum[:, 0:1]),
                 reads=[r_dl], writes=[r_junkf, r_lam])
            P.op("dve", I("scalar_tensor_tensor", out=junkf[:, 0:64], in0=dl[:, 128:192], scalar=1.0, in1=dl[:, 192:256],
                                                         op0=ALU.mult, op1=ALU.mult, accum_out=lsum[:, 1:2]),
                 reads=[r_dl], writes=[r_junkf, r_lam])
            P.op("act", I("activation", out=lexp, in_=lsum, func=AF.Exp), reads=[r_lam], writes=[r_lam])
            P.op("dve", I("tensor_tensor", out=ltmp, in0=lexp[:, 0:1], in1=lexp[:, 1:2], op=ALU.subtract), reads=[r_lam], writes=[r_lam])
            P.op("dve", I("tensor_scalar", out=neglam, in0=ltmp, scalar1=-1.0, scalar2=-lam_init, op0=ALU.mult, op1=ALU.add),
                 reads=[r_lam], writes=[r_lam])
            P.op("dve", I("tensor_scalar", out=gnp, in0=gn, scalar1=1.0 - lam_init, scalar2=None, op0=ALU.mult),
                 reads=[r_gn], writes=[r_gnp])
            P.op("dve", I("tensor_scalar", out=fb8[:, 1:2], in0=fb8[:, 0:1], scalar1=-1.0, scalar2=None, op0=ALU.mult),
                 reads=[r_fb8], writes=[r_fb8])
            P.op("dve", I("memset", ones8, 1.0), reads=r_O, writes=[r_ones8])

            def Pf(eng, emit, reads=(), writes=(), dma_chan=None):
                return P.op(eng, emit, reads=list(reads) + r_O, writes=writes, dma_chan=dma_chan)

            prevC = None
            for ch in range(NG):
                b = misc_bank()
                for c in range(8):
                    Pf("pe", I("matmul", banks[b][0:8, :], lhsT=Wf[:, c, :], rhs=hT[:, c, ch * 512:(ch + 1) * 512],
                                                                   start=(c == 0), stop=(c == 7)),
                         reads=[r_Wf] + r_hT[ch * 4:ch * 4 + 4], writes=[r_bank[b]])
                for hf in range(2):
                    i = ch * 2 + hf
                    s = i % 2
                    Pf("act", I("activation", out=fe, in_=banks[b][0:8, hf * 256:(hf + 1) * 256], func=AF.Exp,
                                                                   scale=-1.0, bias=fb8[:, 1:2]),
                         reads=[r_bank[b], r_fb8], writes=[r_fe])
                    Pf("act", I("activation", out=fsp, in_=fe, func=AF.Ln, bias=1.0, scale=1.0), reads=[r_fe], writes=[r_fsp])
                    init = 0.0 if prevC is None else prevC[:, 255:256]
                    Pf("dve", I("tensor_tensor_scan", out=fC[s], data0=ones8, data1=fsp, initial=init,
                                                                              op0=ALU.mult, op1=ALU.add),
                         reads=[r_ones8, r_fsp, r_fC[1 - s]], writes=[r_fC[s]])
                    prevC = fC[s]
                    a6 = aug6[s]
                    Pf("dve", I("tensor_copy", out=a6[:, 0, :], in_=fC[s]), reads=[r_fC[s]], writes=[r_aug6[s]])
                    Pf("dve", I("tensor_tensor", out=fr, in0=fC[s], in1=a6[:, 0, :], op=ALU.subtract),
                         reads=[r_fC[s], r_aug6[s]], writes=[r_fr])
                    Pf("dve", I("tensor_copy", out=a6[:, 1, :], in_=fr), reads=[r_fr], writes=[r_aug6[s]])
                    Pf("dve", I("tensor_tensor", out=fr, in0=fr, in1=a6[:, 1, :], op=ALU.subtract),
                         reads=[r_fr, r_aug6[s]], writes=[r_fr])
                    Pf("dve", I("tensor_copy", out=a6[:, 2, :], in_=fr), reads=[r_fr], writes=[r_aug6[s]])
                    Pf("dve", I("tensor_scalar", out=a6[:, 3:6, :], in0=a6[:, 0:3, :], scalar1=-1.0, scalar2=None, op0=ALU.mult),
                         reads=[r_aug6[s]], writes=[r_aug6[s]])
                    Pf("sp", I("dma_start", out=caug_d[:, :, i * 256:(i + 1) * 256], in_=a6),
                         reads=[r_aug6[s]], writes=[r_caug[i]], dma_chan=r_aug6[s])

            maps = []
            for a in range(4):
                for c in range(2):
                    maps.append(dict(br=0, unit=a, sub=c, qoff=OFF["diff_q"] + a * 128 + c * 64, koff=OFF["diff_k"] + a * 128 + c * 64,
                                     kind="alibi", slope=2 * (a + 1) - 1, Kc=68, band=None, E=128, vh=a, full=True))
            for h in range(8):
                maps.append(dict(br=1, unit=4 + h // 2, sub=h % 2, qoff=OFF["fox_q"] + h * 64, koff=OFF["fox_k"] + h * 64,
                                 kind="fox", head=h, Kc=70, band=None, E=64, vh=h, full=True))
            for h in range(8):
                maps.append(dict(br=2, unit=8 + h // 2, sub=h % 2, qoff=OFF["dil_q"] + h * 64, koff=OFF["dil_k"] + h * 64,
                                 kind="alibi", slope=h, Kc=68, band=17, E=64, vh=h, full=False))
            voff = [OFF["diff_v"], OFF["fox_v"], OFF["dil_v"]]
            zoff = [OFF["diff_z"], OFF["fox_z"], OFF["dil_z"]]

            def Vview(br):
                if br == 0:
                    return Vaug[:, 0:NT * 516].rearrange("p (t h e) -> p t h e", t=NT, h=4)
                return Vaug.rearrange("p (t h e) -> p t h e", t=NT, h=8)

            def load_wqk(ui):
                mA, mB = maps[2 * ui], maps[2 * ui + 1]
                sA, sB = (2 * ui) % 3, (2 * ui + 1) % 3
                w = ui % 2
                P.op("pool", I("dma_start", out=Wq[w], in_=w_in_d[l, :, mA["qoff"]:mA["qoff"] + 128].rearrange("(c p) n -> p c n", p=128)),
                     writes=[r_Wq[w]], dma_chan=r_Wq[w])
                P.op("pool", I("dma_start", out=Wk[w], in_=w_in_d[l, :, mA["koff"]:mA["koff"] + 128].rearrange("(c p) n -> p c n", p=128)),
                     writes=[r_Wk[w]], dma_chan=r_Wk[w])
                for (m, s, zlo, a0) in ((mA, sA, 64, 64), (mB, sB, 0, 0)):
                    P.op("pool", I("memset", QT[s][zlo:zlo + 64, :], 0.0), writes=[r_QT[s]])
                    P.op("pool", I("memset", KT[s][zlo:zlo + 64, :], 0.0), writes=[r_KT[s]])
                    if m["kind"] == "alibi":
                        P.op("sp", I("dma_start", out=QT[s][a0:a0 + 4, :], in_=qaug_d[:, :]), writes=[r_QT[s]], dma_chan=r_QT[s])
                        P.op("sp", I("dma_start", out=KT[s][a0:a0 + 4, :], in_=kaug_d[m["slope"], :, :]), writes=[r_KT[s]], dma_chan=r_KT[s])
                    else:
                        h = m["head"]
                        P.op("sp", I("dma_start", out=QT[s][a0:a0 + 3, :], in_=ones3_d[:, :]), writes=[r_QT[s]], dma_chan=r_QT[s])
                        P.op("sp", I("dma_start", out=QT[s][a0 + 3:a0 + 6, :], in_=caug_d[h, 3:6, :]), reads=r_caug, writes=[r_QT[s]], dma_chan=r_QT[s])
                        P.op("sp", I("dma_start", out=KT[s][a0:a0 + 3, :], in_=caug_d[h, 0:3, :]), reads=r_caug, writes=[r_KT[s]], dma_chan=r_KT[s])
                        P.op("sp", I("dma_start", out=KT[s][a0 + 3:a0 + 6, :], in_=ones3_d[:, :]), writes=[r_KT[s]], dma_chan=r_KT[s])

            def proj_chunk(ui, ch):
                sA, sB = (2 * ui) % 3, (2 * ui + 1) % 3
                w = ui % 2
                hts = r_hT[ch * 4:ch * 4 + 4]
                cs = slice(ch * 512, (ch + 1) * 512)
                b = misc_bank()
                for c in range(8):
                    P.op("pe", I("matmul", banks[b][:, :], lhsT=Wq[w][:, c, :], rhs=hT[:, c, cs], start=(c == 0), stop=(c == 7)),
                         reads=[r_Wq[w]] + hts, writes=[r_bank[b]])
                P.op("dve", I("tensor_scalar", out=QT[sA][0:64, cs], in0=banks[b][0:64, :], scalar1=0.125, scalar2=None, op0=ALU.mult),
                     reads=[r_bank[b]], writes=[r_QT[sA]])
                P.op("dve", I("tensor_scalar", out=QT[sB][64:128, cs], in0=banks[b][64:128, :], scalar1=0.125, scalar2=None, op0=ALU.mult),
                     reads=[r_bank[b]], writes=[r_QT[sB]])
                b2 = misc_bank()
                for c in range(8):
                    P.op("pe", I("matmul", banks[b2][:, :], lhsT=Wk[w][:, c, :], rhs=hT[:, c, cs], start=(c == 0), stop=(c == 7)),
                         reads=[r_Wk[w]] + hts, writes=[r_bank[b2]])
                P.op("dve", I("tensor_copy", out=KT[sA][0:64, cs], in_=banks[b2][0:64, :]), reads=[r_bank[b2]], writes=[r_KT[sA]])
                P.op("dve", I("tensor_copy", out=KT[sB][64:128, cs], in_=banks[b2][64:128, :]), reads=[r_bank[b2]], writes=[r_KT[sB]])

            def load_wz(u):
                s = u % 2
                br, j = u // 4, u % 4
                o = zoff[br] + j * 128
                P.op("pool", I("dma_start", out=Wz[s], in_=w_in_d[l, :, o:o + 128].rearrange("(c p) n -> p c n", p=128)),
                     writes=[r_Wz[s]], dma_chan=r_Wz[s])

            def z_chunk(u, ch):
                s = u % 2
                b = misc_bank()
                for c in range(8):
                    P.op("pe", I("matmul", banks[b][:, :], lhsT=Wz[s][:, c, :], rhs=hT[:, c, ch * 512:(ch + 1) * 512],
                                                       start=(c == 0), stop=(c == 7)),
                         reads=[r_Wz[s]] + r_hT[ch * 4:ch * 4 + 4], writes=[r_bank[b]])
                zs = 0
                P.op("act", I("activation", out=ze[zs], in_=banks[b][:, :], func=AF.Exp, scale=-1.0), reads=[r_bank[b]], writes=[r_ze[zs]])
                P.op("dve", I("tensor_scalar", out=ze[zs], in0=ze[zs], scalar1=1.0, scalar2=None, op0=ALU.add), reads=[r_ze[zs]], writes=[r_ze[zs]])
                P.op("dve", I("reciprocal", out=ze[zs], in_=ze[zs]), reads=[r_ze[zs]], writes=[r_ze[zs]])
                P.op("dve", I("tensor_tensor", out=siluT[:, ch * 512:(ch + 1) * 512], in0=banks[b][:, :], in1=ze[zs], op=ALU.mult),
                     reads=[r_bank[b], r_ze[zs]], writes=[r_silu[ch]])

            def load_wv(br):
                o = voff[br]
                for hh in range(2):
                    P.op("pool", I("dma_start", out=Wv[:, :, hh * 256:(hh + 1) * 256],
                                                              in_=w_in_d[l, :, o + hh * 256:o + (hh + 1) * 256].rearrange("(c p) n -> p c n", p=128)),
                         writes=[r_Wv], dma_chan=r_Wv)

            def branch_setup(br):
                Vv = Vview(br)
                H, E = (4, 128) if br == 0 else (8, 64)
                P.op("pool", I("memset", Vv[:, :, :, E:E + 1], 1.0), writes=[r_V])
                for t in range(NT):
                    b = misc_bank()
                    for c in range(8):
                        P.op("pe", I("matmul", banks[b][:, :], lhsT=hT[:, c, t * 128:(t + 1) * 128], rhs=Wv[:, c, :],
                                                                     start=(c == 0), stop=(c == 7)),
                             reads=[r_Wv, r_hT[t]], writes=[r_bank[b]])
                    src = banks[b][:, :].rearrange("p (h e) -> p h e", h=H)
                    if t % 2 == 0:
                        P.op("act", I("activation", out=Vv[:, t, :, 0:E], in_=src, func=AF.Copy),
                             reads=[r_bank[b]], writes=[r_V])
                    else:
                        P.op("dve", I("tensor_copy", out=Vv[:, t, :, 0:E], in_=src), reads=[r_bank[b]], writes=[r_V])

            ACCB = (2, 3, 4, 5)

            def attention(mi, filler):
                m = maps[mi]
                s = mi % 3
                E = m["E"]
                Kc = m["Kc"]
                Vv = Vview(m["br"])
                nbk = 2 if E == 128 else 1
                steps = []

                def make_evac(g, b0, accv, bkof):
                    def evac():
                        rs = nxt("rden", 2)
                        rd = rden[rs]
                        if E == 64:
                            denv = banks[b0][:, 0:260].rearrange("p (j e) -> p j e", j=4)[:, :, 64]
                            P.op("dve", I("reciprocal", out=rd, in_=denv), reads=[r_bank[b0]], writes=[r_rden[rs]])
                            for j in range(4):
                                P.op("dve", I("tensor_scalar", out=O[:, 4 * g + j, m["sub"] * 64:(m["sub"] + 1) * 64], in0=accv[j][:, 0:64],
                                              scalar1=rd[:, j:j + 1], scalar2=None, op0=ALU.mult),
                                     reads=[r_bank[b0], r_rden[rs]], writes=[r_O[g]])
                        else:
                            for j in range(4):
                                P.op("dve", I("reciprocal", out=rd[:, j:j + 1], in_=accv[j][:, 128:129]),
                                     reads=[r_bank[bkof[j]]], writes=[r_rden[rs]])
                            if m["sub"] == 0:
                                for j in range(4):
                                    P.op("dve", I("tensor_scalar", out=O[:, 4 * g + j, :], in0=accv[j][:, 0:128], scalar1=rd[:, j:j + 1],
                                                  scalar2=None, op0=ALU.mult),
                                         reads=[r_bank[bkof[j]], r_rden[rs]], writes=[r_O[g]])
                            else:
                                P.op("dve", I("tensor_scalar", out=rd, in0=rd, scalar1=neglam, scalar2=None, op0=ALU.mult),
                                     reads=[r_rden[rs], r_lam], writes=[r_rden[rs]])
                                for j in range(4):
                                    P.op("dve", I("scalar_tensor_tensor", out=O[:, 4 * g + j, :], in0=accv[j][:, 0:128], scalar=rd[:, j:j + 1],
                                                  in1=O[:, 4 * g + j, :], op0=ALU.mult, op1=ALU.add),
                                         reads=[r_bank[bkof[j]], r_rden[rs], r_O[g]], writes=[r_O[g]])
                                    P.op("dve", I("scalar_tensor_tensor", out=junkf, in0=O[:, 4 * g + j, :], scalar=1.0, in1=O[:, 4 * g + j, :],
                                                  op0=ALU.mult, op1=ALU.mult, accum_out=ssq[:, j:j + 1]),
                                         reads=[r_O[g]], writes=[r_junkf, r_rs])
                                P.op("act", I("activation", out=lnv, in_=ssq, func=AF.Ln, scale=1.0 / 128, bias=EPS), reads=[r_rs], writes=[r_rs])
                                P.op("act", I("activation", out=rstd, in_=lnv, func=AF.Exp, scale=-0.5), reads=[r_rs], writes=[r_rs])
                                for j in range(4):
                                    P.op("dve", I("tensor_scalar", out=O[:, 4 * g + j, :], in0=O[:, 4 * g + j, :], scalar1=rstd[:, j:j + 1],
                                                  scalar2=None, op0=ALU.mult),
                                         reads=[r_O[g], r_rs], writes=[r_O[g]])
                    return evac

                STB = (0, 1) if nbk == 2 else (0, 1, 4, 5)
                LOOK = 1 if nbk == 2 else 3
                for g in range(NG):
                    if nbk == 2:
                        b0 = ACCB[2 * nxt("acc2", 2)]
                        ring["acc1"] = 0
                        bks = [b0, b0 + 1]
                        accv = [banks[bks[j // 2]][:, (j % 2) * 129:(j % 2) * 129 + 129] for j in range(4)]
                        bkof = [bks[j // 2] for j in range(4)]
                    else:
                        b0 = ACCB[nxt("acc1", 2)]
                        ring["acc2"] = 0
                        bks = [b0]
                        accv = [banks[b0][:, j * 65:j * 65 + 65] for j in range(4)]
                        bkof = [b0] * 4
                    first = {b: True for b in bks}
                    kb_lo = 0 if m["full"] else max(0, 4 * g - 16)
                    for kb in range(kb_lo, 4 * g + 4):
                        jlo = max(0, kb - 4 * g)
                        jhi = 3 if m["full"] else min(3, kb + 16 - 4 * g)
                        pvs = []
                        for j in range(jlo, jhi + 1):
                            bk = bkof[j]
                            pvs.append((j, bk, first[bk], 4 * g + j, accv[j]))
                            first[bk] = False
                        steps.append(dict(g=g, kb=kb, jlo=jlo, N=(jhi - jlo + 1) * 128, q0=(4 * g + jlo) * 128, sb=STB[nxt("st", len(STB))], p=nxt("pt", 4),
                                          pvs=pvs, last=(kb == 4 * g + 3), evac=make_evac(g, b0, accv, bkof)))

                def do_qk(t):
                    kb, N, q0, sb_ = t["kb"], t["N"], t["q0"], t["sb"]
                    P.op("pe", I("matmul", banks[sb_][:, 0:N], lhsT=KT[s][:, kb * 128:(kb + 1) * 128],
                                 rhs=QT[s][:, q0:q0 + N], start=True, stop=True),
                         reads=[r_KT[s], r_QT[s]], writes=[r_bank[sb_]])

                def do_exp(t):
                    kb, N, sb_, p, g, jlo = t["kb"], t["N"], t["sb"], t["p"], t["g"], t["jlo"]
                    P.op("act", I("activation", out=PT[p][:, 0:N], in_=banks[sb_][:, 0:N], func=AF.Exp),
                         reads=[r_bank[sb_]], writes=[r_PT[p]])
                    if m["full"]:
                        if kb >= 4 * g:
                            P.op("dve", I("tensor_tensor", out=PT[p][:, 0:128], in0=PT[p][:, 0:128], in1=cmask, op=ALU.mult),
                                 reads=[r_PT[p], r_cm], writes=[r_PT[p]])
                    else:
                        dlo = 4 * g + jlo - kb
                        P.op("dve", I("tensor_tensor", out=PT[p][:, 0:N], in0=PT[p][:, 0:N],
                                      in1=dmask[:, dlo * 128:dlo * 128 + N], op=ALU.mult),
                             reads=[r_PT[p], r_dm], writes=[r_PT[p]])

                def do_pv(t):
                    kb, p, jlo = t["kb"], t["p"], t["jlo"]
                    for (j, bk, stf, qb, av) in t["pvs"]:
                        P.op("pe", I("matmul", av, lhsT=PT[p][:, (j - jlo) * 128:(j - jlo + 1) * 128], rhs=Vv[:, kb, m["vh"], :],
                                     start=stf, stop=(kb == qb), skip_group_check=True),
                             reads=[r_PT[p], r_V], writes=[r_bank[bk]])

                n = len(steps)
                for i in range(min(LOOK, n)):
                    do_qk(steps[i])
                for i, t in enumerate(steps):
                    if i + LOOK < n:
                        do_qk(steps[i + LOOK])
                    do_exp(t)
                    do_pv(t)
                    if t["last"]:
                        t["evac"]()
                        filler(t["g"])

            def finalize_unit(u):
                br = u // 4
                for g in range(NG):
                    ob = nxt("obf", 2)
                    P.op("pool", I("tensor_copy", out=Obf[ob], in_=O[:, 4 * g:4 * g + 4, :]), reads=[r_O[g]], writes=[r_Obf[ob]])
                    b = misc_bank()
                    tpv = banks[b][:, :].bitcast(BF16)
                    for j in range(4):
                        P.op("pe", I("transpose", out=tpv[:, j * 128:(j + 1) * 128], in_=Obf[ob][:, j, :], identity=identb[:, :]),
                             reads=[r_Obf[ob], R("identb")], writes=[r_bank[b]])
                    ys = nxt("ys", 2)
                    if br == 0:
                        a = u % 4
                        P.op("dve", I("scalar_tensor_tensor", out=ystage[ys], in0=tpv[:, 0:512], scalar=gnp[:, a:a + 1],
                                      in1=siluT[:, g * 512:(g + 1) * 512], op0=ALU.mult, op1=ALU.mult),
                             reads=[r_bank[b], r_gnp, r_silu[g]], writes=[r_ys[ys]])
                    else:
                        P.op("dve", I("tensor_tensor", out=ystage[ys], in0=tpv[:, 0:512], in1=siluT[:, g * 512:(g + 1) * 512], op=ALU.mult),
                             reads=[r_bank[b], r_silu[g]], writes=[r_ys[ys]])
                    P.op("sp", I("dma_start", out=yT_d[u, :, g * 512:(g + 1) * 512], in_=ystage[ys]),
                         reads=[r_ys[ys]], dma_chan=r_ys[ys])

            nm = len(maps)
            nu = nm // 2
            load_wqk(0)
            load_wv(0)
            for ch in range(NG):
                proj_chunk(0, ch)
            for mi, m in enumerate(maps):
                u = m["unit"]
                ui = mi // 2
                if m["sub"] == 0:
                    load_wz(u)
                elif ui + 1 < nu:
                    load_wqk(ui + 1)
                if mi % 8 == 0:
                    branch_setup(m["br"])
                    if m["br"] < 2:
                        load_wv(m["br"] + 1)

                def filler(g, m=m, u=u, ui=ui):
                    if m["sub"] == 1:
                        if ui + 1 < nu:
                            proj_chunk(ui + 1, g)
                        z_chunk(u, g)
                attention(mi, filler)
                if m["sub"] == 1:
                    finalize_unit(u)
            build.phaseB_bytes = A.off

        def phaseC(l):
            P.epoch += 1
            A.reset()
            last = (l == DEPTH - 1)
            Wb = A.alloc([128, 12, D], BF16); r_Wb = [R("Wb%d" % f) for f in range(8)]
            Wg = A.alloc([128, 8, 3072], BF16); r_Wg = [R("Wg%d" % f) for f in range(8)]
            Wo = A.alloc([128, 8, D], BF16); r_Wo = R("Wo")
            yc = A.alloc([128, 12, 512], BF16); r_yc = R("yc")
            mT = A.alloc([128, 8, 512], BF16); r_mT = [R("mT%d" % f) for f in range(8)]
            sg = [A.alloc([128, 512], F32) for _ in range(2)]; r_sg = [R("sg%d" % i) for i in range(2)]
            macc = A.alloc([128, 512], F32); r_macc = R("macc")
            tmp = A.alloc([128, 512], F32); r_tmp = R("tmp")
            xt = [A.alloc([128, D], F32) for _ in range(2)]; r_xt = [R("cxt%d" % i) for i in range(2)]
            hb = [A.alloc([128, D], BF16) for _ in range(2)]; r_hb = [R("chb%d" % i) for i in range(2)]
            junk = A.alloc([128, D], BF16); r_junk = R("cjunk")
            gbc = A.alloc([128, D], F32); r_gbc = R("cgbc")
            load_gbc(gbc, r_gbc, final_g_d[0:1, :] if last else norm_g_d[l + 1:l + 2, :])
            wbv = w_br_d[l].rearrange("(u p) f -> p u f", p=128)
            wgv = w_in_d[l, :, OFF["merge_g"]:OFF["merge_g"] + 3072].rearrange("(c p) (n q) -> p c n q", p=128, n=3)
            Wg4 = Wg.rearrange("p c (n q) -> p c n q", n=3)
            for f in range(8):
                P.op("pool", I("dma_start", out=Wb[:, :, f * 128:(f + 1) * 128], in_=wbv[:, :, f * 128:(f + 1) * 128]),
                     writes=[r_Wb[f]], dma_chan=r_Wb[f])
                for n in range(3):
                    P.op("pool", I("dma_start", out=Wg4[:, :, n, f * 128:(f + 1) * 128], in_=wgv[:, :, n, f * 128:(f + 1) * 128]),
                         writes=[r_Wg[f]], dma_chan=r_Wg[f])
            wov = w_out_d[l].rearrange("(f p) o -> p f o", p=128)
            for hh in range(2):
                P.op("pool", I("dma_start", out=Wo[:, :, hh * 512:(hh + 1) * 512], in_=wov[:, :, hh * 512:(hh + 1) * 512]),
                     writes=[r_Wo], dma_chan=r_Wo)
            xsrc = x_d if l == 0 else xs_d
            CB = (0, 1, 2, 3)
            OB = (4, 5)
            pending = []

            def flush():
                while pending:
                    pending.pop(0)()

            for tc in range(NG):
                P.op("sp", I("dma_start", out=yc, in_=yT_d[:, :, tc * 512:(tc + 1) * 512].rearrange("u p s -> p u s")),
                     writes=[r_yc], dma_chan=r_yc)
                hts = r_hT[tc * 4:tc * 4 + 4]
                for f in range(8):
                    for n in range(3):
                        bb = CB[nxt("cb", 4)]
                        for j in range(4):
                            P.op("pe", I("matmul", banks[bb][:, :], lhsT=Wb[:, n * 4 + j, f * 128:(f + 1) * 128], rhs=yc[:, n * 4 + j, :],
                                                                                start=(j == 0), stop=(j == 3)),
                                 reads=[r_Wb[f], r_yc], writes=[r_bank[bb]])
                        gb = CB[nxt("cb", 4)]
                        for c in range(8):
                            P.op("pe", I("matmul", banks[gb][:, :], lhsT=Wg[:, c, n * 1024 + f * 128:n * 1024 + (f + 1) * 128],
                                                                                       rhs=hT[:, c, tc * 512:(tc + 1) * 512], start=(c == 0), stop=(c == 7)),
                                 reads=[r_Wg[f]] + hts, writes=[r_bank[gb]])
                        if f == 0 and n == 1:
                            flush()
                        sgi = nxt("sg", 2)
                        P.op("act", I("activation", out=sg[sgi], in_=banks[gb][:, :], func=AF.Sigmoid),
                             reads=[r_bank[gb]], writes=[r_sg[sgi]])
                        if n == 0:
                            P.op("dve", I("tensor_tensor", out=macc, in0=banks[bb][:, :], in1=sg[sgi], op=ALU.mult),
                                 reads=[r_bank[bb], r_sg[sgi]], writes=[r_macc])
                        else:
                            P.op("dve", I("tensor_tensor", out=tmp, in0=banks[bb][:, :], in1=sg[sgi], op=ALU.mult),
                                 reads=[r_bank[bb], r_sg[sgi]], writes=[r_tmp])
                            if n == 1:
                                P.op("dve", I("tensor_tensor", out=macc, in0=macc, in1=tmp, op=ALU.add), reads=[r_macc, r_tmp], writes=[r_macc])
                            else:
                                P.op("dve", I("tensor_tensor", out=mT[:, f, :], in0=macc, in1=tmp, op=ALU.add),
                                     reads=[r_macc, r_tmp], writes=[r_mT[f]])
                for tt in range(4):
                    t = tc * 4 + tt
                    s = nxt("cxt", 2)
                    P.op("sp", I("dma_start", out=xt[s], in_=xsrc[t * 128:(t + 1) * 128, :]), writes=[r_xt[s]], dma_chan=r_xt[s])
                    for hh in range(2):
                        ob = OB[nxt("ob", 2)]
                        for f in range(8):
                            P.op("pe", I("matmul", banks[ob][:, :], lhsT=mT[:, f, tt * 128:(tt + 1) * 128],
                                                                                    rhs=Wo[:, f, hh * 512:(hh + 1) * 512], start=(f == 0), stop=(f == 7)),
                                 reads=r_mT + [r_Wo], writes=[r_bank[ob]])
                        P.op("dve", I("tensor_tensor", out=xt[s][:, hh * 512:(hh + 1) * 512], in0=banks[ob][:, :],
                                                                                 in1=xt[s][:, hh * 512:(hh + 1) * 512], op=ALU.add),
                             reads=[r_bank[ob], r_xt[s]], writes=[r_xt[s]])
                    if not last:
                        P.op("sp", I("dma_start", out=xs_d[t * 128:(t + 1) * 128, :], in_=xt[s]), reads=[r_xt[s]], dma_chan=r_xt[s])
                        pp = norm_tile(xt[s], r_xt[s], t, gbc, r_gbc, (hb, r_hb, junk, r_junk), defer=True)
                        flush()
                        pending.append(pp)
                    else:
                        fs = nxt("fss", 2)
                        ss = small[:, 8 + 4 * fs:12 + 4 * fs]
                        r_ss = R("sm_fss%d" % fs)
                        P.op("act", I("activation", out=junk, in_=xt[s], func=AF.Square, accum_out=ss[:, 0:1]),
                             reads=[r_xt[s]], writes=[r_junk, r_ss])
                        P.op("act", I("activation", out=ss[:, 1:2], in_=ss[:, 0:1], func=AF.Sqrt, scale=1.0 / D, bias=EPS),
                             reads=[r_ss], writes=[r_ss])
                        P.op("dve", I("reciprocal", out=ss[:, 2:3], in_=ss[:, 1:2]), reads=[r_ss], writes=[r_ss])
                        P.op("dve", I("scalar_tensor_tensor", out=xt[s], in0=xt[s], scalar=ss[:, 2:3], in1=gbc, op0=ALU.mult, op1=ALU.mult),
                             reads=[r_xt[s], r_ss, r_gbc], writes=[r_xt[s]])
                        P.op("sp", I("dma_start", out=out_d[t * 128:(t + 1) * 128, :], in_=xt[s]), reads=[r_xt[s]], dma_chan=r_xt[s])
            flush()

        phaseA()
        for l in range(DEPTH):
            P.barrier()
            phaseB(l)
            P.barrier()
            phaseC(l)
        stats = P.emit(nc, st)
        stats["arena_peak"] = A.peak
        build.stats = stats
    return nc


def make_consts(S):
    bf = ml_dtypes.bfloat16
    t = np.arange(S)
    qaug = np.stack([(t // 64) * 64, t % 64, np.ones(S), np.ones(S)]).astype(np.float32)
    kaug = np.zeros((8, 4, S), np.float32)
    for j in range(8):
        sl = 2.0 ** -(j + 1)
        kaug[j, 0] = -sl
        kaug[j, 1] = -sl
        kaug[j, 2] = sl * ((t // 64) * 64)
        kaug[j, 3] = sl * (t % 64)
    k = np.arange(128)[:, None]
    q = np.arange(128)[None, :]
    cmask = (q >= k).astype(np.float32)
    dm = np.zeros((128, 17, 128), np.float32)
    for dlt in range(17):
        d = 128 * dlt + q - k
        mult = ((d >= 0) & (d <= 128)).astype(np.float32) + ((d >= 0) & (d % 4 == 0) & (d <= 512)) + ((d >= 0) & (d % 16 == 0) & (d <= 2048))
        dm[:, dlt, :] = mult
    return dict(ident_bf=np.eye(128).astype(bf), ident_f=np.eye(128).astype(np.float32), qaug=qaug.astype(bf), kaug=kaug.astype(bf),
                ones3=np.ones((3, S)).astype(bf), cmask=cmask.astype(bf), dmask=dm.reshape(128, 17 * 128).astype(bf))


def kernel(x, norm_g, w_in, fox_fb, diff_lam, diff_norm_g, w_branch, w_out, final_g):
    x = np.asarray(x, np.float32)
    B, S, _ = x.shape
    DEPTH = norm_g.shape[0]
    nc = build(S, DEPTH)
    shared = dict(norm_g=np.ascontiguousarray(norm_g, np.float32), w_in=np.ascontiguousarray(w_in, np.float32),
                  fox_fb=np.ascontiguousarray(fox_fb, np.float32),
                  diff_lam=np.ascontiguousarray(np.asarray(diff_lam, np.float32).reshape(DEPTH, 256)),
                  diff_norm_g=np.ascontiguousarray(diff_norm_g, np.float32),
                  w_branch=np.ascontiguousarray(np.asarray(w_branch, np.float32).reshape(DEPTH, 1536, D)),
                  w_out=np.ascontiguousarray(w_out, np.float32),
                  final_g=np.ascontiguousarray(np.asarray(final_g, np.float32).reshape(1, D)))
    shared.update(make_consts(S))
    in_maps = [dict(shared, x=np.ascontiguousarray(x[b])) for b in range(B)]
    res = run_bass_kernel_spmd(nc, in_maps, core_ids=list(range(B)))
    return np.stack([np.asarray(r["out"], np.float32) for r in res.results], axis=0)
```
